# Optimizing a Trainium2 kernel written in Bass

```python
import math
import jax, jax.numpy as jnp
from jax import lax
import numpy as np

D_MODEL = 1024
BATCH = 16
SEQ = 2048
DEPTH = 4

N_A = DEPTH // 2
N_B = DEPTH - N_A
CONV_WIDTH = 31
FFN_CONV_WIDTH = 3
D_FF = 2816
N_HEADS = 8
HEAD_DIM = D_MODEL // N_HEADS
WINDOWS = (128, 512, 2048)
DILATIONS = (1, 4, 16)
N_GROUPS = len(WINDOWS)
Q_WIDTH = N_GROUPS * N_HEADS * HEAD_DIM
BLOCK = 128
EPS = 1e-6
NEG_INF = -1e30

kernel_name = "yoco_conformer_dilated_hybrid"


def rms_norm(x, g):
    xf = x.astype(jnp.float32)
    y = xf * lax.rsqrt(jnp.mean(xf * xf, axis=-1, keepdims=True) + EPS)
    return (y * g.astype(jnp.float32)).astype(x.dtype)


def layer_norm(x, g, b):
    xf = x.astype(jnp.float32)
    mu = jnp.mean(xf, axis=-1, keepdims=True)
    var = jnp.mean(jnp.square(xf - mu), axis=-1, keepdims=True)
    y = (xf - mu) * lax.rsqrt(var + EPS)
    return (y * g.astype(jnp.float32) + b.astype(jnp.float32)).astype(x.dtype)


def causal_dwconv(x, w, b):
    k, c = w.shape
    y = lax.conv_general_dilated(
        x, w[:, None, :].astype(x.dtype), window_strides=(1,), padding=[(k - 1, 0)],
        dimension_numbers=("NWC", "WIO", "NWC"), feature_group_count=c)
    return y + b


def conformer_conv_module(h, w_in, b_in, dw, dw_b, ln_g, ln_b, w_out, b_out):
    u = h @ w_in + b_in
    u = u[..., :D_MODEL] * jax.nn.sigmoid(u[..., D_MODEL:])
    u = causal_dwconv(u, dw, dw_b)
    u = layer_norm(u, ln_g, ln_b)
    u = jax.nn.silu(u)
    return u @ w_out + b_out


def conv_ffn(h, w_in, dw, dw_b, w_out):
    u = causal_dwconv(h @ w_in, dw, dw_b)
    a, g = u[..., :D_FF], u[..., D_FF:]
    return (jax.nn.silu(g) * a) @ w_out


def dilated_branch(q, k, v, window, dil):
    b, s, h, dh = q.shape
    steps = window // dil
    assert steps <= BLOCK and s % dil == 0
    L = s // dil
    nblk = -(-L // BLOCK)
    lp = nblk * BLOCK

    def to_sub(t):
        t = t.reshape(b, L, dil, h, dh).transpose(0, 2, 3, 1, 4)
        return jnp.pad(t, ((0, 0), (0, 0), (0, 0), (0, lp - L), (0, 0)))

    qs, ks, vs = to_sub(q), to_sub(k), to_sub(v)
    q_blk = qs.reshape(b, dil, h, nblk, BLOCK, dh)

    def band(t):
        tp = jnp.pad(t, ((0, 0), (0, 0), (0, 0), (BLOCK, 0), (0, 0)))
        prev = tp[:, :, :, :lp].reshape(b, dil, h, nblk, BLOCK, dh)
        cur = tp[:, :, :, BLOCK:].reshape(b, dil, h, nblk, BLOCK, dh)
        return jnp.concatenate([prev, cur], axis=-2)

    k_band, v_band = band(ks), band(vs)
    scores = jnp.einsum("brhnqc,brhnkc->brhnqk", q_blk, k_band).astype(jnp.float32)
    scores = scores * (1.0 / math.sqrt(dh))

    blk_i = jnp.arange(nblk)[:, None, None]
    qi = jnp.arange(BLOCK)[None, :, None]
    kk = jnp.arange(2 * BLOCK)[None, None, :]
    dist = qi + BLOCK - kk
    key_pos = blk_i * BLOCK + kk - BLOCK
    valid = (dist >= 0) & (dist <= steps) & (key_pos >= 0)
    scores = jnp.where(valid, scores, NEG_INF)

    m = jnp.max(scores, axis=-1, keepdims=True)
    p = jnp.exp(scores - m)
    den = jnp.sum(p, axis=-1, keepdims=True)
    out = jnp.einsum("brhnqk,brhnkc->brhnqc", p.astype(v.dtype), v_band)
    out = out / den.astype(out.dtype)
    lse = (m + jnp.log(den))[..., 0]

    out = out.reshape(b, dil, h, lp, dh)[:, :, :, :L]
    out = out.transpose(0, 3, 1, 2, 4).reshape(b, s, h, dh)
    lse = lse.reshape(b, dil, h, lp)[:, :, :, :L]
    lse = lse.transpose(0, 3, 1, 2).reshape(b, s, h)
    return out, lse


def dilated_mixture_attention(h, w_q, w_o, k, v):
    b, s, _ = h.shape
    q = (h @ w_q).reshape(b, s, N_GROUPS, N_HEADS, HEAD_DIM)
    outs, lses = [], []
    for g in range(N_GROUPS):
        o, l = dilated_branch(q[:, :, g], k[:, :, g], v[:, :, g], WINDOWS[g], DILATIONS[g])
        outs.append(o)
        lses.append(l)
    outs = jnp.stack(outs, axis=0)
    wts = jax.nn.softmax(jnp.stack(lses, axis=0), axis=0)
    merged = jnp.sum(wts[..., None].astype(outs.dtype) * outs, axis=0)
    return merged.reshape(b, s, N_HEADS * HEAD_DIM) @ w_o


def setup_inputs(seed: int = 0) -> dict:
    key = jax.random.key(seed)
    ks = iter(jax.random.split(key, 32))

    def nrm(shape, fan_in):
        return jax.random.normal(next(ks), shape, jnp.float32) * (fan_in ** -0.5)

    def gain(shape):
        return 1.0 + 0.02 * jax.random.normal(next(ks), shape, jnp.float32)

    def bias(shape):
        return 0.01 * jax.random.normal(next(ks), shape, jnp.float32)

    d = D_MODEL
    return {
        "x": jax.random.normal(next(ks), (BATCH, SEQ, d), jnp.float32),
        "mix_pre_g": gain((DEPTH, d)),
        "mix_post_g": gain((DEPTH, d)),
        "ffn_pre_g": gain((DEPTH, d)),
        "ffn_post_g": gain((DEPTH, d)),
        "cm_w_in": nrm((N_A, d, 2 * d), d),
        "cm_b_in": bias((N_A, 2 * d)),
        "cm_dw": nrm((N_A, CONV_WIDTH, d), CONV_WIDTH),
        "cm_dw_b": bias((N_A, d)),
        "cm_ln_g": gain((N_A, d)),
        "cm_ln_b": bias((N_A, d)),
        "cm_w_out": nrm((N_A, d, d), d),
        "cm_b_out": bias((N_A, d)),
        "kv_norm_g": gain((d,)),
        "w_kv": nrm((d, 2 * Q_WIDTH), d),
        "w_q": nrm((N_B, d, Q_WIDTH), d),
        "w_o": nrm((N_B, N_HEADS * HEAD_DIM, d), N_HEADS * HEAD_DIM),
        "ffn_w_in": nrm((DEPTH, d, 2 * D_FF), d),
        "ffn_dw": nrm((DEPTH, FFN_CONV_WIDTH, 2 * D_FF), FFN_CONV_WIDTH),
        "ffn_dw_b": bias((DEPTH, 2 * D_FF)),
        "ffn_w_out": nrm((DEPTH, D_FF, d), D_FF),
    }


def reference(x, mix_pre_g, mix_post_g, ffn_pre_g, ffn_post_g,
              cm_w_in, cm_b_in, cm_dw, cm_dw_b, cm_ln_g, cm_ln_b, cm_w_out, cm_b_out,
              kv_norm_g, w_kv, w_q, w_o,
              ffn_w_in, ffn_dw, ffn_dw_b, ffn_w_out):
    b, s, _ = x.shape
    k_sh = v_sh = None
    for i in range(DEPTH):
        h = rms_norm(x, mix_pre_g[i])
        if i < N_A:
            y = conformer_conv_module(h, cm_w_in[i], cm_b_in[i], cm_dw[i], cm_dw_b[i],
                                      cm_ln_g[i], cm_ln_b[i], cm_w_out[i], cm_b_out[i])
        else:
            j = i - N_A
            y = dilated_mixture_attention(h, w_q[j], w_o[j], k_sh, v_sh)
        x = x + rms_norm(y, mix_post_g[i])
        h = rms_norm(x, ffn_pre_g[i])
        y = conv_ffn(h, ffn_w_in[i], ffn_dw[i], ffn_dw_b[i], ffn_w_out[i])
        x = x + rms_norm(y, ffn_post_g[i])
        if i == N_A - 1:
            kv = (rms_norm(x, kv_norm_g) @ w_kv).reshape(b, s, 2, N_GROUPS, N_HEADS, HEAD_DIM)
            k_sh, v_sh = kv[:, :, 0], kv[:, :, 1]
    return x
```

```python
import numpy as np
import concourse.bass as bass
import concourse.mybir as mybir
from concourse.bass_utils import run_bass_kernel_spmd

F32, BF16 = mybir.dt.float32, mybir.dt.bfloat16
AF = mybir.ActivationFunctionType
ALU = mybir.AluOpType
AX = mybir.AxisListType

D = 1024
S = 2048
NCH = 8
DFF = 2816
NFF = 22
NHEAD = 8
NG = 3
DIL = (1, 4, 16)
EPS = 1e-6
N_CORES = 8
SEQ_PER_CORE = 2


class Prog:
    def __init__(self, nc):
        self.nc = nc
        self.ops = []
        self.alias = {}

    def op(self, eng, fn, reads=(), writes=(), dma=False):
        reads, writes = tuple(reads), tuple(writes)
        extra = tuple(k for k in reads if k[0] in ("ps", "psb") and k not in writes)
        self.ops.append((eng, fn, reads, writes + extra, dma))

    def barrier(self):
        self.ops.append(("barrier", None, (), (), False))

    def emit(self):
        nc = self.nc
        ops = self.ops
        engobj = {"pe": nc.tensor, "act": nc.scalar, "dve": nc.vector,
                  "pool": nc.gpsimd, "sp": nc.sync}
        n = len(ops)
        deps = [None] * n
        needed = [False] * n
        last_w, readers = {}, {}
        last_on = {}
        dmas_since = []
        pend = {e: set() for e in engobj}
        buf_keys, alias_deps = {}, {}
        for i, (eng, fn, R, W, dma) in enumerate(ops):
            if eng == "barrier":
                allp = set(last_on.values()) | set(dmas_since)
                for e in engobj:
                    pend[e] |= allp
                last_w, readers, dmas_since = {}, {}, []
                deps[i] = set()
                continue
            d = set(pend[eng])
            pend[eng] = set()
            for k in R + W:
                bn = k[0]
                buf_keys.setdefault(bn, set()).add(k)
                if bn in self.alias:
                    if bn not in alias_deps:
                        sset = set()
                        for ob in self.alias[bn]:
                            for k2 in buf_keys.get(ob, ()):
                                if k2 in last_w:
                                    sset.add(last_w[k2])
                                sset.update(readers.get(k2, ()))
                        best = {}
                        red = set()
                        for j in sset:
                            if ops[j][4]:
                                red.add(j)
                            else:
                                e_ = ops[j][0]
                                if e_ not in best or best[e_] < j:
                                    best[e_] = j
                        red.update(best.values())
                        alias_deps[bn] = red
                    d |= alias_deps[bn]
            for k in R:
                if k in last_w:
                    d.add(last_w[k])
            for k in W:
                if k in last_w:
                    d.add(last_w[k])
                for r in readers.get(k, ()):
                    d.add(r)
            d2 = set()
            for j in d:
                ej, _, _, _, dj = ops[j]
                if ej == "pe" and eng == "pe" and not dj and not dma:
                    continue
                d2.add(j)
            deps[i] = d2
            for j in d2:
                needed[j] = True
            for k in R:
                readers.setdefault(k, []).append(i)
            for k in W:
                last_w[k] = i
                readers[k] = []
            if dma:
                dmas_since.append(i)
            else:
                last_on[eng] = i
        final = set(last_on.values()) | set(dmas_since)
        for e in engobj:
            final |= pend[e]
        for j in final:
            needed[j] = True
        ROLL = 30000
        sems = {}

        def getsem(name):
            if name not in sems:
                sems[name] = nc.semaphore(name).__enter__()
            return sems[name]

        NPOOL = 24
        cnt = {e: 0 for e in engobj}
        dma_pools = {e: [0, [0] * NPOOL] for e in engobj}
        tok = [None] * n
        known = {e: {} for e in engobj}

        def wait(eng, t):
            name, val = t
            if known[eng].get(name, 0) >= val:
                return
            engobj[eng].wait_ge(getsem(name), val)
            known[eng][name] = val

        for i, (eng, fn, R, W, dma) in enumerate(ops):
            if eng == "barrier":
                continue
            for j in sorted(deps[i]):
                wait(eng, tok[j])
            if dma:
                pool_ = dma_pools[eng]
                k = pool_[0] % NPOOL
                pool_[0] += 1
                name = "dq%s%d" % (eng, k)
                tot = pool_[1]
                if tot[k] > 0:
                    wait(eng, (name, tot[k]))
                tot[k] += 16
                ins = fn()
                ins.then_inc(getsem(name), 16)
                tok[i] = (name, tot[k])
            else:
                ins = fn()
                if needed[i]:
                    cnt[eng] += 1
                    name = "%s%d" % (eng, cnt[eng] // ROLL)
                    val = cnt[eng] % ROLL
                    if val == 0:
                        val = ROLL
                        name = "%s%d" % (eng, cnt[eng] // ROLL - 1)
                    ins.then_inc(getsem(name), 1)
                    tok[i] = (name, val)
        for j in sorted(final):
            if tok[j] is not None:
                wait("sp", tok[j])
        return n


def AP(t, off, dims, F=None, parts=128):
    if F is None:
        F = 1
        for s_ in t.shape[1:]:
            F *= s_
    return bass.AP(t, off, [[F, parts]] + [list(x) for x in dims])


SCALE = 1.0 / float(np.sqrt(128.0))
NEG = -30000.0
ARENA_LO = 16640
ARENA_HI = 229376


def pv_layout():
    cols = {}
    n = [0]

    def add(name, w):
        cols[name] = n[0]
        n[0] += w
    for i in range(4):
        for nm in ("mix_pre_g", "mix_post_g", "ffn_pre_g", "ffn_post_g"):
            add((nm, i), 8)
    add(("kv_norm_g", 0), 8)
    for i in range(2):
        add(("cm_b_in", i), 16)
        add(("cm_dw", i), 248)
        add(("cm_dw_b", i), 8)
        add(("cm_ln_g", i), 8)
        add(("cm_ln_b", i), 8)
        add(("cm_b_out", i), 8)
    for i in range(4):
        add(("ffn_dw", i), 132)
        add(("ffn_dw_b", i), 44)
    return cols, n[0]


PVC, NPV = pv_layout()


def pack_pv(inp):
    pv = np.zeros((128, NPV), np.float32)

    def vec(v):
        v = np.asarray(v, np.float32)
        return v.reshape(-1, 128).T
    for i in range(4):
        for nm in ("mix_pre_g", "mix_post_g", "ffn_pre_g", "ffn_post_g"):
            c = PVC[(nm, i)]
            pv[:, c:c + 8] = vec(inp[nm][i])
    c = PVC[("kv_norm_g", 0)]
    pv[:, c:c + 8] = vec(inp["kv_norm_g"])
    for i in range(2):
        c = PVC[("cm_b_in", i)]
        pv[:, c:c + 16] = vec(inp["cm_b_in"][i])
        c = PVC[("cm_dw", i)]
        dw = np.asarray(inp["cm_dw"][i], np.float32).reshape(31, 8, 128)
        pv[:, c:c + 248] = dw.transpose(2, 1, 0).reshape(128, 248)
        for nm in ("cm_dw_b", "cm_ln_g", "cm_ln_b", "cm_b_out"):
            c = PVC[(nm, i)]
            pv[:, c:c + 8] = vec(inp[nm][i])
    for i in range(4):
        c = PVC[("ffn_dw", i)]
        dw = np.asarray(inp["ffn_dw"][i], np.float32).reshape(3, 44, 128)
        pv[:, c:c + 132] = dw.transpose(2, 1, 0).reshape(128, 132)
        c = PVC[("ffn_dw_b", i)]
        pv[:, c:c + 44] = vec(inp["ffn_dw_b"][i])
    return pv


class Buf:
    def __init__(self, kb, name, F, dtype, nbuf=1):
        self.F = F
        self.key = kb.name(name)
        self.n = nbuf
        self.t = []
        esz = 4 if dtype == F32 else 2
        lo = (kb.aoff + 63) // 64 * 64
        for i in range(nbuf):
            off = (kb.aoff + 63) // 64 * 64
            assert off + F * esz <= ARENA_HI, ("SBUF overflow", name, off, F * esz)
            self.t.append(kb.nc.alloc_sbuf_tensor_at("%s_%d" % (self.key, i), [128, F], dtype, offset=off))
            kb.aoff = off + F * esz
        hi = kb.aoff
        al = [nm for (nm, l2, h2) in kb.all_bufs if l2 < hi and lo < h2]
        if al:
            kb.P.alias[self.key] = al
        kb.all_bufs.append((self.key, lo, hi))

    def ap(self, i, off, dims, parts=128):
        return AP(self.t[i % self.n], off, dims, F=self.F, parts=parts)

    def k(self, i, *sub):
        return (self.key, i % self.n) + tuple(sub)


class K:
    def __init__(self, nseq=SEQ_PER_CORE, stop_after=None):
        self.nseq = nseq
        self.stop_after = stop_after
        nc = bass.Bass("TRN2", target_bir_lowering=False)
        self.nc = nc
        self.P = Prog(nc)
        self.uid = 0
        self.all_bufs = []
        dt = nc.dram_tensor
        self.x_in = dt("x", [nseq, S, D], F32, kind="ExternalInput")
        self.out = dt("out", [nseq, S, D], F32, kind="ExternalOutput")
        self.pv_d = dt("pv", [128, NPV], F32, kind="ExternalInput")
        self.w = {
            "cm_w_in": dt("cm_w_in", [2, D, 2 * D], F32, kind="ExternalInput"),
            "cm_w_out": dt("cm_w_out", [2, D, D], F32, kind="ExternalInput"),
            "w_kv": dt("w_kv", [D, 6144], F32, kind="ExternalInput"),
            "w_q": dt("w_q", [2, D, 3072], F32, kind="ExternalInput"),
            "w_o": dt("w_o", [2, D, D], F32, kind="ExternalInput"),
            "ffn_w_in": dt("ffn_w_in", [4, D, 2 * DFF], F32, kind="ExternalInput"),
            "ffn_w_out": dt("ffn_w_out", [4, DFF, D], F32, kind="ExternalInput"),
        }
        self.scr = {nm: dt("scr_" + nm, [24, 128, S], BF16, kind="Internal") for nm in ("q", "k", "v")}
        self.scr_a = dt("scr_a", [NHEAD, 128, S], BF16, kind="Internal")
        self.wb = {
            "cwin": dt("wb_cwin", [2, 128, NCH * 2 * D], BF16, kind="Internal"),
            "cwout": dt("wb_cwout", [2, 128, NCH * D], BF16, kind="Internal"),
            "wkv": dt("wb_wkv", [48, 128, NCH * 128], BF16, kind="Internal"),
            "wq": dt("wb_wq", [2 * 24, 128, NCH * 128], BF16, kind="Internal"),
            "wo": dt("wb_wo", [2, 128, NCH * D], BF16, kind="Internal"),
            "fwin": dt("wb_fwin", [4 * NFF, 128, NCH * 256], BF16, kind="Internal"),
            "fwout": dt("wb_fwout", [4 * NCH, 128, NFF * 128], BF16, kind="Internal"),
        }
        self.aoff = ARENA_LO
        self.xT = Buf(self, "xT", NCH * S, F32)
        self.identf = Buf(self, "identf", 128, F32)
        self.identb = Buf(self, "identb", 128, BF16)
        self.onesb = Buf(self, "onesb", 128, BF16)
        self.meanb = Buf(self, "meanb", 128, BF16)
        self.maskT = Buf(self, "maskT", 256, BF16)
        self.pv = Buf(self, "pvs", NPV, F32)
        self.halo = Buf(self, "halo", 44 * 2, BF16)
        self.epsb = Buf(self, "epsb", 1, F32)
        self.arena_base = self.aoff
        self.ps = [nc.alloc_psum_tensor("ps%d" % b, [128, 512], F32) for b in range(6)]
        self.psb = [nc.alloc_psum_tensor("psb%d" % b, [128, 1024], BF16) for b in range(2)]
        self.psn = 0
        self.psbn = 0
        self.held = set()

    def name(self, s_):
        self.uid += 1
        return "%s_%d" % (s_, self.uid)

    def bank(self):
        while True:
            b = self.psn % 6
            self.psn += 1
            if b not in self.held:
                return b

    def epsap(self):
        return self.epsb.ap(0, 0, [[1, 1]])

    def pvap(self, col, n=1):
        return self.pv.ap(0, col, [[1, n]])

    def phase(self):
        self.aoff = self.arena_base

    def consts(self):
        nc, P = self.nc, self.P
        idf = self.identf.ap(0, 0, [[1, 128]])
        P.op("pool", lambda: nc.gpsimd.memset(idf, 0.0), writes=[("identf",)])
        P.op("pool", lambda: nc.gpsimd.affine_select(
            out=idf, in_=idf, pattern=[[-1, 128]], compare_op=ALU.not_equal, fill=1.0,
            base=0, channel_multiplier=1), reads=[("identf",)], writes=[("identf",)])
        P.op("dve", lambda: nc.vector.tensor_copy(out=self.identb.ap(0, 0, [[1, 128]]), in_=idf),
             reads=[("identf",)], writes=[("identb",)])
        P.op("dve", lambda: nc.vector.memset(self.onesb.ap(0, 0, [[1, 128]]), 1.0), writes=[("onesb",)])
        P.op("dve", lambda: nc.vector.memset(self.meanb.ap(0, 0, [[1, 128]]), 1.0 / D), writes=[("meanb",)])
        P.op("dve", lambda: nc.vector.memset(self.epsap(), EPS), writes=[("eps",)])
        mk = self.maskT
        P.op("pool", lambda: nc.gpsimd.memset(mk.ap(0, 0, [[1, 256]]), 1.0), writes=[("maskT",)])
        P.op("pool", lambda: nc.gpsimd.affine_select(
            out=mk.ap(0, 0, [[1, 128]]), in_=mk.ap(0, 0, [[1, 128]]), pattern=[[-1, 128]],
            compare_op=ALU.is_ge, fill=0.0, base=0, channel_multiplier=1),
            reads=[("maskT",)], writes=[("maskT",)])
        P.op("pool", lambda: nc.gpsimd.affine_select(
            out=mk.ap(0, 128, [[1, 128]]), in_=mk.ap(0, 128, [[1, 128]]), pattern=[[1, 128]],
            compare_op=ALU.is_ge, fill=0.0, base=0, channel_multiplier=-1),
            reads=[("maskT",)], writes=[("maskT",)])
        P.op("sp", lambda: nc.sync.dma_start(out=self.pv.ap(0, 0, [[1, NPV]]),
                                             in_=bass.AP(self.pv_d, 0, [[NPV, 128], [1, NPV]])),
             writes=[("pv",)], dma=True)

    CK = [("identf",), ("identb",), ("onesb",), ("meanb",), ("maskadd",), ("pv",)]

    def xk(self, c, tile):
        return ("xT", c, tile)

    def xap(self, c, t0, n=512):
        return self.xT.ap(0, c * S + t0, [[1, n]])

    def load_x(self, s_):
        nc, P = self.nc, self.P
        stg = Buf(self, "xstg", D, F32, 2)
        idf = self.identf.ap(0, 0, [[1, 128]])
        for tt in range(S // 128):
            src = bass.AP(self.x_in, (s_ * S + tt * 128) * D, [[D, 128], [1, D]])
            P.op("sp", (lambda tt=tt, src=src: nc.sync.dma_start(out=stg.ap(tt, 0, [[1, D]]), in_=src)),
                 writes=[stg.k(tt)], dma=True)
            for c0 in range(0, NCH, 4):
                b = self.bank()
                pst = self.ps[b]

                def mm(tt=tt, pst=pst, c0=c0):
                    ins = None
                    for cc in range(4):
                        ins = nc.tensor.transpose(AP(pst, cc * 128, [[1, 128]]),
                                                  stg.ap(tt, (c0 + cc) * 128, [[1, 128]]), idf)
                    return ins
                P.op("pe", mm, reads=[stg.k(tt), ("identf",)], writes=[("ps", b)])
                dst = self.xT.ap(0, c0 * S + tt * 128, [[S, 4], [1, 128]])
                srcp = AP(pst, 0, [[128, 4], [1, 128]])
                P.op("act", (lambda dst=dst, srcp=srcp: nc.scalar.copy(out=dst, in_=srcp)),
                     reads=[("ps", b)], writes=[self.xk(c0 + cc, tt // 4) for cc in range(4)])

    def store_x(self, s_):
        nc, P = self.nc, self.P
        stg = Buf(self, "ostg", D, F32, 2)
        idf = self.identf.ap(0, 0, [[1, 128]])
        for tt in range(S // 128):
            for c0 in range(0, NCH, 4):
                b = self.bank()
                pst = self.ps[b]

                def mm(pst=pst, c0=c0, tt=tt):
                    ins = None
                    for cc in range(4):
                        ins = nc.tensor.transpose(AP(pst, cc * 128, [[1, 128]]),
                                                  self.xT.ap(0, (c0 + cc) * S + tt * 128, [[1, 128]]), idf)
                    return ins
                P.op("pe", mm, reads=[self.xk(c0 + cc, tt // 4) for cc in range(4)] + [("identf",)],
                     writes=[("ps", b)])
                dst = stg.ap(tt, c0 * 128, [[1, 512]])
                srcp = AP(pst, 0, [[1, 512]])
                P.op("act", (lambda dst=dst, srcp=srcp: nc.scalar.copy(out=dst, in_=srcp)),
                     reads=[("ps", b)], writes=[stg.k(tt)])
            dstd = bass.AP(self.out, (s_ * S + tt * 128) * D, [[D, 128], [1, D]])
            P.op("pool", (lambda tt=tt, dstd=dstd: nc.gpsimd.dma_start(out=dstd, in_=stg.ap(tt, 0, [[1, D]]))),
                 reads=[stg.k(tt)], writes=[("outd", s_, tt)], dma=True)

    def cast(self, dname, didx, ddims, wname, wbase, sdims):
        nc = self.nc
        dh = self.wb[dname]
        per = 1
        for s_ in dh.shape[1:]:
            per *= s_
        dst = bass.AP(dh, didx * per, [list(x) for x in ddims])
        src = bass.AP(self.w[wname], wbase, [list(x) for x in sdims])
        self.P.op("pool", (lambda: nc.gpsimd.dma_start(out=dst, in_=src)), writes=[("wb", dname, didx)], dma=True)

    def precast_conf(self, li):
        for q4 in range(4):
            pass
        nc = self.nc
        dh = self.wb["cwin"]
        for q4 in range(4):
            dst = bass.AP(dh, li * 128 * NCH * 2 * D + q4 * 512, [[NCH * 2 * D, 128], [2 * D, NCH], [1, 512]])
            src = bass.AP(self.w["cm_w_in"], li * D * 2 * D + q4 * 512, [[2 * D, 128], [128 * 2 * D, NCH], [1, 512]])
            self.P.op("pool", (lambda dst=dst, src=src: nc.gpsimd.dma_start(out=dst, in_=src)),
                      writes=[("wb", "cwin", li, q4)], dma=True)
        dh = self.wb["cwout"]
        for q2 in range(2):
            dst = bass.AP(dh, li * 128 * NCH * D + q2 * 512, [[NCH * D, 128], [D, NCH], [1, 512]])
            src = bass.AP(self.w["cm_w_out"], li * D * D + q2 * 512, [[D, 128], [128 * D, NCH], [1, 512]])
            self.P.op("pool", (lambda dst=dst, src=src: nc.gpsimd.dma_start(out=dst, in_=src)),
                      writes=[("wb", "cwout", li, q2)], dma=True)

    def precast_ffn(self, li):
        nc = self.nc
        dh = self.wb["fwin"]
        for j in range(NFF):
            for half in range(2):
                dst = bass.AP(dh, (li * NFF + j) * 128 * NCH * 256 + half * 128,
                              [[NCH * 256, 128], [256, NCH], [1, 128]])
                src = bass.AP(self.w["ffn_w_in"], li * D * 2 * DFF + half * DFF + j * 128,
                              [[2 * DFF, 128], [128 * 2 * DFF, NCH], [1, 128]])
                self.P.op("pool", (lambda dst=dst, src=src: nc.gpsimd.dma_start(out=dst, in_=src)),
                          writes=[("wb", "fwin", li, j, half)], dma=True)
        dh = self.wb["fwout"]
        for m in range(NCH):
            dst = bass.AP(dh, (li * NCH + m) * 128 * NFF * 128, [[NFF * 128, 128], [128, NFF], [1, 128]])
            src = bass.AP(self.w["ffn_w_out"], li * DFF * D + m * 128, [[D, 128], [128 * D, NFF], [1, 128]])
            self.P.op("pool", (lambda dst=dst, src=src: nc.gpsimd.dma_start(out=dst, in_=src)),
                      writes=[("wb", "fwout", li, m)], dma=True)

    def precast_proj(self, dname, base_idx, n, wname, wbase, rowstride, col0s):
        nc = self.nc
        dh = self.wb[dname]
        for i in range(n):
            dst = bass.AP(dh, (base_idx + i) * 128 * NCH * 128, [[NCH * 128, 128], [128, NCH], [1, 128]])
            src = bass.AP(self.w[wname], wbase + col0s[i], [[rowstride, 128], [128 * rowstride, NCH], [1, 128]])
            self.P.op("pool", (lambda dst=dst, src=src: nc.gpsimd.dma_start(out=dst, in_=src)),
                      writes=[("wb", dname, base_idx + i)], dma=True)

    def precast_wo(self, lj):
        nc = self.nc
        dh = self.wb["wo"]
        for q2 in range(2):
            dst = bass.AP(dh, lj * 128 * NCH * D + q2 * 512, [[NCH * D, 128], [D, NCH], [1, 512]])
            src = bass.AP(self.w["w_o"], lj * D * D + q2 * 512, [[D, 128], [128 * D, NCH], [1, 512]])
            self.P.op("pool", (lambda dst=dst, src=src: nc.gpsimd.dma_start(out=dst, in_=src)),
                      writes=[("wb", "wo", lj, q2)], dma=True)

    def precast_all(self):
        kvcols = [gh * 128 for gh in range(24)] + [3072 + gh * 128 for gh in range(24)]
        self.precast_conf(0)
        self.precast_ffn(0)
        self.precast_conf(1)
        self.precast_ffn(1)
        self.precast_proj("wkv", 0, 48, "w_kv", 0, 6144, kvcols)
        for lj in range(2):
            self.precast_proj("wq", lj * 24, 24, "w_q", lj * D * 3072, 3072, [gh * 128 for gh in range(24)])
            self.precast_wo(lj)
            self.precast_ffn(2 + lj)

    def wload(self, dst_ap, key, dname, didx, off, n, rkeys):
        nc = self.nc
        dh = self.wb[dname]
        per = 1
        for s_ in dh.shape[1:]:
            per *= s_
        rowlen = per // 128
        src = bass.AP(dh, didx * per + off, [[rowlen, 128], [1, n]])
        self.P.op("sp", (lambda: nc.sync.dma_start(out=dst_ap, in_=src)), reads=rkeys, writes=[key], dma=True)

    def rms_rstd(self, src, srck, rstd_ap, rstd_key, sq, n=512):
        nc, P = self.nc, self.P
        b = self.bank()
        pst = AP(self.ps[b], 0, [[1, n]])
        for c in range(NCH):
            i = self.uid
            self.uid += 1
            sa = sq.ap(i, 0, [[1, n]])
            P.op("act", (lambda sa=sa, c=c: nc.scalar.activation(out=sa, in_=src(c), func=AF.Square)),
                 reads=[srck(c)], writes=[sq.k(i)])
            P.op("pe", (lambda sa=sa, c=c: nc.tensor.matmul(pst, lhsT=self.meanb.ap(0, 0, [[1, 128]]), rhs=sa,
                                                            start=(c == 0), stop=(c == NCH - 1))),
                 reads=[sq.k(i), ("meanb",)], writes=[("ps", b)])
        P.op("act", (lambda: nc.scalar.activation(out=rstd_ap, in_=pst, func=AF.Sqrt, bias=self.epsap(), scale=1.0)),
             reads=[("ps", b), ("eps",)], writes=[rstd_key])
        P.op("dve", (lambda: nc.vector.reciprocal(out=rstd_ap, in_=rstd_ap)),
             reads=[rstd_key], writes=[rstd_key])

    def pre_norm(self, gcol, t0, hT, hi, sq, rstd, n=512, hoff=0, hstride=None, slot=0):
        nc, P = self.nc, self.P
        if hstride is None:
            hstride = n
        tile = t0 // 512
        ri = self.uid
        self.uid += 1
        self.rms_rstd(lambda c: self.xap(c, t0, n), lambda c: self.xk(c, tile),
                      rstd.ap(ri, 0, [[1, n]]), rstd.k(ri), sq, n)
        for c in range(NCH):
            P.op("dve", (lambda c=c: nc.vector.scalar_tensor_tensor(
                out=hT.ap(hi, hoff + c * hstride, [[1, n]]), in0=self.xap(c, t0, n), scalar=self.pvap(gcol + c),
                in1=rstd.ap(ri, 0, [[1, n]]), op0=ALU.mult, op1=ALU.mult)),
                reads=[self.xk(c, tile), ("pv",), rstd.k(ri)], writes=[hT.k(hi, c, slot)])

    def post_norm_residual(self, gcol, t0, yt, sq, rstd, tmp, n=512):
        nc, P = self.nc, self.P
        tile = t0 // 512
        ri = self.uid
        self.uid += 1
        self.rms_rstd(lambda c: yt.ap(0, c * n, [[1, n]]), lambda c: yt.k(0, c),
                      rstd.ap(ri, 0, [[1, n]]), rstd.k(ri), sq, n)
        for c in range(NCH):
            ti = self.uid
            self.uid += 1
            P.op("dve", (lambda c=c, ti=ti: nc.vector.scalar_tensor_tensor(
                out=tmp.ap(ti, 0, [[1, n]]), in0=yt.ap(0, c * n, [[1, n]]), scalar=self.pvap(gcol + c),
                in1=rstd.ap(ri, 0, [[1, n]]), op0=ALU.mult, op1=ALU.mult)),
                reads=[yt.k(0, c), ("pv",), rstd.k(ri)], writes=[tmp.k(ti)])
            P.op("dve", (lambda c=c, ti=ti: nc.vector.tensor_tensor(
                out=self.xap(c, t0, n), in0=self.xap(c, t0, n), in1=tmp.ap(ti, 0, [[1, n]]), op=ALU.add)),
                reads=[tmp.k(ti), self.xk(c, tile)], writes=[self.xk(c, tile)])

    def outproj_postnorm(self, mmf, evf, gcol, tcol, yt, sq, rstd, tmp, add_eng="pool"):
        nc, P = self.nc, self.P
        bS = self.bank()
        self.held = {bS}
        pst = AP(self.ps[bS], 0, [[1, 512]])
        ri = self.uid
        self.uid += 1
        sqs = []

        def stat(m):
            sa, sk = sqs[m]
            P.op("pe", (lambda: nc.tensor.matmul(pst, lhsT=self.meanb.ap(0, 0, [[1, 128]]), rhs=sa,
                                                 start=(m == 0), stop=(m == NCH - 1))),
                 reads=[sk, ("meanb",)], writes=[("ps", bS)])
        for m in range(NCH):
            b = self.bank()
            fn, reads = mmf(m, b)
            P.op("pe", fn, reads=reads, writes=[("ps", b)])
            if m > 0:
                stat(m - 1)
            efn, ereads = evf(m, b)
            P.op("act", efn, reads=[("ps", b)] + ereads, writes=[yt.k(0, m)])
            qi = self.uid
            self.uid += 1
            sa = sq.ap(qi, 0, [[1, 512]])
            sqs.append((sa, sq.k(qi)))
            P.op("act", (lambda m=m, sa=sa: nc.scalar.activation(
                out=sa, in_=yt.ap(0, m * 512, [[1, 512]]), func=AF.Square)),
                reads=[yt.k(0, m)], writes=[sq.k(qi)])
        stat(NCH - 1)
        self.held = set()
        ra = rstd.ap(ri, 0, [[1, 512]])
        P.op("act", (lambda: nc.scalar.activation(out=ra, in_=pst, func=AF.Sqrt, bias=self.epsap(), scale=1.0)),
             reads=[("ps", bS), ("eps",)], writes=[rstd.k(ri)])
        P.op("dve", (lambda: nc.vector.reciprocal(out=ra, in_=ra)), reads=[rstd.k(ri)], writes=[rstd.k(ri)])
        tile = tcol // 512
        for c in range(NCH):
            ti = self.uid
            self.uid += 1
            P.op("dve", (lambda c=c, ti=ti: nc.vector.scalar_tensor_tensor(
                out=tmp.ap(ti, 0, [[1, 512]]), in0=yt.ap(0, c * 512, [[1, 512]]), scalar=self.pvap(gcol + c),
                in1=ra, op0=ALU.mult, op1=ALU.mult)),
                reads=[yt.k(0, c), ("pv",), rstd.k(ri)], writes=[tmp.k(ti)])
            eobj = nc.gpsimd if add_eng == "pool" else nc.vector
            P.op(add_eng, (lambda c=c, ti=ti, eobj=eobj: eobj.tensor_tensor(
                out=self.xap(c, tcol), in0=self.xap(c, tcol), in1=tmp.ap(ti, 0, [[1, 512]]), op=ALU.add)),
                reads=[tmp.k(ti), self.xk(c, tile)], writes=[self.xk(c, tile)])

    def conformer(self, li):
        nc, P = self.nc, self.P
        self.phase()
        GL = 30 + S
        glu = Buf(self, "glu", NCH * GL, BF16)
        mark = self.aoff
        win = Buf(self, "cwin", NCH * 2 * D, BF16)
        hT = Buf(self, "chT", NCH * 512, BF16, 2)
        sq = Buf(self, "csq", 512, BF16, 2)
        rstd = Buf(self, "crstd", 512, F32, 2)
        sig = Buf(self, "csig", 512, F32, 2)
        for q4 in range(4):
            self.wload(win.ap(0, q4 * 4096, [[1, 4096]]), win.k(0, q4), "cwin", li, q4 * 4096, 4096,
                       [("wb", "cwin", li, qq) for qq in range(4)])
        for c in range(NCH):
            P.op("dve", (lambda c=c: nc.vector.memset(glu.ap(0, c * GL, [[1, 30]]), 0.0)), writes=[glu.k(0, c, "z")])
        bcol = PVC[("cm_b_in", li)]
        for tile in range(4):
            t0 = tile * 512
            self.pre_norm(PVC[("mix_pre_g", li)], t0, hT, tile, sq, rstd)
            for m in range(NCH):
                ba, bg = self.bank(), self.bank()
                for (b, oc) in ((ba, m), (bg, m + 8)):
                    def mm(b=b, oc=oc, tile=tile):
                        ins = None
                        for kc in range(NCH):
                            ins = nc.tensor.matmul(AP(self.ps[b], 0, [[1, 512]]),
                                                   lhsT=win.ap(0, kc * 2 * D + oc * 128, [[1, 128]]),
                                                   rhs=hT.ap(tile, kc * 512, [[1, 512]]),
                                                   start=(kc == 0), stop=(kc == NCH - 1))
                        return ins
                    P.op("pe", mm, reads=[win.k(0, qq) for qq in range(4)] + [hT.k(tile, kc, 0) for kc in range(NCH)],
                         writes=[("ps", b)])
                si = self.uid
                self.uid += 1
                P.op("act", (lambda bg=bg, si=si, m=m: nc.scalar.activation(
                    out=sig.ap(si, 0, [[1, 512]]), in_=AP(self.ps[bg], 0, [[1, 512]]), func=AF.Sigmoid,
                    bias=self.pvap(bcol + 8 + m), scale=1.0)),
                    reads=[("ps", bg), ("pv",)], writes=[sig.k(si)])
                P.op("dve", (lambda ba=ba, si=si, m=m, t0=t0: nc.vector.scalar_tensor_tensor(
                    out=glu.ap(0, m * GL + 30 + t0, [[1, 512]]), in0=AP(self.ps[ba], 0, [[1, 512]]),
                    scalar=self.pvap(bcol + m), in1=sig.ap(si, 0, [[1, 512]]), op0=ALU.add, op1=ALU.mult)),
                    reads=[("ps", ba), ("pv",), sig.k(si)], writes=[glu.k(0, m, tile)])
        self.aoff = mark
        wout = Buf(self, "cwout", NCH * D, BF16)
        diag = Buf(self, "cdiag", 31 * 128, BF16, 2)
        vt = Buf(self, "cvt", NCH * 512, F32)
        vb = Buf(self, "cvb", 512, BF16, 2)
        sq = Buf(self, "csq2", 512, BF16, 2)
        st4 = Buf(self, "cst4", 512, F32, 4)
        sT = Buf(self, "csT", NCH * 512, BF16)
        yt = Buf(self, "cyt", NCH * 512, F32)
        rstd = Buf(self, "crstd2", 512, F32, 2)
        tmp = Buf(self, "ctmp", 512, F32, 2)
        for q2 in range(2):
            self.wload(wout.ap(0, q2 * 4096, [[1, 4096]]), wout.k(0, q2), "cwout", li, q2 * 4096, 4096,
                       [("wb", "cwout", li, qq) for qq in range(2)])
        dwc = PVC[("cm_dw", li)]
        idb = self.identb
        for tile in range(4):
            t0 = tile * 512
            bM, bQ = self.bank(), self.bank()
            self.held = {bM, bQ}
            for c in range(NCH):
                di = tile * NCH + c
                P.op("dve", (lambda c=c, di=di: nc.vector.tensor_tensor(
                    out=diag.ap(di, 0, [[128, 31], [1, 128]]), in0=idb.ap(0, 0, [[0, 31], [1, 128]]),
                    in1=self.pv.ap(0, dwc + c * 31, [[1, 31], [0, 128]]), op=ALU.mult)),
                    reads=[("identb",), ("pv",)], writes=[diag.k(di)])
                b = self.bank()

                def mm(c=c, di=di, b=b, t0=t0):
                    ins = None
                    for k in range(31):
                        ins = nc.tensor.matmul(AP(self.ps[b], 0, [[1, 512]]),
                                               lhsT=diag.ap(di, k * 128, [[1, 128]]),
                                               rhs=glu.ap(0, c * GL + t0 + k, [[1, 512]]),
                                               start=(k == 0), stop=(k == 30))
                    return ins
                rk = [glu.k(0, c, tile), glu.k(0, c, "z")] + ([glu.k(0, c, tile - 1)] if tile > 0 else [])
                P.op("pe", mm, reads=rk + [diag.k(di)], writes=[("ps", b)])
                pb = AP(self.ps[b], 0, [[1, 512]])
                bias = self.pvap(PVC[("cm_dw_b", li)] + c)
                vi = self.uid
                self.uid += 1
                P.op("act", (lambda c=c, pb=pb, bias=bias: nc.scalar.activation(
                    out=vt.ap(0, c * 512, [[1, 512]]), in_=pb, func=AF.Identity, bias=bias, scale=1.0)),
                    reads=[("ps", b), ("pv",)], writes=[vt.k(0, c)])
                P.op("act", (lambda vi=vi, pb=pb, bias=bias: nc.scalar.activation(
                    out=vb.ap(vi, 0, [[1, 512]]), in_=pb, func=AF.Identity, bias=bias, scale=1.0)),
                    reads=[("ps", b), ("pv",)], writes=[vb.k(vi)])
                P.op("act", (lambda vi=vi, pb=pb, bias=bias: nc.scalar.activation(
                    out=sq.ap(vi, 0, [[1, 512]]), in_=pb, func=AF.Square, bias=bias, scale=1.0)),
                    reads=[("ps", b), ("pv",)], writes=[sq.k(vi)])
                mb = self.meanb.ap(0, 0, [[1, 128]])
                P.op("pe", (lambda vi=vi, c=c, bM=bM: nc.tensor.matmul(
                    AP(self.ps[bM], 0, [[1, 512]]), lhsT=mb, rhs=vb.ap(vi, 0, [[1, 512]]),
                    start=(c == 0), stop=(c == NCH - 1))), reads=[vb.k(vi), ("meanb",)], writes=[("ps", bM)])
                P.op("pe", (lambda vi=vi, c=c, bQ=bQ: nc.tensor.matmul(
                    AP(self.ps[bQ], 0, [[1, 512]]), lhsT=mb, rhs=sq.ap(vi, 0, [[1, 512]]),
                    start=(c == 0), stop=(c == NCH - 1))), reads=[sq.k(vi), ("meanb",)], writes=[("ps", bQ)])
            self.held = set()
            mu, var, rs = (st4.ap(j, 0, [[1, 512]]) for j in range(3))
            pM, pQ = AP(self.ps[bM], 0, [[1, 512]]), AP(self.ps[bQ], 0, [[1, 512]])
            P.op("dve", (lambda mu=mu, pM=pM: nc.vector.tensor_copy(out=mu, in_=pM)), reads=[("ps", bM)], writes=[st4.k(0)])
            P.op("dve", (lambda mu=mu, var=var: nc.vector.tensor_tensor(out=var, in0=mu, in1=mu, op=ALU.mult)),
                 reads=[st4.k(0)], writes=[st4.k(1)])
            P.op("dve", (lambda pQ=pQ, var=var: nc.vector.tensor_tensor(out=var, in0=pQ, in1=var, op=ALU.subtract)),
                 reads=[("ps", bQ), st4.k(1)], writes=[st4.k(1)])
            P.op("act", (lambda rs=rs, var=var: nc.scalar.activation(out=rs, in_=var, func=AF.Sqrt, bias=self.epsap(), scale=1.0)),
                 reads=[st4.k(1), ("eps",)], writes=[st4.k(2)])
            P.op("dve", (lambda rs=rs: nc.vector.reciprocal(out=rs, in_=rs)),
                 reads=[st4.k(2)], writes=[st4.k(2)])
            for c in range(NCH):
                va = vt.ap(0, c * 512, [[1, 512]])
                P.op("dve", (lambda va=va, mu=mu: nc.vector.tensor_tensor(out=va, in0=va, in1=mu, op=ALU.subtract)),
                     reads=[vt.k(0, c), st4.k(0)], writes=[vt.k(0, c)])
                P.op("dve", (lambda va=va, rs=rs: nc.vector.tensor_tensor(out=va, in0=va, in1=rs, op=ALU.mult)),
                     reads=[vt.k(0, c), st4.k(2)], writes=[vt.k(0, c)])
                P.op("act", (lambda va=va, c=c: nc.scalar.activation(
                    out=sT.ap(0, c * 512, [[1, 512]]), in_=va, func=AF.Silu,
                    bias=self.pvap(PVC[("cm_ln_b", li)] + c), scale=self.pvap(PVC[("cm_ln_g", li)] + c))),
                    reads=[vt.k(0, c), ("pv",)], writes=[sT.k(0, c)])
            def mmf(m, b):
                def mm():
                    ins = None
                    for kc in range(NCH):
                        ins = nc.tensor.matmul(AP(self.ps[b], 0, [[1, 512]]),
                                               lhsT=wout.ap(0, kc * D + m * 128, [[1, 128]]),
                                               rhs=sT.ap(0, kc * 512, [[1, 512]]),
                                               start=(kc == 0), stop=(kc == NCH - 1))
                    return ins
                return mm, [wout.k(0, 0), wout.k(0, 1)] + [sT.k(0, kc) for kc in range(NCH)]

            def evf(m, b):
                return (lambda: nc.scalar.activation(
                    out=yt.ap(0, m * 512, [[1, 512]]), in_=AP(self.ps[b], 0, [[1, 512]]), func=AF.Identity,
                    bias=self.pvap(PVC[("cm_b_out", li)] + m), scale=1.0)), [("pv",)]
            self.outproj_postnorm(mmf, evf, PVC[("mix_post_g", li)], t0, yt, sq, rstd, tmp, add_eng="dve")

    def ffn(self, li):
        nc, P = self.nc, self.P
        self.phase()
        TT = 1024
        SG = TT + 4
        hT = Buf(self, "fhT", NCH * TT, BF16)
        win = Buf(self, "fwin", NCH * 256, BF16, 3)
        stg = Buf(self, "fstg", 2 * SG, BF16, 2)
        acc = Buf(self, "facc", 2 * TT, F32, 2)
        gated = Buf(self, "fgated", NFF * TT, BF16)
        wout = Buf(self, "fwout", NFF * 128, BF16, 2)
        yt = Buf(self, "fyt", NCH * 512, F32)
        sq = Buf(self, "fsq", 512, BF16, 2)
        rstd = Buf(self, "frstd", 512, F32, 2)
        tmp = Buf(self, "ftmp", 512, F32, 2)
        P.op("dve", (lambda: nc.vector.memset(self.halo.ap(0, 0, [[1, 88]]), 0.0)),
             writes=[("halo", j) for j in range(44)])
        dwc, dbc = PVC[("ffn_dw", li)], PVC[("ffn_dw_b", li)]

        def load_win(wi):
            j = wi % NFF
            self.wload(win.ap(wi, 0, [[1, NCH * 256]]), win.k(wi), "fwin", li * NFF + j, 0, NCH * 256,
                       [("wb", "fwin", li, j, 0), ("wb", "fwin", li, j, 1)])

        def load_wout(wo):
            m = wo % NCH
            self.wload(wout.ap(wo, 0, [[1, NFF * 128]]), wout.k(wo), "fwout", li * NCH + m, 0, NFF * 128,
                       [("wb", "fwout", li, m)])
        nwin = 2 * NFF
        load_win(0)
        load_win(1)

        def prenorm(hs):
            for sub in range(2):
                self.pre_norm(PVC[("ffn_pre_g", li)], hs * TT + sub * 512, hT, 0, sq, rstd, hoff=sub * 512,
                              hstride=TT, slot=sub)

        def stageA(hs, j):
            wi = hs * NFF + j
            si = wi
            for half in range(2):
                ch = j + half * NFF
                so = half * SG
                P.op("dve", (lambda so=so, ch=ch: nc.vector.tensor_copy(
                    out=stg.ap(si, so, [[1, 2]]), in_=self.halo.ap(0, ch * 2, [[1, 2]]))),
                    reads=[("halo", ch)], writes=[stg.k(si, half, "h")])
                for sub in range(2):
                    b = self.bank()

                    def mm(b=b, half=half, sub=sub):
                        ins = None
                        for kc in range(NCH):
                            ins = nc.tensor.matmul(AP(self.ps[b], 0, [[1, 512]]),
                                                   lhsT=win.ap(wi, kc * 256 + half * 128, [[1, 128]]),
                                                   rhs=hT.ap(0, kc * TT + sub * 512, [[1, 512]]),
                                                   start=(kc == 0), stop=(kc == NCH - 1))
                        return ins
                    P.op("pe", mm, reads=[win.k(wi)] + [hT.k(0, kc, sub) for kc in range(NCH)],
                         writes=[("ps", b)])
                    P.op("act", (lambda b=b, so=so, sub=sub: nc.scalar.copy(
                        out=stg.ap(si, so + 2 + sub * 512, [[1, 512]]), in_=AP(self.ps[b], 0, [[1, 512]]))),
                        reads=[("ps", b)], writes=[stg.k(si, half, sub)])
                P.op("dve", (lambda so=so, ch=ch: nc.vector.tensor_copy(
                    out=self.halo.ap(0, ch * 2, [[1, 2]]), in_=stg.ap(si, so + TT, [[1, 2]]))),
                    reads=[stg.k(si, half, 1)], writes=[("halo", ch)])
                aa = acc.ap(si, half * TT, [[1, TT]])
                rk = [stg.k(si, half, 0), stg.k(si, half, 1), stg.k(si, half, "h"), ("pv",)]
                P.op("act", (lambda so=so, ch=ch, aa=aa: nc.scalar.activation(
                    out=aa, in_=stg.ap(si, so, [[1, TT]]), func=AF.Identity,
                    bias=self.pvap(dbc + ch), scale=self.pvap(dwc + ch * 3))),
                    reads=rk, writes=[acc.k(si, half)])
                for k in (1, 2):
                    P.op("dve", (lambda so=so, ch=ch, aa=aa, k=k: nc.vector.scalar_tensor_tensor(
                        out=aa, in0=stg.ap(si, so + k, [[1, TT]]), scalar=self.pvap(dwc + ch * 3 + k),
                        in1=aa, op0=ALU.mult, op1=ALU.add)),
                        reads=rk + [acc.k(si, half)], writes=[acc.k(si, half)])

        def stageB(hs, j):
            si = hs * NFF + j
            ag = acc.ap(si, TT, [[1, TT]])
            P.op("act", (lambda: nc.scalar.activation(out=ag, in_=ag, func=AF.Silu)),
                 reads=[acc.k(si, 1)], writes=[acc.k(si, 1)])
            P.op("pool", (lambda: nc.gpsimd.tensor_tensor(
                out=gated.ap(0, j * TT, [[1, TT]]), in0=acc.ap(si, 0, [[1, TT]]), in1=ag, op=ALU.mult)),
                reads=[acc.k(si, 0), acc.k(si, 1)], writes=[gated.k(0, j)])

        prenorm(0)
        for hs in range(2):
            t0 = hs * TT
            for j in range(NFF):
                wi = hs * NFF + j
                if wi + 2 < nwin:
                    load_win(wi + 2)
                if j == NFF - 1:
                    load_wout(hs * 2 * NCH)
                stageA(hs, j)
                if j > 0:
                    stageB(hs, j - 1)
            stageB(hs, NFF - 1)
            if hs == 0:
                prenorm(1)
            gcol = PVC[("ffn_post_g", li)]
            for sub in range(2):
                tcol = t0 + sub * 512
                bS = self.bank()
                self.held = {bS}
                pst = AP(self.ps[bS], 0, [[1, 512]])
                ri = self.uid
                self.uid += 1
                sqs = []

                def stat(m, bS=bS, pst=pst):
                    sa, sk = sqs[m]
                    P.op("pe", (lambda: nc.tensor.matmul(pst, lhsT=self.meanb.ap(0, 0, [[1, 128]]), rhs=sa,
                                                         start=(m == 0), stop=(m == NCH - 1))),
                         reads=[sk, ("meanb",)], writes=[("ps", bS)])
                for m in range(NCH):
                    wo = (hs * 2 + sub) * NCH + m
                    if not (sub == 1 and m == NCH - 1):
                        load_wout(wo + 1)
                    b = self.bank()

                    def mm(b=b, wo=wo, sub=sub):
                        ins = None
                        for kc in range(NFF):
                            ins = nc.tensor.matmul(AP(self.ps[b], 0, [[1, 512]]),
                                                   lhsT=wout.ap(wo, kc * 128, [[1, 128]]),
                                                   rhs=gated.ap(0, kc * TT + sub * 512, [[1, 512]]),
                                                   start=(kc == 0), stop=(kc == NFF - 1))
                        return ins
                    P.op("pe", mm, reads=[wout.k(wo)] + [gated.k(0, kc) for kc in range(NFF)], writes=[("ps", b)])
                    if m > 0:
                        stat(m - 1)
                    P.op("act", (lambda m=m, b=b: nc.scalar.copy(
                        out=yt.ap(0, m * 512, [[1, 512]]), in_=AP(self.ps[b], 0, [[1, 512]]))),
                        reads=[("ps", b)], writes=[yt.k(0, m)])
                    qi = self.uid
                    self.uid += 1
                    sa = sq.ap(qi, 0, [[1, 512]])
                    sqs.append((sa, sq.k(qi)))
                    P.op("act", (lambda m=m, sa=sa: nc.scalar.activation(
                        out=sa, in_=yt.ap(0, m * 512, [[1, 512]]), func=AF.Square)),
                        reads=[yt.k(0, m)], writes=[sq.k(qi)])
                stat(NCH - 1)
                self.held = set()
                ra = rstd.ap(ri, 0, [[1, 512]])
                P.op("act", (lambda ra=ra, pst=pst: nc.scalar.activation(out=ra, in_=pst, func=AF.Sqrt,
                                                                         bias=self.epsap(), scale=1.0)),
                     reads=[("ps", bS), ("eps",)], writes=[rstd.k(ri)])
                P.op("dve", (lambda ra=ra: nc.vector.reciprocal(out=ra, in_=ra)), reads=[rstd.k(ri)], writes=[rstd.k(ri)])
                tile = tcol // 512
                for c in range(NCH):
                    ti = self.uid
                    self.uid += 1
                    P.op("dve", (lambda c=c, ti=ti, ra=ra: nc.vector.scalar_tensor_tensor(
                        out=tmp.ap(ti, 0, [[1, 512]]), in0=yt.ap(0, c * 512, [[1, 512]]), scalar=self.pvap(gcol + c),
                        in1=ra, op0=ALU.mult, op1=ALU.mult)),
                        reads=[yt.k(0, c), ("pv",), rstd.k(ri)], writes=[tmp.k(ti)])
                    P.op("pool", (lambda c=c, ti=ti, tcol=tcol: nc.gpsimd.tensor_tensor(
                        out=self.xap(c, tcol), in0=self.xap(c, tcol), in1=tmp.ap(ti, 0, [[1, 512]]), op=ALU.add)),
                        reads=[tmp.k(ti), self.xk(c, tile)], writes=[self.xk(c, tile)])

    def project(self, gcol, dname, base_idx, outs):
        nc, P = self.nc, self.P
        self.phase()
        hT = Buf(self, "phT", NCH * S, BF16)
        sq = Buf(self, "psq", 512, BF16, 2)
        rstd = Buf(self, "prstd", 512, F32, 2)
        wp = Buf(self, "pw", NCH * 128, BF16, 3)
        stg = Buf(self, "pstg", S, BF16, 2)

        def load(oi):
            self.wload(wp.ap(oi, 0, [[1, NCH * 128]]), wp.k(oi), dname, base_idx + oi, 0, NCH * 128,
                       [("wb", dname, base_idx + oi)])
        load(0)
        load(1)
        self.pre_norm(gcol, 0, hT, 0, sq, rstd, hoff=0, hstride=S, slot=0)
        self.pre_norm(gcol, 512, hT, 0, sq, rstd, hoff=512, hstride=S, slot=1)
        for oi, (sname, idx) in enumerate(outs):
            if oi + 2 < len(outs):
                load(oi + 2)
            for tile in range(4):
                if oi == 0 and tile < 2:
                    self.pre_norm(gcol, (tile + 2) * 512, hT, 0, sq, rstd, hoff=(tile + 2) * 512, hstride=S,
                                  slot=tile + 2)
                b = self.bank()

                def mm(b=b, oi=oi, tile=tile):
                    ins = None
                    for kc in range(NCH):
                        ins = nc.tensor.matmul(AP(self.ps[b], 0, [[1, 512]]),
                                               lhsT=wp.ap(oi, kc * 128, [[1, 128]]),
                                               rhs=hT.ap(0, kc * S + tile * 512, [[1, 512]]),
                                               start=(kc == 0), stop=(kc == NCH - 1))
                    return ins
                P.op("pe", mm, reads=[wp.k(oi)] + [hT.k(0, kc, tile) for kc in range(NCH)], writes=[("ps", b)])
                P.op("act", (lambda b=b, oi=oi, tile=tile: nc.scalar.copy(
                    out=stg.ap(oi, tile * 512, [[1, 512]]), in_=AP(self.ps[b], 0, [[1, 512]]))),
                    reads=[("ps", b)], writes=[stg.k(oi, tile)])
            dst = bass.AP(self.scr[sname], idx * 128 * S, [[S, 128], [1, S]])
            P.op("pool", (lambda oi=oi, dst=dst: nc.gpsimd.dma_start(out=dst, in_=stg.ap(oi, 0, [[1, S]]))),
                 reads=[stg.k(oi, t) for t in range(4)], writes=[("scr", sname, idx)], dma=True)

    def attention(self, lj, li):
        nc, P = self.nc, self.P
        self.project(PVC[("mix_pre_g", li)], "wq", lj * 24, [("q", gh) for gh in range(24)])
        self.phase()
        mark = self.aoff
        OM01 = Buf(self, "aOM01", 4 * S, BF16)
        OM2 = Buf(self, "aOM2", 2 * S, BF16, 2)
        Dd01 = Buf(self, "aDd01", 2 * S, F32)
        Dd2 = Buf(self, "aDd2", S, F32, 2)
        tA = Buf(self, "atA", S, BF16)
        tE = Buf(self, "atE", S, F32)
        nacc = Buf(self, "anacc", S, F32)
        Dacc = Buf(self, "aDacc", S, F32)
        aS = Buf(self, "aaS", S, BF16, 2)
        qkv = Buf(self, "aqkv", 3 * S, BF16, 2)
        vtok = Buf(self, "avtok", 16 * 128, BF16)
        pbuf = Buf(self, "aP", 256, BF16, 4)
        ptb = Buf(self, "aPT", 256, BF16, 4)
        dg = Buf(self, "adg", 128, BF16, 6)
        st = Buf(self, "ast", 2, F32, 6)
        mbf = Buf(self, "ambf", 2, BF16, 6)
        idb = self.identb.ap(0, 0, [[1, 128]])
        ones = self.onesb.ap(0, 0, [[1, 128]])
        ghs = [(h, g) for h in range(NHEAD) for g in (2, 1, 0)]

        def load_qkv(ci):
            h, g = ghs[ci]
            gh = g * 8 + h
            for qi, nm in enumerate(("q", "k", "v")):
                src = bass.AP(self.scr[nm], gh * 128 * S, [[S, 128], [1, S]])
                P.op("sp", (lambda qi=qi, src=src, ci=ci: nc.sync.dma_start(out=qkv.ap(ci, qi * S, [[1, S]]), in_=src)),
                     reads=[("scr", nm, gh)], writes=[qkv.k(ci, qi)], dma=True)

        def wins(g, n):
            return [0, 1, 2, 3] if g == 2 else ([n] if g == 1 else [n // 4])
        load_qkv(0)
        bi = 0
        pending = []
        sbank = [0]
        xbank = [0]
        mcount = [0]
        for ci, (h, g) in enumerate(ghs):
            if ci + 1 < len(ghs):
                load_qkv(ci + 1)
            r = DIL[g]
            nb = (S // r) // 128
            blocks = [(rho, n) for rho in range(r) for n in range(nb)]
            for hb in range(2):
                pbk = self.psb[hb]

                def mmV(pbk=pbk, hb=hb, r=r, blocks=blocks, ci=ci):
                    ins = None
                    for sl in range(8):
                        rho, n = blocks[hb * 8 + sl]
                        ins = nc.tensor.transpose(AP(pbk, sl * 128, [[1, 128]]),
                                                  qkv.ap(ci, 2 * S + rho + r * 128 * n, [[r, 128]]), idb)
                    return ins
                P.op("pe", mmV, reads=[qkv.k(ci, 2), ("identb",)], writes=[("psb", hb)])
                P.op("act", (lambda pbk=pbk, hb=hb, ci=ci: nc.scalar.copy(
                    out=vtok.ap(ci, hb * 1024, [[1, 1024]]), in_=AP(pbk, 0, [[1, 1024]]))),
                    reads=[("psb", hb)], writes=[vtok.k(ci, hb)])

            def stageA(bidx, bi, ci=ci, r=r, blocks=blocks):
                rho, n = blocks[bidx]
                nk = 256 if n > 0 else 128
                koff = rho + r * 128 * (n - 1 if n > 0 else 0)
                qap = qkv.ap(ci, rho + r * 128 * n, [[r, 128]])
                kap = qkv.ap(ci, S + koff, [[r, nk]])
                b = sbank[0] % 3
                sbank[0] += 1
                pS = AP(self.ps[b], 0, [[1, nk]])
                P.op("pe", (lambda: nc.tensor.matmul(pS, lhsT=qap, rhs=kap, start=True, stop=True)),
                     reads=[qkv.k(ci, 0), qkv.k(ci, 1)], writes=[("ps", b)])
                ng = st.ap(bi, 1, [[1, 1]])
                mb = mbf.ap(bi, 0, [[1, 1]])
                P.op("dve", (lambda: nc.vector.reduce_max(out=mb, in_=pS, axis=AX.X)),
                     reads=[("ps", b)], writes=[mbf.k(bi)])
                P.op("dve", (lambda: nc.vector.tensor_scalar(out=ng, in0=mb, scalar1=-SCALE, scalar2=None, op0=ALU.mult)),
                     reads=[mbf.k(bi)], writes=[st.k(bi, 1)])
                P.op("dve", (lambda: nc.vector.tensor_scalar(
                    out=dg.ap(bi, 0, [[1, 128]]), in0=idb, scalar1=ng, scalar2=-1.0 / SCALE,
                    op0=ALU.mult, op1=ALU.mult)),
                    reads=[st.k(bi, 1), ("identb",)], writes=[dg.k(bi)])
                P.op("act", (lambda: nc.scalar.activation(
                    out=pbuf.ap(bi, 0, [[1, nk]]), in_=pS, func=AF.Exp, bias=ng, scale=SCALE)),
                    reads=[("ps", b), st.k(bi, 1)], writes=[pbuf.k(bi)])

            def stageB(bidx, bi, blocks=blocks):
                rho, n = blocks[bidx]
                nkb = 2 if n > 0 else 1
                nk = nkb * 128
                pbk = self.psb[bi % 2]

                def mmT():
                    ins = None
                    for kb in range(nkb):
                        ins = nc.tensor.transpose(AP(pbk, kb * 128, [[1, 128]]),
                                                  pbuf.ap(bi, kb * 128, [[1, 128]]), idb)
                    return ins
                P.op("pe", mmT, reads=[pbuf.k(bi), ("identb",)], writes=[("psb", bi % 2)])
                P.op("dve", (lambda: nc.vector.tensor_tensor(
                    out=ptb.ap(bi, 0, [[1, nk]]), in0=AP(pbk, 0, [[1, nk]]),
                    in1=self.maskT.ap(0, 256 - nk, [[1, nk]]), op=ALU.mult)),
                    reads=[("psb", bi % 2), ("maskT",)], writes=[ptb.k(bi)])

            def stageC(bidx, bi, g=g, r=r, blocks=blocks, ci=ci, h=h):
                rho, n = blocks[bidx]
                nkb = 2 if n > 0 else 1
                kblocks = ([bidx - 1, bidx] if n > 0 else [bidx])
                bx = 3 + xbank[0] % 3
                xbank[0] += 1

                def mmO():
                    ins = None
                    for kb in range(nkb):
                        ins = nc.tensor.matmul(AP(self.ps[bx], 0, [[1, 128]]),
                                               lhsT=vtok.ap(ci, kblocks[kb] * 128, [[1, 128]]),
                                               rhs=ptb.ap(bi, kb * 128, [[1, 128]]),
                                               start=(kb == 0), stop=(kb == nkb - 1))
                    ins = nc.tensor.matmul(AP(self.ps[bx], 128, [[1, 128]]), lhsT=ones,
                                           rhs=dg.ap(bi, 0, [[1, 128]]), start=True, stop=True)
                    for kb in range(nkb):
                        ins = nc.tensor.matmul(AP(self.ps[bx], 256, [[1, 128]]), lhsT=ones,
                                               rhs=ptb.ap(bi, kb * 128, [[1, 128]]),
                                               start=(kb == 0), stop=(kb == nkb - 1))
                    return ins
                P.op("pe", mmO, reads=[vtok.k(ci, 0), vtok.k(ci, 1), ptb.k(bi), dg.k(bi), ("onesb",)], writes=[("ps", bx)])
                toff = rho + r * 128 * n
                ws = wins(g, n)
                if g == 2:
                    omap = OM2.ap(h, toff, [[S, 2], [r, 128]])
                    omk = [OM2.k(h, w) for w in ws]
                    dap = Dd2.ap(h, toff, [[r, 128]])
                    dk_ = [Dd2.k(h, w) for w in ws]
                else:
                    omap = OM01.ap(0, g * S + toff, [[2 * S, 2], [r, 128]])
                    omk = [OM01.k(0, g, w) for w in ws]
                    dap = Dd01.ap(0, g * S + toff, [[r, 128]])
                    dk_ = [Dd01.k(0, g, w) for w in ws]
                P.op("act", (lambda: nc.scalar.copy(out=omap, in_=AP(self.ps[bx], 0, [[128, 2], [1, 128]]))),
                     reads=[("ps", bx)], writes=omk)
                P.op("act", (lambda: nc.scalar.copy(out=dap, in_=AP(self.ps[bx], 256, [[1, 128]]))),
                     reads=[("ps", bx)], writes=dk_)

            def merge_ops(h=h):
                W4 = range(4)

                def om(kind, g_, w):
                    if g_ == 2:
                        return OM2.ap(h, kind * S + w * 512, [[1, 512]])
                    return OM01.ap(0, (kind * 2 + g_) * S + w * 512, [[1, 512]])

                def dd(g_, w):
                    if g_ == 2:
                        return Dd2.ap(h, w * 512, [[1, 512]])
                    return Dd01.ap(0, g_ * S + w * 512, [[1, 512]])

                def ok(g_, w):
                    return [OM2.k(h, w)] if g_ == 2 else [OM01.k(0, g_, w)]

                def dk(g_, w):
                    return [Dd2.k(h, w)] if g_ == 2 else [Dd01.k(0, g_, w)]

                def tAa(w):
                    return tA.ap(0, w * 512, [[1, 512]])

                def tEa(w):
                    return tE.ap(0, w * 512, [[1, 512]])

                def na(w):
                    return nacc.ap(0, w * 512, [[1, 512]])

                def da(w):
                    return Dacc.ap(0, w * 512, [[1, 512]])
                ops = []

                def add(ph, eng, fn, reads, writes, dma=False):
                    ops.append((ph, lambda: P.op(eng, fn, reads=reads, writes=writes, dma=dma)))
                for w in W4:
                    add(0, "dve", (lambda w=w: nc.vector.tensor_tensor(out=tAa(w), in0=om(1, 0, w), in1=om(1, 1, w), op=ALU.max)),
                        ok(0, w) + ok(1, w), [tA.k(0, w)])
                for w in W4:
                    add(0, "dve", (lambda w=w: nc.vector.tensor_tensor(out=tAa(w), in0=tAa(w), in1=om(1, 2, w), op=ALU.max)),
                        ok(2, w) + [tA.k(0, w)], [tA.k(0, w)])
                for oi_, g_ in enumerate((1, 0, 2)):
                    for w in W4:
                        add(oi_, "dve", (lambda g_=g_, w=w: nc.vector.tensor_tensor(
                            out=tEa(w), in0=om(1, g_, w), in1=tAa(w), op=ALU.subtract)),
                            ok(g_, w) + [tA.k(0, w)], [tE.k(0, w)])
                    for w in W4:
                        add(oi_, "act", (lambda w=w: nc.scalar.activation(out=tEa(w), in_=tEa(w), func=AF.Exp, scale=SCALE)),
                            [tE.k(0, w)], [tE.k(0, w)])
                    if oi_ == 0:
                        for w in W4:
                            add(oi_, "dve", (lambda g_=g_, w=w: nc.vector.tensor_tensor(
                                out=da(w), in0=dd(g_, w), in1=tEa(w), op=ALU.mult)),
                                dk(g_, w) + [tE.k(0, w)], [Dacc.k(0, w)])
                        for w in W4:
                            add(oi_, "dve", (lambda g_=g_, w=w: nc.vector.tensor_tensor(
                                out=na(w), in0=om(0, g_, w), in1=tEa(w), op=ALU.mult)),
                                ok(g_, w) + [tE.k(0, w)], [nacc.k(0, w)])
                    else:
                        for w in W4:
                            add(oi_, "dve", (lambda g_=g_, w=w: nc.vector.tensor_tensor(
                                out=dd(g_, w), in0=dd(g_, w), in1=tEa(w), op=ALU.mult)),
                                dk(g_, w) + [tE.k(0, w)], dk(g_, w))
                        for w in W4:
                            add(oi_, "dve", (lambda g_=g_, w=w: nc.vector.tensor_tensor(
                                out=tEa(w), in0=om(0, g_, w), in1=tEa(w), op=ALU.mult)),
                                ok(g_, w) + [tE.k(0, w)], [tE.k(0, w)])
                        for w in W4:
                            add(oi_, "pool", (lambda g_=g_, w=w: nc.gpsimd.tensor_tensor(
                                out=da(w), in0=da(w), in1=dd(g_, w), op=ALU.add)),
                                dk(g_, w) + [Dacc.k(0, w)], [Dacc.k(0, w)])
                        for w in W4:
                            add(oi_, "pool", (lambda w=w: nc.gpsimd.tensor_tensor(out=na(w), in0=na(w), in1=tEa(w), op=ALU.add)),
                                [nacc.k(0, w), tE.k(0, w)], [nacc.k(0, w)])
                for w in W4:
                    add(2, "act", (lambda w=w: nc.scalar.activation(out=da(w), in_=da(w), func=AF.Ln)),
                        [Dacc.k(0, w)], [Dacc.k(0, w)])
                for w in W4:
                    add(2, "act", (lambda w=w: nc.scalar.activation(out=da(w), in_=da(w), func=AF.Exp, scale=-1.0)),
                        [Dacc.k(0, w)], [Dacc.k(0, w)])
                for w in W4:
                    add(2, "dve", (lambda w=w: nc.vector.tensor_tensor(
                        out=aS.ap(h, w * 512, [[1, 512]]), in0=na(w), in1=da(w), op=ALU.mult)),
                        [Dacc.k(0, w), nacc.k(0, w)], [aS.k(h, w)])
                dst = bass.AP(self.scr_a, h * 128 * S, [[S, 128], [1, S]])
                add(2, "pool", (lambda: nc.gpsimd.dma_start(out=dst, in_=aS.ap(h, 0, [[1, S]]))),
                    [aS.k(h, w) for w in W4], [("scra", h)], dma=True)
                return ops
            nblk = len(blocks)
            phase_i = 2 - g
            mine = [e for (ph, e) in pending if ph == phase_i]
            per_it = -(-len(mine) // nblk)
            mi_ = 0
            LB, LC = 2, 4
            for it in range(nblk + LC):
                if it < nblk:
                    stageA(it, bi + it)
                if 0 <= it - LB < nblk:
                    stageB(it - LB, bi + it - LB)
                if 0 <= it - LC < nblk:
                    stageC(it - LC, bi + it - LC)
                if it >= 1:
                    for _ in range(per_it):
                        if mi_ < len(mine):
                            mine[mi_]()
                            mi_ += 1
            while mi_ < len(mine):
                mine[mi_]()
                mi_ += 1
            bi += nblk
            if g == 0:
                pending = merge_ops()
        for (ph, e) in pending:
            e()
        self.aoff = mark
        wo = Buf(self, "awo", NCH * D, BF16)
        at = Buf(self, "aat", NHEAD * 512, BF16, 2)
        yt = Buf(self, "ayt", NCH * 512, F32)
        sq = Buf(self, "asq", 512, BF16, 2)
        rstd = Buf(self, "arstd", 512, F32, 2)
        tmp = Buf(self, "atmp2", 512, F32, 2)
        for q2 in range(2):
            self.wload(wo.ap(0, q2 * 4096, [[1, 4096]]), wo.k(0, q2), "wo", lj, q2 * 4096, 4096,
                       [("wb", "wo", lj, qq) for qq in range(2)])

        def load_at(tile):
            src = bass.AP(self.scr_a, tile * 512, [[S, 128], [128 * S, NHEAD], [1, 512]])
            P.op("sp", (lambda: nc.sync.dma_start(out=at.ap(tile, 0, [[512, NHEAD], [1, 512]]), in_=src)),
                 reads=[("scra", hh) for hh in range(NHEAD)], writes=[at.k(tile)], dma=True)
        load_at(0)
        for tile in range(4):
            t0 = tile * 512
            if tile + 1 < 4:
                load_at(tile + 1)
            def mmf(m, b, tile=tile):
                def mm():
                    ins = None
                    for kc in range(NCH):
                        ins = nc.tensor.matmul(AP(self.ps[b], 0, [[1, 512]]),
                                               lhsT=wo.ap(0, kc * D + m * 128, [[1, 128]]),
                                               rhs=at.ap(tile, kc * 512, [[1, 512]]),
                                               start=(kc == 0), stop=(kc == NCH - 1))
                    return ins
                return mm, [wo.k(0, 0), wo.k(0, 1), at.k(tile)]

            def evf(m, b):
                return (lambda: nc.scalar.copy(out=yt.ap(0, m * 512, [[1, 512]]),
                                               in_=AP(self.ps[b], 0, [[1, 512]]))), []
            self.outproj_postnorm(mmf, evf, PVC[("mix_post_g", li)], t0, yt, sq, rstd, tmp)

    def kv_project(self):
        outs = [("k", gh) for gh in range(24)] + [("v", gh) for gh in range(24)]
        self.project(PVC[("kv_norm_g", 0)], "wkv", 0, outs)

    def build(self):
        self.consts()
        self.precast_all()
        stop = self.stop_after
        for s_ in range(self.nseq):
            self.phase()
            self.load_x(s_)
            done = False
            for li in range(4):
                if li < 2:
                    self.conformer(li)
                else:
                    self.attention(li - 2, li)
                if stop == ("mix", li):
                    break
                self.ffn(li)
                if stop == ("ffn", li):
                    break
                if li == 1:
                    self.kv_project()
            self.phase()
            self.store_x(s_)
        n = self.P.emit()
        return self.nc, n


WNAMES = ("cm_w_in", "cm_w_out", "w_kv", "w_q", "w_o", "ffn_w_in", "ffn_w_out")


def kernel(**inputs):
    x = np.ascontiguousarray(np.asarray(inputs["x"], dtype=np.float32))
    kb = K()
    nc, _ = kb.build()
    pv = pack_pv(inputs)
    ws = {nm: np.ascontiguousarray(np.asarray(inputs[nm], dtype=np.float32)) for nm in WNAMES}
    in_maps = []
    for c in range(N_CORES):
        m = {"x": x[c * SEQ_PER_CORE:(c + 1) * SEQ_PER_CORE], "pv": pv}
        m.update(ws)
        in_maps.append(m)
    res = run_bass_kernel_spmd(nc, in_maps, core_ids=list(range(N_CORES)))
    return np.concatenate([r["out"] for r in res.results], axis=0)
```

```python
import numpy as np
import concourse.bass as bass
import concourse.mybir as mybir
from concourse.bass_utils import run_bass_kernel_spmd

F32, BF16 = mybir.dt.float32, mybir.dt.bfloat16
AF = mybir.ActivationFunctionType
ALU = mybir.AluOpType
AX = mybir.AxisListType

D = 1024
S = 2048
NCH = 8
DFF = 2816
NFF = 22
NHEAD = 8
NG = 3
DIL = (1, 4, 16)
EPS = 1e-6
N_CORES = 8
SEQ_PER_CORE = 2


class Prog:
    def __init__(self, nc):
        self.nc = nc
        self.ops = []
        self.alias = {}

    def op(self, eng, fn, reads=(), writes=(), dma=False):
        reads, writes = tuple(reads), tuple(writes)
        extra = tuple(k for k in reads if k[0] in ("ps", "psb") and k not in writes)
        self.ops.append((eng, fn, reads, writes + extra, dma))

    def barrier(self):
        self.ops.append(("barrier", None, (), (), False))

    def emit(self):
        nc = self.nc
        ops = self.ops
        engobj = {"pe": nc.tensor, "act": nc.scalar, "dve": nc.vector,
                  "pool": nc.gpsimd, "sp": nc.sync}
        n = len(ops)
        deps = [None] * n
        needed = [False] * n
        last_w, readers = {}, {}
        last_on = {}
        dmas_since = []
        pend = {e: set() for e in engobj}
        buf_keys, alias_deps = {}, {}
        for i, (eng, fn, R, W, dma) in enumerate(ops):
            if eng == "barrier":
                allp = set(last_on.values()) | set(dmas_since)
                for e in engobj:
                    pend[e] |= allp
                last_w, readers, dmas_since = {}, {}, []
                deps[i] = set()
                continue
            d = set(pend[eng])
            pend[eng] = set()
            for k in R + W:
                bn = k[0]
                buf_keys.setdefault(bn, set()).add(k)
                if bn in self.alias:
                    if bn not in alias_deps:
                        sset = set()
                        for ob in self.alias[bn]:
                            for k2 in buf_keys.get(ob, ()):
                                if k2 in last_w:
                                    sset.add(last_w[k2])
                                sset.update(readers.get(k2, ()))
                        best = {}
                        red = set()
                        for j in sset:
                            if ops[j][4]:
                                red.add(j)
                            else:
                                e_ = ops[j][0]
                                if e_ not in best or best[e_] < j:
                                    best[e_] = j
                        red.update(best.values())
                        alias_deps[bn] = red
                    d |= alias_deps[bn]
            for k in R:
                if k in last_w:
                    d.add(last_w[k])
            for k in W:
                if k in last_w:
                    d.add(last_w[k])
                for r in readers.get(k, ()):
                    d.add(r)
            d2 = set()
            for j in d:
                ej, _, _, _, dj = ops[j]
                if ej == "pe" and eng == "pe" and not dj and not dma:
                    continue
                d2.add(j)
            deps[i] = d2
            for j in d2:
                needed[j] = True
            for k in R:
                readers.setdefault(k, []).append(i)
            for k in W:
                last_w[k] = i
                readers[k] = []
            if dma:
                dmas_since.append(i)
            else:
                last_on[eng] = i
        final = set(last_on.values()) | set(dmas_since)
        for e in engobj:
            final |= pend[e]
        for j in final:
            needed[j] = True
        ROLL = 30000
        sems = {}

        def getsem(name):
            if name not in sems:
                sems[name] = nc.semaphore(name).__enter__()
            return sems[name]

        NPOOL = 24
        cnt = {e: 0 for e in engobj}
        dma_pools = {e: [0, [0] * NPOOL] for e in engobj}
        tok = [None] * n
        known = {e: {} for e in engobj}

        def wait(eng, t):
            name, val = t
            if known[eng].get(name, 0) >= val:
                return
            engobj[eng].wait_ge(getsem(name), val)
            known[eng][name] = val

        for i, (eng, fn, R, W, dma) in enumerate(ops):
            if eng == "barrier":
                continue
            for j in sorted(deps[i]):
                wait(eng, tok[j])
            if dma:
                pool_ = dma_pools[eng]
                k = pool_[0] % NPOOL
                pool_[0] += 1
                name = "dq%s%d" % (eng, k)
                tot = pool_[1]
                if tot[k] > 0:
                    wait(eng, (name, tot[k]))
                tot[k] += 16
                ins = fn()
                ins.then_inc(getsem(name), 16)
                tok[i] = (name, tot[k])
            else:
                ins = fn()
                if needed[i]:
                    cnt[eng] += 1
                    name = "%s%d" % (eng, cnt[eng] // ROLL)
                    val = cnt[eng] % ROLL
                    if val == 0:
                        val = ROLL
                        name = "%s%d" % (eng, cnt[eng] // ROLL - 1)
                    ins.then_inc(getsem(name), 1)
                    tok[i] = (name, val)
        for j in sorted(final):
            if tok[j] is not None:
                wait("sp", tok[j])
        return n


def AP(t, off, dims, F=None, parts=128):
    if F is None:
        F = 1
        for s_ in t.shape[1:]:
            F *= s_
    return bass.AP(t, off, [[F, parts]] + [list(x) for x in dims])


SCALE = 1.0 / float(np.sqrt(128.0))
NEG = -30000.0
ARENA_LO = 16640
ARENA_HI = 229376


def pv_layout():
    cols = {}
    n = [0]

    def add(name, w):
        cols[name] = n[0]
        n[0] += w
    for i in range(4):
        for nm in ("mix_pre_g", "mix_post_g", "ffn_pre_g", "ffn_post_g"):
            add((nm, i), 8)
    add(("kv_norm_g", 0), 8)
    for i in range(2):
        add(("cm_b_in", i), 16)
        add(("cm_dw", i), 248)
        add(("cm_dw_b", i), 8)
        add(("cm_ln_g", i), 8)
        add(("cm_ln_b", i), 8)
        add(("cm_b_out", i), 8)
    for i in range(4):
        add(("ffn_dw", i), 132)
        add(("ffn_dw_b", i), 44)
    return cols, n[0]


PVC, NPV = pv_layout()


def pack_pv(inp):
    pv = np.zeros((128, NPV), np.float32)

    def vec(v):
        v = np.asarray(v, np.float32)
        return v.reshape(-1, 128).T
    for i in range(4):
        for nm in ("mix_pre_g", "mix_post_g", "ffn_pre_g", "ffn_post_g"):
            c = PVC[(nm, i)]
            pv[:, c:c + 8] = vec(inp[nm][i])
    c = PVC[("kv_norm_g", 0)]
    pv[:, c:c + 8] = vec(inp["kv_norm_g"])
    for i in range(2):
        c = PVC[("cm_b_in", i)]
        pv[:, c:c + 16] = vec(inp["cm_b_in"][i])
        c = PVC[("cm_dw", i)]
        dw = np.asarray(inp["cm_dw"][i], np.float32).reshape(31, 8, 128)
        pv[:, c:c + 248] = dw.transpose(2, 1, 0).reshape(128, 248)
        for nm in ("cm_dw_b", "cm_ln_g", "cm_ln_b", "cm_b_out"):
            c = PVC[(nm, i)]
            pv[:, c:c + 8] = vec(inp[nm][i])
    for i in range(4):
        c = PVC[("ffn_dw", i)]
        dw = np.asarray(inp["ffn_dw"][i], np.float32).reshape(3, 44, 128)
        pv[:, c:c + 132] = dw.transpose(2, 1, 0).reshape(128, 132)
        c = PVC[("ffn_dw_b", i)]
        pv[:, c:c + 44] = vec(inp["ffn_dw_b"][i])
    return pv


class Buf:
    def __init__(self, kb, name, F, dtype, nbuf=1):
        self.F = F
        self.key = kb.name(name)
        self.n = nbuf
        self.t = []
        esz = 4 if dtype == F32 else 2
        lo = (kb.aoff + 63) // 64 * 64
        for i in range(nbuf):
            off = (kb.aoff + 63) // 64 * 64
            assert off + F * esz <= ARENA_HI, ("SBUF overflow", name, off, F * esz)
            self.t.append(kb.nc.alloc_sbuf_tensor_at("%s_%d" % (self.key, i), [128, F], dtype, offset=off))
            kb.aoff = off + F * esz
        hi = kb.aoff
        al = [nm for (nm, l2, h2) in kb.all_bufs if l2 < hi and lo < h2]
        if al:
            kb.P.alias[self.key] = al
        kb.all_bufs.append((self.key, lo, hi))

    def ap(self, i, off, dims, parts=128):
        return AP(self.t[i % self.n], off, dims, F=self.F, parts=parts)

    def k(self, i, *sub):
        return (self.key, i % self.n) + tuple(sub)


class K:
    def __init__(self, nseq=SEQ_PER_CORE, stop_after=None):
        self.nseq = nseq
        self.stop_after = stop_after
        nc = bass.Bass("TRN2", target_bir_lowering=False)
        self.nc = nc
        self.P = Prog(nc)
        self.uid = 0
        self.all_bufs = []
        dt = nc.dram_tensor
        self.x_in = dt("x", [nseq, S, D], F32, kind="ExternalInput")
        self.out = dt("out", [nseq, S, D], F32, kind="ExternalOutput")
        self.pv_d = dt("pv", [128, NPV], F32, kind="ExternalInput")
        self.w = {
            "cm_w_in": dt("cm_w_in", [2, D, 2 * D], F32, kind="ExternalInput"),
            "cm_w_out": dt("cm_w_out", [2, D, D], F32, kind="ExternalInput"),
            "w_kv": dt("w_kv", [D, 6144], F32, kind="ExternalInput"),
            "w_q": dt("w_q", [2, D, 3072], F32, kind="ExternalInput"),
            "w_o": dt("w_o", [2, D, D], F32, kind="ExternalInput"),
            "ffn_w_in": dt("ffn_w_in", [4, D, 2 * DFF], F32, kind="ExternalInput"),
            "ffn_w_out": dt("ffn_w_out", [4, DFF, D], F32, kind="ExternalInput"),
        }
        self.scr = {nm: dt("scr_" + nm, [24, 128, S], BF16, kind="Internal") for nm in ("q", "k", "v")}
        self.scr_a = dt("scr_a", [NHEAD, 128, S], BF16, kind="Internal")
        self.wb = {
            "cwin": dt("wb_cwin", [2, 128, NCH * 2 * D], BF16, kind="Internal"),
            "cwout": dt("wb_cwout", [2, 128, NCH * D], BF16, kind="Internal"),
            "wkv": dt("wb_wkv", [48, 128, NCH * 128], BF16, kind="Internal"),
            "wq": dt("wb_wq", [2 * 24, 128, NCH * 128], BF16, kind="Internal"),
            "wo": dt("wb_wo", [2, 128, NCH * D], BF16, kind="Internal"),
            "fwin": dt("wb_fwin", [4 * NFF, 128, NCH * 256], BF16, kind="Internal"),
            "fwout": dt("wb_fwout", [4 * NCH, 128, NFF * 128], BF16, kind="Internal"),
        }
        self.aoff = ARENA_LO
        self.xT = Buf(self, "xT", NCH * S, F32)
        self.identf = Buf(self, "identf", 128, F32)
        self.identb = Buf(self, "identb", 128, BF16)
        self.onesb = Buf(self, "onesb", 128, BF16)
        self.meanb = Buf(self, "meanb", 128, BF16)
        self.maskT = Buf(self, "maskT", 256, BF16)
        self.pv = Buf(self, "pvs", NPV, F32)
        self.halo = Buf(self, "halo", 44 * 2, BF16)
        self.epsb = Buf(self, "epsb", 1, F32)
        self.arena_base = self.aoff
        self.ps = [nc.alloc_psum_tensor("ps%d" % b, [128, 512], F32) for b in range(6)]
        self.psb = [nc.alloc_psum_tensor("psb%d" % b, [128, 1024], BF16) for b in range(2)]
        self.psn = 0
        self.psbn = 0
        self.held = set()

    def name(self, s_):
        self.uid += 1
        return "%s_%d" % (s_, self.uid)

    def bank(self):
        while True:
            b = self.psn % 6
            self.psn += 1
            if b not in self.held:
                return b

    def epsap(self):
        return self.epsb.ap(0, 0, [[1, 1]])

    def pvap(self, col, n=1):
        return self.pv.ap(0, col, [[1, n]])

    def phase(self):
        self.aoff = self.arena_base

    def consts(self):
        nc, P = self.nc, self.P
        idf = self.identf.ap(0, 0, [[1, 128]])
        P.op("pool", lambda: nc.gpsimd.memset(idf, 0.0), writes=[("identf",)])
        P.op("pool", lambda: nc.gpsimd.affine_select(
            out=idf, in_=idf, pattern=[[-1, 128]], compare_op=ALU.not_equal, fill=1.0,
            base=0, channel_multiplier=1), reads=[("identf",)], writes=[("identf",)])
        P.op("dve", lambda: nc.vector.tensor_copy(out=self.identb.ap(0, 0, [[1, 128]]), in_=idf),
             reads=[("identf",)], writes=[("identb",)])
        P.op("dve", lambda: nc.vector.memset(self.onesb.ap(0, 0, [[1, 128]]), 1.0), writes=[("onesb",)])
        P.op("dve", lambda: nc.vector.memset(self.meanb.ap(0, 0, [[1, 128]]), 1.0 / D), writes=[("meanb",)])
        P.op("dve", lambda: nc.vector.memset(self.epsap(), EPS), writes=[("eps",)])
        mk = self.maskT
        P.op("pool", lambda: nc.gpsimd.memset(mk.ap(0, 0, [[1, 256]]), 1.0), writes=[("maskT",)])
        P.op("pool", lambda: nc.gpsimd.affine_select(
            out=mk.ap(0, 0, [[1, 128]]), in_=mk.ap(0, 0, [[1, 128]]), pattern=[[-1, 128]],
            compare_op=ALU.is_ge, fill=0.0, base=0, channel_multiplier=1),
            reads=[("maskT",)], writes=[("maskT",)])
        P.op("pool", lambda: nc.gpsimd.affine_select(
            out=mk.ap(0, 128, [[1, 128]]), in_=mk.ap(0, 128, [[1, 128]]), pattern=[[1, 128]],
            compare_op=ALU.is_ge, fill=0.0, base=0, channel_multiplier=-1),
            reads=[("maskT",)], writes=[("maskT",)])
        P.op("sp", lambda: nc.sync.dma_start(out=self.pv.ap(0, 0, [[1, NPV]]),
                                             in_=bass.AP(self.pv_d, 0, [[NPV, 128], [1, NPV]])),
             writes=[("pv",)], dma=True)

    CK = [("identf",), ("identb",), ("onesb",), ("meanb",), ("maskadd",), ("pv",)]

    def xk(self, c, tile):
        return ("xT", c, tile)

    def xap(self, c, t0, n=512):
        return self.xT.ap(0, c * S + t0, [[1, n]])

    def load_x(self, s_):
        nc, P = self.nc, self.P
        stg = Buf(self, "xstg", D, F32, 2)
        idf = self.identf.ap(0, 0, [[1, 128]])
        for tt in range(S // 128):
            src = bass.AP(self.x_in, (s_ * S + tt * 128) * D, [[D, 128], [1, D]])
            P.op("sp", (lambda tt=tt, src=src: nc.sync.dma_start(out=stg.ap(tt, 0, [[1, D]]), in_=src)),
                 writes=[stg.k(tt)], dma=True)
            for c0 in range(0, NCH, 4):
                b = self.bank()
                pst = self.ps[b]

                def mm(tt=tt, pst=pst, c0=c0):
                    ins = None
                    for cc in range(4):
                        ins = nc.tensor.transpose(AP(pst, cc * 128, [[1, 128]]),
                                                  stg.ap(tt, (c0 + cc) * 128, [[1, 128]]), idf)
                    return ins
                P.op("pe", mm, reads=[stg.k(tt), ("identf",)], writes=[("ps", b)])
                dst = self.xT.ap(0, c0 * S + tt * 128, [[S, 4], [1, 128]])
                srcp = AP(pst, 0, [[128, 4], [1, 128]])
                P.op("act", (lambda dst=dst, srcp=srcp: nc.scalar.copy(out=dst, in_=srcp)),
                     reads=[("ps", b)], writes=[self.xk(c0 + cc, tt // 4) for cc in range(4)])

    def store_x(self, s_):
        nc, P = self.nc, self.P
        stg = Buf(self, "ostg", D, F32, 2)
        idf = self.identf.ap(0, 0, [[1, 128]])
        for tt in range(S // 128):
            for c0 in range(0, NCH, 4):
                b = self.bank()
                pst = self.ps[b]

                def mm(pst=pst, c0=c0, tt=tt):
                    ins = None
                    for cc in range(4):
                        ins = nc.tensor.transpose(AP(pst, cc * 128, [[1, 128]]),
                                                  self.xT.ap(0, (c0 + cc) * S + tt * 128, [[1, 128]]), idf)
                    return ins
                P.op("pe", mm, reads=[self.xk(c0 + cc, tt // 4) for cc in range(4)] + [("identf",)],
                     writes=[("ps", b)])
                dst = stg.ap(tt, c0 * 128, [[1, 512]])
                srcp = AP(pst, 0, [[1, 512]])
                P.op("act", (lambda dst=dst, srcp=srcp: nc.scalar.copy(out=dst, in_=srcp)),
                     reads=[("ps", b)], writes=[stg.k(tt)])
            dstd = bass.AP(self.out, (s_ * S + tt * 128) * D, [[D, 128], [1, D]])
            P.op("pool", (lambda tt=tt, dstd=dstd: nc.gpsimd.dma_start(out=dstd, in_=stg.ap(tt, 0, [[1, D]]))),
                 reads=[stg.k(tt)], writes=[("outd", s_, tt)], dma=True)

    def cast(self, dname, didx, ddims, wname, wbase, sdims):
        nc = self.nc
        dh = self.wb[dname]
        per = 1
        for s_ in dh.shape[1:]:
            per *= s_
        dst = bass.AP(dh, didx * per, [list(x) for x in ddims])
        src = bass.AP(self.w[wname], wbase, [list(x) for x in sdims])
        self.P.op("pool", (lambda: nc.gpsimd.dma_start(out=dst, in_=src)), writes=[("wb", dname, didx)], dma=True)

    def precast_conf(self, li):
        for q4 in range(4):
            pass
        nc = self.nc
        dh = self.wb["cwin"]
        for q4 in range(4):
            dst = bass.AP(dh, li * 128 * NCH * 2 * D + q4 * 512, [[NCH * 2 * D, 128], [2 * D, NCH], [1, 512]])
            src = bass.AP(self.w["cm_w_in"], li * D * 2 * D + q4 * 512, [[2 * D, 128], [128 * 2 * D, NCH], [1, 512]])
            self.P.op("pool", (lambda dst=dst, src=src: nc.gpsimd.dma_start(out=dst, in_=src)),
                      writes=[("wb", "cwin", li, q4)], dma=True)
        dh = self.wb["cwout"]
        for q2 in range(2):
            dst = bass.AP(dh, li * 128 * NCH * D + q2 * 512, [[NCH * D, 128], [D, NCH], [1, 512]])
            src = bass.AP(self.w["cm_w_out"], li * D * D + q2 * 512, [[D, 128], [128 * D, NCH], [1, 512]])
            self.P.op("pool", (lambda dst=dst, src=src: nc.gpsimd.dma_start(out=dst, in_=src)),
                      writes=[("wb", "cwout", li, q2)], dma=True)

    def precast_ffn(self, li):
        nc = self.nc
        dh = self.wb["fwin"]
        for j in range(NFF):
            for half in range(2):
                dst = bass.AP(dh, (li * NFF + j) * 128 * NCH * 256 + half * 128,
                              [[NCH * 256, 128], [256, NCH], [1, 128]])
                src = bass.AP(self.w["ffn_w_in"], li * D * 2 * DFF + half * DFF + j * 128,
                              [[2 * DFF, 128], [128 * 2 * DFF, NCH], [1, 128]])
                self.P.op("pool", (lambda dst=dst, src=src: nc.gpsimd.dma_start(out=dst, in_=src)),
                          writes=[("wb", "fwin", li, j, half)], dma=True)
        dh = self.wb["fwout"]
        for m in range(NCH):
            dst = bass.AP(dh, (li * NCH + m) * 128 * NFF * 128, [[NFF * 128, 128], [128, NFF], [1, 128]])
            src = bass.AP(self.w["ffn_w_out"], li * DFF * D + m * 128, [[D, 128], [128 * D, NFF], [1, 128]])
            self.P.op("pool", (lambda dst=dst, src=src: nc.gpsimd.dma_start(out=dst, in_=src)),
                      writes=[("wb", "fwout", li, m)], dma=True)

    def precast_proj(self, dname, base_idx, n, wname, wbase, rowstride, col0s):
        nc = self.nc
        dh = self.wb[dname]
        for i in range(n):
            dst = bass.AP(dh, (base_idx + i) * 128 * NCH * 128, [[NCH * 128, 128], [128, NCH], [1, 128]])
            src = bass.AP(self.w[wname], wbase + col0s[i], [[rowstride, 128], [128 * rowstride, NCH], [1, 128]])
            self.P.op("pool", (lambda dst=dst, src=src: nc.gpsimd.dma_start(out=dst, in_=src)),
                      writes=[("wb", dname, base_idx + i)], dma=True)

    def precast_wo(self, lj):
        nc = self.nc
        dh = self.wb["wo"]
        for q2 in range(2):
            dst = bass.AP(dh, lj * 128 * NCH * D + q2 * 512, [[NCH * D, 128], [D, NCH], [1, 512]])
            src = bass.AP(self.w["w_o"], lj * D * D + q2 * 512, [[D, 128], [128 * D, NCH], [1, 512]])
            self.P.op("pool", (lambda dst=dst, src=src: nc.gpsimd.dma_start(out=dst, in_=src)),
                      writes=[("wb", "wo", lj, q2)], dma=True)

    def precast_all(self):
        kvcols = [gh * 128 for gh in range(24)] + [3072 + gh * 128 for gh in range(24)]
        self.precast_conf(0)
        self.precast_ffn(0)
        self.precast_conf(1)
        self.precast_ffn(1)
        self.precast_proj("wkv", 0, 48, "w_kv", 0, 6144, kvcols)
        for lj in range(2):
            self.precast_proj("wq", lj * 24, 24, "w_q", lj * D * 3072, 3072, [gh * 128 for gh in range(24)])
            self.precast_wo(lj)
            self.precast_ffn(2 + lj)

    def wload(self, dst_ap, key, dname, didx, off, n, rkeys):
        nc = self.nc
        dh = self.wb[dname]
        per = 1
        for s_ in dh.shape[1:]:
            per *= s_
        rowlen = per // 128
        src = bass.AP(dh, didx * per + off, [[rowlen, 128], [1, n]])
        self.P.op("sp", (lambda: nc.sync.dma_start(out=dst_ap, in_=src)), reads=rkeys, writes=[key], dma=True)

    def rms_rstd(self, src, srck, rstd_ap, rstd_key, sq, n=512):
        nc, P = self.nc, self.P
        b = self.bank()
        pst = AP(self.ps[b], 0, [[1, n]])
        for c in range(NCH):
            i = self.uid
            self.uid += 1
            sa = sq.ap(i, 0, [[1, n]])
            P.op("act", (lambda sa=sa, c=c: nc.scalar.activation(out=sa, in_=src(c), func=AF.Square)),
                 reads=[srck(c)], writes=[sq.k(i)])
            P.op("pe", (lambda sa=sa, c=c: nc.tensor.matmul(pst, lhsT=self.meanb.ap(0, 0, [[1, 128]]), rhs=sa,
                                                            start=(c == 0), stop=(c == NCH - 1))),
                 reads=[sq.k(i), ("meanb",)], writes=[("ps", b)])
        P.op("act", (lambda: nc.scalar.activation(out=rstd_ap, in_=pst, func=AF.Sqrt, bias=self.epsap(), scale=1.0)),
             reads=[("ps", b), ("eps",)], writes=[rstd_key])
        P.op("dve", (lambda: nc.vector.reciprocal(out=rstd_ap, in_=rstd_ap)),
             reads=[rstd_key], writes=[rstd_key])

    def pre_norm(self, gcol, t0, hT, hi, sq, rstd, n=512, hoff=0, hstride=None, slot=0):
        nc, P = self.nc, self.P
        if hstride is None:
            hstride = n
        tile = t0 // 512
        ri = self.uid
        self.uid += 1
        self.rms_rstd(lambda c: self.xap(c, t0, n), lambda c: self.xk(c, tile),
                      rstd.ap(ri, 0, [[1, n]]), rstd.k(ri), sq, n)
        for c in range(NCH):
            P.op("dve", (lambda c=c: nc.vector.scalar_tensor_tensor(
                out=hT.ap(hi, hoff + c * hstride, [[1, n]]), in0=self.xap(c, t0, n), scalar=self.pvap(gcol + c),
                in1=rstd.ap(ri, 0, [[1, n]]), op0=ALU.mult, op1=ALU.mult)),
                reads=[self.xk(c, tile), ("pv",), rstd.k(ri)], writes=[hT.k(hi, c, slot)])

    def post_norm_residual(self, gcol, t0, yt, sq, rstd, tmp, n=512):
        nc, P = self.nc, self.P
        tile = t0 // 512
        ri = self.uid
        self.uid += 1
        self.rms_rstd(lambda c: yt.ap(0, c * n, [[1, n]]), lambda c: yt.k(0, c),
                      rstd.ap(ri, 0, [[1, n]]), rstd.k(ri), sq, n)
        for c in range(NCH):
            ti = self.uid
            self.uid += 1
            P.op("dve", (lambda c=c, ti=ti: nc.vector.scalar_tensor_tensor(
                out=tmp.ap(ti, 0, [[1, n]]), in0=yt.ap(0, c * n, [[1, n]]), scalar=self.pvap(gcol + c),
                in1=rstd.ap(ri, 0, [[1, n]]), op0=ALU.mult, op1=ALU.mult)),
                reads=[yt.k(0, c), ("pv",), rstd.k(ri)], writes=[tmp.k(ti)])
            P.op("dve", (lambda c=c, ti=ti: nc.vector.tensor_tensor(
                out=self.xap(c, t0, n), in0=self.xap(c, t0, n), in1=tmp.ap(ti, 0, [[1, n]]), op=ALU.add)),
                reads=[tmp.k(ti), self.xk(c, tile)], writes=[self.xk(c, tile)])

    def outproj_postnorm(self, mmf, evf, gcol, tcol, yt, sq, rstd, tmp, add_eng="pool"):
        nc, P = self.nc, self.P
        bS = self.bank()
        self.held = {bS}
        pst = AP(self.ps[bS], 0, [[1, 512]])
        ri = self.uid
        self.uid += 1
        sqs = []

        def stat(m):
            sa, sk = sqs[m]
            P.op("pe", (lambda: nc.tensor.matmul(pst, lhsT=self.meanb.ap(0, 0, [[1, 128]]), rhs=sa,
                                                 start=(m == 0), stop=(m == NCH - 1))),
                 reads=[sk, ("meanb",)], writes=[("ps", bS)])
        for m in range(NCH):
            b = self.bank()
            fn, reads = mmf(m, b)
            P.op("pe", fn, reads=reads, writes=[("ps", b)])
            if m > 0:
                stat(m - 1)
            efn, ereads = evf(m, b)
            P.op("act", efn, reads=[("ps", b)] + ereads, writes=[yt.k(0, m)])
            qi = self.uid
            self.uid += 1
            sa = sq.ap(qi, 0, [[1, 512]])
            sqs.append((sa, sq.k(qi)))
            P.op("act", (lambda m=m, sa=sa: nc.scalar.activation(
                out=sa, in_=yt.ap(0, m * 512, [[1, 512]]), func=AF.Square)),
                reads=[yt.k(0, m)], writes=[sq.k(qi)])
        stat(NCH - 1)
        self.held = set()
        ra = rstd.ap(ri, 0, [[1, 512]])
        P.op("act", (lambda: nc.scalar.activation(out=ra, in_=pst, func=AF.Sqrt, bias=self.epsap(), scale=1.0)),
             reads=[("ps", bS), ("eps",)], writes=[rstd.k(ri)])
        P.op("dve", (lambda: nc.vector.reciprocal(out=ra, in_=ra)), reads=[rstd.k(ri)], writes=[rstd.k(ri)])
        tile = tcol // 512
        for c in range(NCH):
            ti = self.uid
            self.uid += 1
            P.op("dve", (lambda c=c, ti=ti: nc.vector.scalar_tensor_tensor(
                out=tmp.ap(ti, 0, [[1, 512]]), in0=yt.ap(0, c * 512, [[1, 512]]), scalar=self.pvap(gcol + c),
                in1=ra, op0=ALU.mult, op1=ALU.mult)),
                reads=[yt.k(0, c), ("pv",), rstd.k(ri)], writes=[tmp.k(ti)])
            eobj = nc.gpsimd if add_eng == "pool" else nc.vector
            P.op(add_eng, (lambda c=c, ti=ti, eobj=eobj: eobj.tensor_tensor(
                out=self.xap(c, tcol), in0=self.xap(c, tcol), in1=tmp.ap(ti, 0, [[1, 512]]), op=ALU.add)),
                reads=[tmp.k(ti), self.xk(c, tile)], writes=[self.xk(c, tile)])

    def conformer(self, li):
        nc, P = self.nc, self.P
        self.phase()
        GL = 30 + S
        glu = Buf(self, "glu", NCH * GL, BF16)
        mark = self.aoff
        win = Buf(self, "cwin", NCH * 2 * D, BF16)
        hT = Buf(self, "chT", NCH * 512, BF16, 2)
        sq = Buf(self, "csq", 512, BF16, 2)
        rstd = Buf(self, "crstd", 512, F32, 2)
        sig = Buf(self, "csig", 512, F32, 2)
        for q4 in range(4):
            self.wload(win.ap(0, q4 * 4096, [[1, 4096]]), win.k(0, q4), "cwin", li, q4 * 4096, 4096,
                       [("wb", "cwin", li, qq) for qq in range(4)])
        for c in range(NCH):
            P.op("dve", (lambda c=c: nc.vector.memset(glu.ap(0, c * GL, [[1, 30]]), 0.0)), writes=[glu.k(0, c, "z")])
        bcol = PVC[("cm_b_in", li)]
        for tile in range(4):
            t0 = tile * 512
            self.pre_norm(PVC[("mix_pre_g", li)], t0, hT, tile, sq, rstd)
            for m in range(NCH):
                ba, bg = self.bank(), self.bank()
                for (b, oc) in ((ba, m), (bg, m + 8)):
                    def mm(b=b, oc=oc, tile=tile):
                        ins = None
                        for kc in range(NCH):
                            ins = nc.tensor.matmul(AP(self.ps[b], 0, [[1, 512]]),
                                                   lhsT=win.ap(0, kc * 2 * D + oc * 128, [[1, 128]]),
                                                   rhs=hT.ap(tile, kc * 512, [[1, 512]]),
                                                   start=(kc == 0), stop=(kc == NCH - 1))
                        return ins
                    P.op("pe", mm, reads=[win.k(0, qq) for qq in range(4)] + [hT.k(tile, kc, 0) for kc in range(NCH)],
                         writes=[("ps", b)])
                si = self.uid
                self.uid += 1
                P.op("act", (lambda bg=bg, si=si, m=m: nc.scalar.activation(
                    out=sig.ap(si, 0, [[1, 512]]), in_=AP(self.ps[bg], 0, [[1, 512]]), func=AF.Sigmoid,
                    bias=self.pvap(bcol + 8 + m), scale=1.0)),
                    reads=[("ps", bg), ("pv",)], writes=[sig.k(si)])
                P.op("dve", (lambda ba=ba, si=si, m=m, t0=t0: nc.vector.scalar_tensor_tensor(
                    out=glu.ap(0, m * GL + 30 + t0, [[1, 512]]), in0=AP(self.ps[ba], 0, [[1, 512]]),
                    scalar=self.pvap(bcol + m), in1=sig.ap(si, 0, [[1, 512]]), op0=ALU.add, op1=ALU.mult)),
                    reads=[("ps", ba), ("pv",), sig.k(si)], writes=[glu.k(0, m, tile)])
        self.aoff = mark
        wout = Buf(self, "cwout", NCH * D, BF16)
        diag = Buf(self, "cdiag", 31 * 128, BF16, 2)
        vt = Buf(self, "cvt", NCH * 512, F32)
        vb = Buf(self, "cvb", 512, BF16, 2)
        sq = Buf(self, "csq2", 512, BF16, 2)
        st4 = Buf(self, "cst4", 512, F32, 4)
        sT = Buf(self, "csT", NCH * 512, BF16)
        yt = Buf(self, "cyt", NCH * 512, F32)
        rstd = Buf(self, "crstd2", 512, F32, 2)
        tmp = Buf(self, "ctmp", 512, F32, 2)
        for q2 in range(2):
            self.wload(wout.ap(0, q2 * 4096, [[1, 4096]]), wout.k(0, q2), "cwout", li, q2 * 4096, 4096,
                       [("wb", "cwout", li, qq) for qq in range(2)])
        dwc = PVC[("cm_dw", li)]
        idb = self.identb
        for tile in range(4):
            t0 = tile * 512
            bM, bQ = self.bank(), self.bank()
            self.held = {bM, bQ}
            for c in range(NCH):
                di = tile * NCH + c
                P.op("dve", (lambda c=c, di=di: nc.vector.tensor_tensor(
                    out=diag.ap(di, 0, [[128, 31], [1, 128]]), in0=idb.ap(0, 0, [[0, 31], [1, 128]]),
                    in1=self.pv.ap(0, dwc + c * 31, [[1, 31], [0, 128]]), op=ALU.mult)),
                    reads=[("identb",), ("pv",)], writes=[diag.k(di)])
                b = self.bank()

                def mm(c=c, di=di, b=b, t0=t0):
                    ins = None
                    for k in range(31):
                        ins = nc.tensor.matmul(AP(self.ps[b], 0, [[1, 512]]),
                                               lhsT=diag.ap(di, k * 128, [[1, 128]]),
                                               rhs=glu.ap(0, c * GL + t0 + k, [[1, 512]]),
                                               start=(k == 0), stop=(k == 30))
                    return ins
                rk = [glu.k(0, c, tile), glu.k(0, c, "z")] + ([glu.k(0, c, tile - 1)] if tile > 0 else [])
                P.op("pe", mm, reads=rk + [diag.k(di)], writes=[("ps", b)])
                pb = AP(self.ps[b], 0, [[1, 512]])
                bias = self.pvap(PVC[("cm_dw_b", li)] + c)
                vi = self.uid
                self.uid += 1
                P.op("act", (lambda c=c, pb=pb, bias=bias: nc.scalar.activation(
                    out=vt.ap(0, c * 512, [[1, 512]]), in_=pb, func=AF.Identity, bias=bias, scale=1.0)),
                    reads=[("ps", b), ("pv",)], writes=[vt.k(0, c)])
                P.op("act", (lambda vi=vi, pb=pb, bias=bias: nc.scalar.activation(
                    out=vb.ap(vi, 0, [[1, 512]]), in_=pb, func=AF.Identity, bias=bias, scale=1.0)),
                    reads=[("ps", b), ("pv",)], writes=[vb.k(vi)])
                P.op("act", (lambda vi=vi, pb=pb, bias=bias: nc.scalar.activation(
                    out=sq.ap(vi, 0, [[1, 512]]), in_=pb, func=AF.Square, bias=bias, scale=1.0)),
                    reads=[("ps", b), ("pv",)], writes=[sq.k(vi)])
                mb = self.meanb.ap(0, 0, [[1, 128]])
                P.op("pe", (lambda vi=vi, c=c, bM=bM: nc.tensor.matmul(
                    AP(self.ps[bM], 0, [[1, 512]]), lhsT=mb, rhs=vb.ap(vi, 0, [[1, 512]]),
                    start=(c == 0), stop=(c == NCH - 1))), reads=[vb.k(vi), ("meanb",)], writes=[("ps", bM)])
                P.op("pe", (lambda vi=vi, c=c, bQ=bQ: nc.tensor.matmul(
                    AP(self.ps[bQ], 0, [[1, 512]]), lhsT=mb, rhs=sq.ap(vi, 0, [[1, 512]]),
                    start=(c == 0), stop=(c == NCH - 1))), reads=[sq.k(vi), ("meanb",)], writes=[("ps", bQ)])
            self.held = set()
            mu, var, rs = (st4.ap(j, 0, [[1, 512]]) for j in range(3))
            pM, pQ = AP(self.ps[bM], 0, [[1, 512]]), AP(self.ps[bQ], 0, [[1, 512]])
            P.op("dve", (lambda mu=mu, pM=pM: nc.vector.tensor_copy(out=mu, in_=pM)), reads=[("ps", bM)], writes=[st4.k(0)])
            P.op("dve", (lambda mu=mu, var=var: nc.vector.tensor_tensor(out=var, in0=mu, in1=mu, op=ALU.mult)),
                 reads=[st4.k(0)], writes=[st4.k(1)])
            P.op("dve", (lambda pQ=pQ, var=var: nc.vector.tensor_tensor(out=var, in0=pQ, in1=var, op=ALU.subtract)),
                 reads=[("ps", bQ), st4.k(1)], writes=[st4.k(1)])
            P.op("act", (lambda rs=rs, var=var: nc.scalar.activation(out=rs, in_=var, func=AF.Sqrt, bias=self.epsap(), scale=1.0)),
                 reads=[st4.k(1), ("eps",)], writes=[st4.k(2)])
            P.op("dve", (lambda rs=rs: nc.vector.reciprocal(out=rs, in_=rs)),
                 reads=[st4.k(2)], writes=[st4.k(2)])
            for c in range(NCH):
                va = vt.ap(0, c * 512, [[1, 512]])
                P.op("dve", (lambda va=va, mu=mu: nc.vector.tensor_tensor(out=va, in0=va, in1=mu, op=ALU.subtract)),
                     reads=[vt.k(0, c), st4.k(0)], writes=[vt.k(0, c)])
                P.op("dve", (lambda va=va, rs=rs: nc.vector.tensor_tensor(out=va, in0=va, in1=rs, op=ALU.mult)),
                     reads=[vt.k(0, c), st4.k(2)], writes=[vt.k(0, c)])
                P.op("act", (lambda va=va, c=c: nc.scalar.activation(
                    out=sT.ap(0, c * 512, [[1, 512]]), in_=va, func=AF.Silu,
                    bias=self.pvap(PVC[("cm_ln_b", li)] + c), scale=self.pvap(PVC[("cm_ln_g", li)] + c))),
                    reads=[vt.k(0, c), ("pv",)], writes=[sT.k(0, c)])
            def mmf(m, b):
                def mm():
                    ins = None
                    for kc in range(NCH):
                        ins = nc.tensor.matmul(AP(self.ps[b], 0, [[1, 512]]),
                                               lhsT=wout.ap(0, kc * D + m * 128, [[1, 128]]),
                                               rhs=sT.ap(0, kc * 512, [[1, 512]]),
                                               start=(kc == 0), stop=(kc == NCH - 1))
                    return ins
                return mm, [wout.k(0, 0), wout.k(0, 1)] + [sT.k(0, kc) for kc in range(NCH)]

            def evf(m, b):
                return (lambda: nc.scalar.activation(
                    out=yt.ap(0, m * 512, [[1, 512]]), in_=AP(self.ps[b], 0, [[1, 512]]), func=AF.Identity,
                    bias=self.pvap(PVC[("cm_b_out", li)] + m), scale=1.0)), [("pv",)]
            self.outproj_postnorm(mmf, evf, PVC[("mix_post_g", li)], t0, yt, sq, rstd, tmp, add_eng="dve")

    def ffn(self, li):
        nc, P = self.nc, self.P
        self.phase()
        TT = 1024
        SG = TT + 4
        hT = Buf(self, "fhT", NCH * TT, BF16)
        win = Buf(self, "fwin", NCH * 256, BF16, 3)
        stg = Buf(self, "fstg", 2 * SG, BF16, 2)
        acc = Buf(self, "facc", 2 * TT, F32, 2)
        gated = Buf(self, "fgated", NFF * TT, BF16)
        wout = Buf(self, "fwout", NFF * 128, BF16, 2)
        yt = Buf(self, "fyt", NCH * 512, F32)
        sq = Buf(self, "fsq", 512, BF16, 2)
        rstd = Buf(self, "frstd", 512, F32, 2)
        tmp = Buf(self, "ftmp", 512, F32, 2)
        P.op("dve", (lambda: nc.vector.memset(self.halo.ap(0, 0, [[1, 88]]), 0.0)),
             writes=[("halo", j) for j in range(44)])
        dwc, dbc = PVC[("ffn_dw", li)], PVC[("ffn_dw_b", li)]

        def load_win(wi):
            j = wi % NFF
            self.wload(win.ap(wi, 0, [[1, NCH * 256]]), win.k(wi), "fwin", li * NFF + j, 0, NCH * 256,
                       [("wb", "fwin", li, j, 0), ("wb", "fwin", li, j, 1)])

        def load_wout(wo):
            m = wo % NCH
            self.wload(wout.ap(wo, 0, [[1, NFF * 128]]), wout.k(wo), "fwout", li * NCH + m, 0, NFF * 128,
                       [("wb", "fwout", li, m)])
        nwin = 2 * NFF
        load_win(0)
        load_win(1)

        def prenorm(hs):
            for sub in range(2):
                self.pre_norm(PVC[("ffn_pre_g", li)], hs * TT + sub * 512, hT, 0, sq, rstd, hoff=sub * 512,
                              hstride=TT, slot=sub)

        def stageA(hs, j):
            wi = hs * NFF + j
            si = wi
            for half in range(2):
                ch = j + half * NFF
                so = half * SG
                P.op("dve", (lambda so=so, ch=ch: nc.vector.tensor_copy(
                    out=stg.ap(si, so, [[1, 2]]), in_=self.halo.ap(0, ch * 2, [[1, 2]]))),
                    reads=[("halo", ch)], writes=[stg.k(si, half, "h")])
                for sub in range(2):
                    b = self.bank()

                    def mm(b=b, half=half, sub=sub):
                        ins = None
                        for kc in range(NCH):
                            ins = nc.tensor.matmul(AP(self.ps[b], 0, [[1, 512]]),
                                                   lhsT=win.ap(wi, kc * 256 + half * 128, [[1, 128]]),
                                                   rhs=hT.ap(0, kc * TT + sub * 512, [[1, 512]]),
                                                   start=(kc == 0), stop=(kc == NCH - 1))
                        return ins
                    P.op("pe", mm, reads=[win.k(wi)] + [hT.k(0, kc, sub) for kc in range(NCH)],
                         writes=[("ps", b)])
                    P.op("act", (lambda b=b, so=so, sub=sub: nc.scalar.copy(
                        out=stg.ap(si, so + 2 + sub * 512, [[1, 512]]), in_=AP(self.ps[b], 0, [[1, 512]]))),
                        reads=[("ps", b)], writes=[stg.k(si, half, sub)])
                P.op("dve", (lambda so=so, ch=ch: nc.vector.tensor_copy(
                    out=self.halo.ap(0, ch * 2, [[1, 2]]), in_=stg.ap(si, so + TT, [[1, 2]]))),
                    reads=[stg.k(si, half, 1)], writes=[("halo", ch)])
                aa = acc.ap(si, half * TT, [[1, TT]])
                rk = [stg.k(si, half, 0), stg.k(si, half, 1), stg.k(si, half, "h"), ("pv",)]
                P.op("act", (lambda so=so, ch=ch, aa=aa: nc.scalar.activation(
                    out=aa, in_=stg.ap(si, so, [[1, TT]]), func=AF.Identity,
                    bias=self.pvap(dbc + ch), scale=self.pvap(dwc + ch * 3))),
                    reads=rk, writes=[acc.k(si, half)])
                for k in (1, 2):
                    P.op("dve", (lambda so=so, ch=ch, aa=aa, k=k: nc.vector.scalar_tensor_tensor(
                        out=aa, in0=stg.ap(si, so + k, [[1, TT]]), scalar=self.pvap(dwc + ch * 3 + k),
                        in1=aa, op0=ALU.mult, op1=ALU.add)),
                        reads=rk + [acc.k(si, half)], writes=[acc.k(si, half)])

        def stageB(hs, j):
            si = hs * NFF + j
            ag = acc.ap(si, TT, [[1, TT]])
            P.op("act", (lambda: nc.scalar.activation(out=ag, in_=ag, func=AF.Silu)),
                 reads=[acc.k(si, 1)], writes=[acc.k(si, 1)])
            P.op("pool", (lambda: nc.gpsimd.tensor_tensor(
                out=gated.ap(0, j * TT, [[1, TT]]), in0=acc.ap(si, 0, [[1, TT]]), in1=ag, op=ALU.mult)),
                reads=[acc.k(si, 0), acc.k(si, 1)], writes=[gated.k(0, j)])

        prenorm(0)
        for hs in range(2):
            t0 = hs * TT
            for j in range(NFF):
                wi = hs * NFF + j
                if wi + 2 < nwin:
                    load_win(wi + 2)
                if j == NFF - 1:
                    load_wout(hs * 2 * NCH)
                stageA(hs, j)
                if j > 0:
                    stageB(hs, j - 1)
            stageB(hs, NFF - 1)
            if hs == 0:
                prenorm(1)
            gcol = PVC[("ffn_post_g", li)]
            for sub in range(2):
                tcol = t0 + sub * 512
                bS = self.bank()
                self.held = {bS}
                pst = AP(self.ps[bS], 0, [[1, 512]])
                ri = self.uid
                self.uid += 1
                sqs = []

                def stat(m, bS=bS, pst=pst):
                    sa, sk = sqs[m]
                    P.op("pe", (lambda: nc.tensor.matmul(pst, lhsT=self.meanb.ap(0, 0, [[1, 128]]), rhs=sa,
                                                         start=(m == 0), stop=(m == NCH - 1))),
                         reads=[sk, ("meanb",)], writes=[("ps", bS)])
                for m in range(NCH):
                    wo = (hs * 2 + sub) * NCH + m
                    if not (sub == 1 and m == NCH - 1):
                        load_wout(wo + 1)
                    b = self.bank()

                    def mm(b=b, wo=wo, sub=sub):
                        ins = None
                        for kc in range(NFF):
                            ins = nc.tensor.matmul(AP(self.ps[b], 0, [[1, 512]]),
                                                   lhsT=wout.ap(wo, kc * 128, [[1, 128]]),
                                                   rhs=gated.ap(0, kc * TT + sub * 512, [[1, 512]]),
                                                   start=(kc == 0), stop=(kc == NFF - 1))
                        return ins
                    P.op("pe", mm, reads=[wout.k(wo)] + [gated.k(0, kc) for kc in range(NFF)], writes=[("ps", b)])
                    if m > 0:
                        stat(m - 1)
                    P.op("act", (lambda m=m, b=b: nc.scalar.copy(
                        out=yt.ap(0, m * 512, [[1, 512]]), in_=AP(self.ps[b], 0, [[1, 512]]))),
                        reads=[("ps", b)], writes=[yt.k(0, m)])
                    qi = self.uid
                    self.uid += 1
                    sa = sq.ap(qi, 0, [[1, 512]])
                    sqs.append((sa, sq.k(qi)))
                    P.op("act", (lambda m=m, sa=sa: nc.scalar.activation(
                        out=sa, in_=yt.ap(0, m * 512, [[1, 512]]), func=AF.Square)),
                        reads=[yt.k(0, m)], writes=[sq.k(qi)])
                stat(NCH - 1)
                self.held = set()
                ra = rstd.ap(ri, 0, [[1, 512]])
                P.op("act", (lambda ra=ra, pst=pst: nc.scalar.activation(out=ra, in_=pst, func=AF.Sqrt,
                                                                         bias=self.epsap(), scale=1.0)),
                     reads=[("ps", bS), ("eps",)], writes=[rstd.k(ri)])
                P.op("dve", (lambda ra=ra: nc.vector.reciprocal(out=ra, in_=ra)), reads=[rstd.k(ri)], writes=[rstd.k(ri)])
                tile = tcol // 512
                for c in range(NCH):
                    ti = self.uid
                    self.uid += 1
                    P.op("dve", (lambda c=c, ti=ti, ra=ra: nc.vector.scalar_tensor_tensor(
                        out=tmp.ap(ti, 0, [[1, 512]]), in0=yt.ap(0, c * 512, [[1, 512]]), scalar=self.pvap(gcol + c),
                        in1=ra, op0=ALU.mult, op1=ALU.mult)),
                        reads=[yt.k(0, c), ("pv",), rstd.k(ri)], writes=[tmp.k(ti)])
                    P.op("pool", (lambda c=c, ti=ti, tcol=tcol: nc.gpsimd.tensor_tensor(
                        out=self.xap(c, tcol), in0=self.xap(c, tcol), in1=tmp.ap(ti, 0, [[1, 512]]), op=ALU.add)),
                        reads=[tmp.k(ti), self.xk(c, tile)], writes=[self.xk(c, tile)])

    def project(self, gcol, dname, base_idx, outs):
        nc, P = self.nc, self.P
        self.phase()
        hT = Buf(self, "phT", NCH * S, BF16)
        sq = Buf(self, "psq", 512, BF16, 2)
        rstd = Buf(self, "prstd", 512, F32, 2)
        wp = Buf(self, "pw", NCH * 128, BF16, 3)
        stg = Buf(self, "pstg", S, BF16, 2)

        def load(oi):
            self.wload(wp.ap(oi, 0, [[1, NCH * 128]]), wp.k(oi), dname, base_idx + oi, 0, NCH * 128,
                       [("wb", dname, base_idx + oi)])
        load(0)
        load(1)
        self.pre_norm(gcol, 0, hT, 0, sq, rstd, hoff=0, hstride=S, slot=0)
        self.pre_norm(gcol, 512, hT, 0, sq, rstd, hoff=512, hstride=S, slot=1)
        for oi, (sname, idx) in enumerate(outs):
            if oi + 2 < len(outs):
                load(oi + 2)
            for tile in range(4):
                if oi == 0 and tile < 2:
                    self.pre_norm(gcol, (tile + 2) * 512, hT, 0, sq, rstd, hoff=(tile + 2) * 512, hstride=S,
                                  slot=tile + 2)
                b = self.bank()

                def mm(b=b, oi=oi, tile=tile):
                    ins = None
                    for kc in range(NCH):
                        ins = nc.tensor.matmul(AP(self.ps[b], 0, [[1, 512]]),
                                               lhsT=wp.ap(oi, kc * 128, [[1, 128]]),
                                               rhs=hT.ap(0, kc * S + tile * 512, [[1, 512]]),
                                               start=(kc == 0), stop=(kc == NCH - 1))
                    return ins
                P.op("pe", mm, reads=[wp.k(oi)] + [hT.k(0, kc, tile) for kc in range(NCH)], writes=[("ps", b)])
                P.op("act", (lambda b=b, oi=oi, tile=tile: nc.scalar.copy(
                    out=stg.ap(oi, tile * 512, [[1, 512]]), in_=AP(self.ps[b], 0, [[1, 512]]))),
                    reads=[("ps", b)], writes=[stg.k(oi, tile)])
            dst = bass.AP(self.scr[sname], idx * 128 * S, [[S, 128], [1, S]])
            P.op("pool", (lambda oi=oi, dst=dst: nc.gpsimd.dma_start(out=dst, in_=stg.ap(oi, 0, [[1, S]]))),
                 reads=[stg.k(oi, t) for t in range(4)], writes=[("scr", sname, idx)], dma=True)

    def attention(self, lj, li):
        nc, P = self.nc, self.P
        self.project(PVC[("mix_pre_g", li)], "wq", lj * 24, [("q", gh) for gh in range(24)])
        self.phase()
        mark = self.aoff
        OM01 = Buf(self, "aOM01", 4 * S, BF16)
        OM2 = Buf(self, "aOM2", 2 * S, BF16, 2)
        Dd01 = Buf(self, "aDd01", 2 * S, F32)
        Dd2 = Buf(self, "aDd2", S, F32, 2)
        tA = Buf(self, "atA", S, BF16)
        tE = Buf(self, "atE", S, F32)
        nacc = Buf(self, "anacc", S, F32)
        Dacc = Buf(self, "aDacc", S, F32)
        aS = Buf(self, "aaS", S, BF16, 2)
        qkv = Buf(self, "aqkv", 3 * S, BF16, 2)
        vtok = Buf(self, "avtok", 16 * 128, BF16)
        pbuf = Buf(self, "aP", 256, BF16, 4)
        ptb = Buf(self, "aPT", 256, BF16, 4)
        dg = Buf(self, "adg", 128, BF16, 6)
        st = Buf(self, "ast", 2, F32, 6)
        mbf = Buf(self, "ambf", 2, BF16, 6)
        idb = self.identb.ap(0, 0, [[1, 128]])
        ones = self.onesb.ap(0, 0, [[1, 128]])
        ghs = [(h, g) for h in range(NHEAD) for g in (2, 1, 0)]

        def load_qkv(ci):
            h, g = ghs[ci]
            gh = g * 8 + h
            for qi, nm in enumerate(("q", "k", "v")):
                src = bass.AP(self.scr[nm], gh * 128 * S, [[S, 128], [1, S]])
                P.op("sp", (lambda qi=qi, src=src, ci=ci: nc.sync.dma_start(out=qkv.ap(ci, qi * S, [[1, S]]), in_=src)),
                     reads=[("scr", nm, gh)], writes=[qkv.k(ci, qi)], dma=True)

        def wins(g, n):
            return [0, 1, 2, 3] if g == 2 else ([n] if g == 1 else [n // 4])
        load_qkv(0)
        bi = 0
        pending = []
        sbank = [0]
        xbank = [0]
        mcount = [0]
        for ci, (h, g) in enumerate(ghs):
            if ci + 1 < len(ghs):
                load_qkv(ci + 1)
            r = DIL[g]
            nb = (S // r) // 128
            blocks = [(rho, n) for rho in range(r) for n in range(nb)]
            for hb in range(2):
                pbk = self.psb[hb]

                def mmV(pbk=pbk, hb=hb, r=r, blocks=blocks, ci=ci):
                    ins = None
                    for sl in range(8):
                        rho, n = blocks[hb * 8 + sl]
                        ins = nc.tensor.transpose(AP(pbk, sl * 128, [[1, 128]]),
                                                  qkv.ap(ci, 2 * S + rho + r * 128 * n, [[r, 128]]), idb)
                    return ins
                P.op("pe", mmV, reads=[qkv.k(ci, 2), ("identb",)], writes=[("psb", hb)])
                P.op("act", (lambda pbk=pbk, hb=hb, ci=ci: nc.scalar.copy(
                    out=vtok.ap(ci, hb * 1024, [[1, 1024]]), in_=AP(pbk, 0, [[1, 1024]]))),
                    reads=[("psb", hb)], writes=[vtok.k(ci, hb)])

            def stageA(bidx, bi, ci=ci, r=r, blocks=blocks):
                rho, n = blocks[bidx]
                nk = 256 if n > 0 else 128
                koff = rho + r * 128 * (n - 1 if n > 0 else 0)
                qap = qkv.ap(ci, rho + r * 128 * n, [[r, 128]])
                kap = qkv.ap(ci, S + koff, [[r, nk]])
                b = sbank[0] % 3
                sbank[0] += 1
                pS = AP(self.ps[b], 0, [[1, nk]])
                P.op("pe", (lambda: nc.tensor.matmul(pS, lhsT=qap, rhs=kap, start=True, stop=True)),
                     reads=[qkv.k(ci, 0), qkv.k(ci, 1)], writes=[("ps", b)])
                ng = st.ap(bi, 1, [[1, 1]])
                mb = mbf.ap(bi, 0, [[1, 1]])
                P.op("dve", (lambda: nc.vector.reduce_max(out=mb, in_=pS, axis=AX.X)),
                     reads=[("ps", b)], writes=[mbf.k(bi)])
                P.op("dve", (lambda: nc.vector.tensor_scalar(out=ng, in0=mb, scalar1=-SCALE, scalar2=None, op0=ALU.mult)),
                     reads=[mbf.k(bi)], writes=[st.k(bi, 1)])
                P.op("dve", (lambda: nc.vector.tensor_scalar(
                    out=dg.ap(bi, 0, [[1, 128]]), in0=idb, scalar1=ng, scalar2=-1.0 / SCALE,
                    op0=ALU.mult, op1=ALU.mult)),
                    reads=[st.k(bi, 1), ("identb",)], writes=[dg.k(bi)])
                P.op("act", (lambda: nc.scalar.activation(
                    out=pbuf.ap(bi, 0, [[1, nk]]), in_=pS, func=AF.Exp, bias=ng, scale=SCALE)),
                    reads=[("ps", b), st.k(bi, 1)], writes=[pbuf.k(bi)])

            def stageB(bidx, bi, blocks=blocks):
                rho, n = blocks[bidx]
                nkb = 2 if n > 0 else 1
                nk = nkb * 128
                pbk = self.psb[bi % 2]

                def mmT():
                    ins = None
                    for kb in range(nkb):
                        ins = nc.tensor.transpose(AP(pbk, kb * 128, [[1, 128]]),
                                                  pbuf.ap(bi, kb * 128, [[1, 128]]), idb)
                    return ins
                P.op("pe", mmT, reads=[pbuf.k(bi), ("identb",)], writes=[("psb", bi % 2)])
                P.op("dve", (lambda: nc.vector.tensor_tensor(
                    out=ptb.ap(bi, 0, [[1, nk]]), in0=AP(pbk, 0, [[1, nk]]),
                    in1=self.maskT.ap(0, 256 - nk, [[1, nk]]), op=ALU.mult)),
                    reads=[("psb", bi % 2), ("maskT",)], writes=[ptb.k(bi)])

            def stageC(bidx, bi, g=g, r=r, blocks=blocks, ci=ci, h=h):
                rho, n = blocks[bidx]
                nkb = 2 if n > 0 else 1
                kblocks = ([bidx - 1, bidx] if n > 0 else [bidx])
                bx = 3 + xbank[0] % 3
                xbank[0] += 1

                def mmO():
                    ins = None
                    for kb in range(nkb):
                        ins = nc.tensor.matmul(AP(self.ps[bx], 0, [[1, 128]]),
                                               lhsT=vtok.ap(ci, kblocks[kb] * 128, [[1, 128]]),
                                               rhs=ptb.ap(bi, kb * 128, [[1, 128]]),
                                               start=(kb == 0), stop=(kb == nkb - 1))
                    ins = nc.tensor.matmul(AP(self.ps[bx], 128, [[1, 128]]), lhsT=ones,
                                           rhs=dg.ap(bi, 0, [[1, 128]]), start=True, stop=True)
                    for kb in range(nkb):
                        ins = nc.tensor.matmul(AP(self.ps[bx], 256, [[1, 128]]), lhsT=ones,
                                               rhs=ptb.ap(bi, kb * 128, [[1, 128]]),
                                               start=(kb == 0), stop=(kb == nkb - 1))
                    return ins
                P.op("pe", mmO, reads=[vtok.k(ci, 0), vtok.k(ci, 1), ptb.k(bi), dg.k(bi), ("onesb",)], writes=[("ps", bx)])
                toff = rho + r * 128 * n
                ws = wins(g, n)
                if g == 2:
                    omap = OM2.ap(h, toff, [[S, 2], [r, 128]])
                    omk = [OM2.k(h, w) for w in ws]
                    dap = Dd2.ap(h, toff, [[r, 128]])
                    dk_ = [Dd2.k(h, w) for w in ws]
                else:
                    omap = OM01.ap(0, g * S + toff, [[2 * S, 2], [r, 128]])
                    omk = [OM01.k(0, g, w) for w in ws]
                    dap = Dd01.ap(0, g * S + toff, [[r, 128]])
                    dk_ = [Dd01.k(0, g, w) for w in ws]
                P.op("act", (lambda: nc.scalar.copy(out=omap, in_=AP(self.ps[bx], 0, [[128, 2], [1, 128]]))),
                     reads=[("ps", bx)], writes=omk)
                P.op("act", (lambda: nc.scalar.copy(out=dap, in_=AP(self.ps[bx], 256, [[1, 128]]))),
                     reads=[("ps", bx)], writes=dk_)

            def merge_ops(h=h):
                W4 = range(4)

                def om(kind, g_, w):
                    if g_ == 2:
                        return OM2.ap(h, kind * S + w * 512, [[1, 512]])
                    return OM01.ap(0, (kind * 2 + g_) * S + w * 512, [[1, 512]])

                def dd(g_, w):
                    if g_ == 2:
                        return Dd2.ap(h, w * 512, [[1, 512]])
                    return Dd01.ap(0, g_ * S + w * 512, [[1, 512]])

                def ok(g_, w):
                    return [OM2.k(h, w)] if g_ == 2 else [OM01.k(0, g_, w)]

                def dk(g_, w):
                    return [Dd2.k(h, w)] if g_ == 2 else [Dd01.k(0, g_, w)]

                def tAa(w):
                    return tA.ap(0, w * 512, [[1, 512]])

                def tEa(w):
                    return tE.ap(0, w * 512, [[1, 512]])

                def na(w):
                    return nacc.ap(0, w * 512, [[1, 512]])

                def da(w):
                    return Dacc.ap(0, w * 512, [[1, 512]])
                ops = []

                def add(ph, eng, fn, reads, writes, dma=False):
                    ops.append((ph, lambda: P.op(eng, fn, reads=reads, writes=writes, dma=dma)))
                for w in W4:
                    add(0, "dve", (lambda w=w: nc.vector.tensor_tensor(out=tAa(w), in0=om(1, 0, w), in1=om(1, 1, w), op=ALU.max)),
                        ok(0, w) + ok(1, w), [tA.k(0, w)])
                for w in W4:
                    add(0, "dve", (lambda w=w: nc.vector.tensor_tensor(out=tAa(w), in0=tAa(w), in1=om(1, 2, w), op=ALU.max)),
                        ok(2, w) + [tA.k(0, w)], [tA.k(0, w)])
                for oi_, g_ in enumerate((1, 0, 2)):
                    for w in W4:
                        add(oi_, "pool", (lambda g_=g_, w=w: nc.gpsimd.tensor_tensor(
                            out=tEa(w), in0=om(1, g_, w), in1=tAa(w), op=ALU.subtract)),
                            ok(g_, w) + [tA.k(0, w)], [tE.k(0, w)])
                    for w in W4:
                        add(oi_, "act", (lambda w=w: nc.scalar.activation(out=tEa(w), in_=tEa(w), func=AF.Exp, scale=SCALE)),
                            [tE.k(0, w)], [tE.k(0, w)])
                    if oi_ == 0:
                        for w in W4:
                            add(oi_, "pool", (lambda g_=g_, w=w: nc.gpsimd.tensor_tensor(
                                out=da(w), in0=dd(g_, w), in1=tEa(w), op=ALU.mult)),
                                dk(g_, w) + [tE.k(0, w)], [Dacc.k(0, w)])
                        for w in W4:
                            add(oi_, "dve", (lambda g_=g_, w=w: nc.vector.tensor_tensor(
                                out=na(w), in0=om(0, g_, w), in1=tEa(w), op=ALU.mult)),
                                ok(g_, w) + [tE.k(0, w)], [nacc.k(0, w)])
                    else:
                        for w in W4:
                            add(oi_, "pool", (lambda g_=g_, w=w: nc.gpsimd.tensor_tensor(
                                out=dd(g_, w), in0=dd(g_, w), in1=tEa(w), op=ALU.mult)),
                                dk(g_, w) + [tE.k(0, w)], dk(g_, w))
                        for w in W4:
                            add(oi_, "dve", (lambda g_=g_, w=w: nc.vector.tensor_tensor(
                                out=tEa(w), in0=om(0, g_, w), in1=tEa(w), op=ALU.mult)),
                                ok(g_, w) + [tE.k(0, w)], [tE.k(0, w)])
                        for w in W4:
                            add(oi_, "pool", (lambda g_=g_, w=w: nc.gpsimd.tensor_tensor(
                                out=da(w), in0=da(w), in1=dd(g_, w), op=ALU.add)),
                                dk(g_, w) + [Dacc.k(0, w)], [Dacc.k(0, w)])
                        for w in W4:
                            add(oi_, "pool", (lambda w=w: nc.gpsimd.tensor_tensor(out=na(w), in0=na(w), in1=tEa(w), op=ALU.add)),
                                [nacc.k(0, w), tE.k(0, w)], [nacc.k(0, w)])
                for w in W4:
                    add(2, "act", (lambda w=w: nc.scalar.activation(out=da(w), in_=da(w), func=AF.Ln)),
                        [Dacc.k(0, w)], [Dacc.k(0, w)])
                for w in W4:
                    add(2, "act", (lambda w=w: nc.scalar.activation(out=da(w), in_=da(w), func=AF.Exp, scale=-1.0)),
                        [Dacc.k(0, w)], [Dacc.k(0, w)])
                for w in W4:
                    add(2, "dve", (lambda w=w: nc.vector.tensor_tensor(
                        out=aS.ap(h, w * 512, [[1, 512]]), in0=na(w), in1=da(w), op=ALU.mult)),
                        [Dacc.k(0, w), nacc.k(0, w)], [aS.k(h, w)])
                dst = bass.AP(self.scr_a, h * 128 * S, [[S, 128], [1, S]])
                add(2, "pool", (lambda: nc.gpsimd.dma_start(out=dst, in_=aS.ap(h, 0, [[1, S]]))),
                    [aS.k(h, w) for w in W4], [("scra", h)], dma=True)
                return ops
            nblk = len(blocks)
            phase_i = 2 - g
            mine = [e for (ph, e) in pending if ph == phase_i]
            per_it = -(-len(mine) // nblk)
            mi_ = 0
            LB, LC = 2, 4
            for it in range(nblk + LC):
                if it < nblk:
                    stageA(it, bi + it)
                if 0 <= it - LB < nblk:
                    stageB(it - LB, bi + it - LB)
                if 0 <= it - LC < nblk:
                    stageC(it - LC, bi + it - LC)
                if it >= 1:
                    for _ in range(per_it):
                        if mi_ < len(mine):
                            mine[mi_]()
                            mi_ += 1
            while mi_ < len(mine):
                mine[mi_]()
                mi_ += 1
            bi += nblk
            if g == 0:
                pending = merge_ops()
        for (ph, e) in pending:
            e()
        self.aoff = mark
        wo = Buf(self, "awo", NCH * D, BF16)
        at = Buf(self, "aat", NHEAD * 512, BF16, 2)
        yt = Buf(self, "ayt", NCH * 512, F32)
        sq = Buf(self, "asq", 512, BF16, 2)
        rstd = Buf(self, "arstd", 512, F32, 2)
        tmp = Buf(self, "atmp2", 512, F32, 2)
        for q2 in range(2):
            self.wload(wo.ap(0, q2 * 4096, [[1, 4096]]), wo.k(0, q2), "wo", lj, q2 * 4096, 4096,
                       [("wb", "wo", lj, qq) for qq in range(2)])

        def load_at(tile):
            src = bass.AP(self.scr_a, tile * 512, [[S, 128], [128 * S, NHEAD], [1, 512]])
            P.op("sp", (lambda: nc.sync.dma_start(out=at.ap(tile, 0, [[512, NHEAD], [1, 512]]), in_=src)),
                 reads=[("scra", hh) for hh in range(NHEAD)], writes=[at.k(tile)], dma=True)
        load_at(0)
        for tile in range(4):
            t0 = tile * 512
            if tile + 1 < 4:
                load_at(tile + 1)
            def mmf(m, b, tile=tile):
                def mm():
                    ins = None
                    for kc in range(NCH):
                        ins = nc.tensor.matmul(AP(self.ps[b], 0, [[1, 512]]),
                                               lhsT=wo.ap(0, kc * D + m * 128, [[1, 128]]),
                                               rhs=at.ap(tile, kc * 512, [[1, 512]]),
                                               start=(kc == 0), stop=(kc == NCH - 1))
                    return ins
                return mm, [wo.k(0, 0), wo.k(0, 1), at.k(tile)]

            def evf(m, b):
                return (lambda: nc.scalar.copy(out=yt.ap(0, m * 512, [[1, 512]]),
                                               in_=AP(self.ps[b], 0, [[1, 512]]))), []
            self.outproj_postnorm(mmf, evf, PVC[("mix_post_g", li)], t0, yt, sq, rstd, tmp)

    def kv_project(self):
        outs = [("k", gh) for gh in range(24)] + [("v", gh) for gh in range(24)]
        self.project(PVC[("kv_norm_g", 0)], "wkv", 0, outs)

    def build(self):
        self.consts()
        self.precast_all()
        stop = self.stop_after
        for s_ in range(self.nseq):
            self.phase()
            self.load_x(s_)
            done = False
            for li in range(4):
                if li < 2:
                    self.conformer(li)
                else:
                    self.attention(li - 2, li)
                if stop == ("mix", li):
                    break
                self.ffn(li)
                if stop == ("ffn", li):
                    break
                if li == 1:
                    self.kv_project()
            self.phase()
            self.store_x(s_)
        n = self.P.emit()
        return self.nc, n


WNAMES = ("cm_w_in", "cm_w_out", "w_kv", "w_q", "w_o", "ffn_w_in", "ffn_w_out")


def kernel(**inputs):
    x = np.ascontiguousarray(np.asarray(inputs["x"], dtype=np.float32))
    kb = K()
    nc, _ = kb.build()
    pv = pack_pv(inputs)
    ws = {nm: np.ascontiguousarray(np.asarray(inputs[nm], dtype=np.float32)) for nm in WNAMES}
    in_maps = []
    for c in range(N_CORES):
        m = {"x": x[c * SEQ_PER_CORE:(c + 1) * SEQ_PER_CORE], "pv": pv}
        m.update(ws)
        in_maps.append(m)
    res = run_bass_kernel_spmd(nc, in_maps, core_ids=list(range(N_CORES)))
    return np.concatenate([r["out"] for r in res.results], axis=0)
```

```python
import numpy as np
import concourse.bass as bass
import concourse.mybir as mybir
from concourse.bass_utils import run_bass_kernel_spmd

F32, BF16 = mybir.dt.float32, mybir.dt.bfloat16
AF = mybir.ActivationFunctionType
ALU = mybir.AluOpType
AX = mybir.AxisListType

D = 1024
S = 2048
NCH = 8
DFF = 2816
NFF = 22
NHEAD = 8
NG = 3
DIL = (1, 4, 16)
EPS = 1e-6
N_CORES = 8
SEQ_PER_CORE = 2


class Prog:
    def __init__(self, nc):
        self.nc = nc
        self.ops = []
        self.alias = {}

    def op(self, eng, fn, reads=(), writes=(), dma=False):
        reads, writes = tuple(reads), tuple(writes)
        extra = tuple(k for k in reads if k[0] in ("ps", "psb") and k not in writes)
        self.ops.append((eng, fn, reads, writes + extra, dma))

    def barrier(self):
        self.ops.append(("barrier", None, (), (), False))

    def emit(self):
        nc = self.nc
        ops = self.ops
        engobj = {"pe": nc.tensor, "act": nc.scalar, "dve": nc.vector,
                  "pool": nc.gpsimd, "sp": nc.sync}
        n = len(ops)
        deps = [None] * n
        needed = [False] * n
        last_w, readers = {}, {}
        last_on = {}
        dmas_since = []
        pend = {e: set() for e in engobj}
        buf_keys, alias_deps = {}, {}
        for i, (eng, fn, R, W, dma) in enumerate(ops):
            if eng == "barrier":
                allp = set(last_on.values()) | set(dmas_since)
                for e in engobj:
                    pend[e] |= allp
                last_w, readers, dmas_since = {}, {}, []
                deps[i] = set()
                continue
            d = set(pend[eng])
            pend[eng] = set()
            for k in R + W:
                bn = k[0]
                buf_keys.setdefault(bn, set()).add(k)
                if bn in self.alias:
                    if bn not in alias_deps:
                        sset = set()
                        for ob in self.alias[bn]:
                            for k2 in buf_keys.get(ob, ()):
                                if k2 in last_w:
                                    sset.add(last_w[k2])
                                sset.update(readers.get(k2, ()))
                        best = {}
                        red = set()
                        for j in sset:
                            if ops[j][4]:
                                red.add(j)
                            else:
                                e_ = ops[j][0]
                                if e_ not in best or best[e_] < j:
                                    best[e_] = j
                        red.update(best.values())
                        alias_deps[bn] = red
                    d |= alias_deps[bn]
            for k in R:
                if k in last_w:
                    d.add(last_w[k])
            for k in W:
                if k in last_w:
                    d.add(last_w[k])
                for r in readers.get(k, ()):
                    d.add(r)
            d2 = set()
            for j in d:
                ej, _, _, _, dj = ops[j]
                if ej == "pe" and eng == "pe" and not dj and not dma:
                    continue
                d2.add(j)
            deps[i] = d2
            for j in d2:
                needed[j] = True
            for k in R:
                readers.setdefault(k, []).append(i)
            for k in W:
                last_w[k] = i
                readers[k] = []
            if dma:
                dmas_since.append(i)
            else:
                last_on[eng] = i
        final = set(last_on.values()) | set(dmas_since)
        for e in engobj:
            final |= pend[e]
        for j in final:
            needed[j] = True
        ROLL = 30000
        sems = {}

        def getsem(name):
            if name not in sems:
                sems[name] = nc.semaphore(name).__enter__()
            return sems[name]

        NPOOL = 24
        cnt = {e: 0 for e in engobj}
        dma_pools = {e: [0, [0] * NPOOL] for e in engobj}
        tok = [None] * n
        known = {e: {} for e in engobj}

        def wait(eng, t):
            name, val = t
            if known[eng].get(name, 0) >= val:
                return
            engobj[eng].wait_ge(getsem(name), val)
            known[eng][name] = val

        for i, (eng, fn, R, W, dma) in enumerate(ops):
            if eng == "barrier":
                continue
            for j in sorted(deps[i]):
                wait(eng, tok[j])
            if dma:
                pool_ = dma_pools[eng]
                k = pool_[0] % NPOOL
                pool_[0] += 1
                name = "dq%s%d" % (eng, k)
                tot = pool_[1]
                if tot[k] > 0:
                    wait(eng, (name, tot[k]))
                tot[k] += 16
                ins = fn()
                ins.then_inc(getsem(name), 16)
                tok[i] = (name, tot[k])
            else:
                ins = fn()
                if needed[i]:
                    cnt[eng] += 1
                    name = "%s%d" % (eng, cnt[eng] // ROLL)
                    val = cnt[eng] % ROLL
                    if val == 0:
                        val = ROLL
                        name = "%s%d" % (eng, cnt[eng] // ROLL - 1)
                    ins.then_inc(getsem(name), 1)
                    tok[i] = (name, val)
        for j in sorted(final):
            if tok[j] is not None:
                wait("sp", tok[j])
        return n


def AP(t, off, dims, F=None, parts=128):
    if F is None:
        F = 1
        for s_ in t.shape[1:]:
            F *= s_
    return bass.AP(t, off, [[F, parts]] + [list(x) for x in dims])


SCALE = 1.0 / float(np.sqrt(128.0))
NEG = -30000.0
ARENA_LO = 16640
ARENA_HI = 229376


def pv_layout():
    cols = {}
    n = [0]

    def add(name, w):
        cols[name] = n[0]
        n[0] += w
    for i in range(4):
        for nm in ("mix_pre_g", "mix_post_g", "ffn_pre_g", "ffn_post_g"):
            add((nm, i), 8)
    add(("kv_norm_g", 0), 8)
    for i in range(2):
        add(("cm_b_in", i), 16)
        add(("cm_dw", i), 248)
        add(("cm_dw_b", i), 8)
        add(("cm_ln_g", i), 8)
        add(("cm_ln_b", i), 8)
        add(("cm_b_out", i), 8)
    for i in range(4):
        add(("ffn_dw", i), 132)
        add(("ffn_dw_b", i), 44)
    return cols, n[0]


PVC, NPV = pv_layout()


def pack_pv(inp):
    pv = np.zeros((128, NPV), np.float32)

    def vec(v):
        v = np.asarray(v, np.float32)
        return v.reshape(-1, 128).T
    for i in range(4):
        for nm in ("mix_pre_g", "mix_post_g", "ffn_pre_g", "ffn_post_g"):
            c = PVC[(nm, i)]
            pv[:, c:c + 8] = vec(inp[nm][i])
    c = PVC[("kv_norm_g", 0)]
    pv[:, c:c + 8] = vec(inp["kv_norm_g"])
    for i in range(2):
        c = PVC[("cm_b_in", i)]
        pv[:, c:c + 16] = vec(inp["cm_b_in"][i])
        c = PVC[("cm_dw", i)]
        dw = np.asarray(inp["cm_dw"][i], np.float32).reshape(31, 8, 128)
        pv[:, c:c + 248] = dw.transpose(2, 1, 0).reshape(128, 248)
        for nm in ("cm_dw_b", "cm_ln_g", "cm_ln_b", "cm_b_out"):
            c = PVC[(nm, i)]
            pv[:, c:c + 8] = vec(inp[nm][i])
    for i in range(4):
        c = PVC[("ffn_dw", i)]
        dw = np.asarray(inp["ffn_dw"][i], np.float32).reshape(3, 44, 128)
        pv[:, c:c + 132] = dw.transpose(2, 1, 0).reshape(128, 132)
        c = PVC[("ffn_dw_b", i)]
        pv[:, c:c + 44] = vec(inp["ffn_dw_b"][i])
    return pv


class Buf:
    def __init__(self, kb, name, F, dtype, nbuf=1):
        self.F = F
        self.key = kb.name(name)
        self.n = nbuf
        self.t = []
        esz = 4 if dtype == F32 else 2
        lo = (kb.aoff + 63) // 64 * 64
        for i in range(nbuf):
            off = (kb.aoff + 63) // 64 * 64
            assert off + F * esz <= ARENA_HI, ("SBUF overflow", name, off, F * esz)
            self.t.append(kb.nc.alloc_sbuf_tensor_at("%s_%d" % (self.key, i), [128, F], dtype, offset=off))
            kb.aoff = off + F * esz
        hi = kb.aoff
        al = [nm for (nm, l2, h2) in kb.all_bufs if l2 < hi and lo < h2]
        if al:
            kb.P.alias[self.key] = al
        kb.all_bufs.append((self.key, lo, hi))

    def ap(self, i, off, dims, parts=128):
        return AP(self.t[i % self.n], off, dims, F=self.F, parts=parts)

    def k(self, i, *sub):
        return (self.key, i % self.n) + tuple(sub)


class K:
    def __init__(self, nseq=SEQ_PER_CORE, stop_after=None):
        self.nseq = nseq
        self.stop_after = stop_after
        nc = bass.Bass("TRN2", target_bir_lowering=False)
        self.nc = nc
        self.P = Prog(nc)
        self.uid = 0
        self.all_bufs = []
        dt = nc.dram_tensor
        self.x_in = dt("x", [nseq, S, D], F32, kind="ExternalInput")
        self.out = dt("out", [nseq, S, D], F32, kind="ExternalOutput")
        self.pv_d = dt("pv", [128, NPV], F32, kind="ExternalInput")
        self.w = {
            "cm_w_in": dt("cm_w_in", [2, D, 2 * D], F32, kind="ExternalInput"),
            "cm_w_out": dt("cm_w_out", [2, D, D], F32, kind="ExternalInput"),
            "w_kv": dt("w_kv", [D, 6144], F32, kind="ExternalInput"),
            "w_q": dt("w_q", [2, D, 3072], F32, kind="ExternalInput"),
            "w_o": dt("w_o", [2, D, D], F32, kind="ExternalInput"),
            "ffn_w_in": dt("ffn_w_in", [4, D, 2 * DFF], F32, kind="ExternalInput"),
            "ffn_w_out": dt("ffn_w_out", [4, DFF, D], F32, kind="ExternalInput"),
        }
        self.scr = {nm: dt("scr_" + nm, [24, 128, S], BF16, kind="Internal") for nm in ("q", "k", "v")}
        self.scr_a = dt("scr_a", [NHEAD, 128, S], BF16, kind="Internal")
        self.wb = {
            "cwin": dt("wb_cwin", [2, 128, NCH * 2 * D], BF16, kind="Internal"),
            "cwout": dt("wb_cwout", [2, 128, NCH * D], BF16, kind="Internal"),
            "wkv": dt("wb_wkv", [48, 128, NCH * 128], BF16, kind="Internal"),
            "wq": dt("wb_wq", [2 * 24, 128, NCH * 128], BF16, kind="Internal"),
            "wo": dt("wb_wo", [2, 128, NCH * D], BF16, kind="Internal"),
            "fwin": dt("wb_fwin", [4 * NFF, 128, NCH * 256], BF16, kind="Internal"),
            "fwout": dt("wb_fwout", [4 * NCH, 128, NFF * 128], BF16, kind="Internal"),
        }
        self.aoff = ARENA_LO
        self.xT = Buf(self, "xT", NCH * S, F32)
        self.identf = Buf(self, "identf", 128, F32)
        self.identb = Buf(self, "identb", 128, BF16)
        self.onesb = Buf(self, "onesb", 128, BF16)
        self.meanb = Buf(self, "meanb", 128, BF16)
        self.maskT = Buf(self, "maskT", 256, BF16)
        self.pv = Buf(self, "pvs", NPV, F32)
        self.halo = Buf(self, "halo", 44 * 2, BF16)
        self.epsb = Buf(self, "epsb", 1, F32)
        self.arena_base = self.aoff
        self.ps = [nc.alloc_psum_tensor("ps%d" % b, [128, 512], F32) for b in range(6)]
        self.psb = [nc.alloc_psum_tensor("psb%d" % b, [128, 1024], BF16) for b in range(2)]
        self.psn = 0
        self.psbn = 0
        self.held = set()

    def name(self, s_):
        self.uid += 1
        return "%s_%d" % (s_, self.uid)

    def bank(self):
        while True:
            b = self.psn % 6
            self.psn += 1
            if b not in self.held:
                return b

    def epsap(self):
        return self.epsb.ap(0, 0, [[1, 1]])

    def pvap(self, col, n=1):
        return self.pv.ap(0, col, [[1, n]])

    def phase(self):
        self.aoff = self.arena_base

    def consts(self):
        nc, P = self.nc, self.P
        idf = self.identf.ap(0, 0, [[1, 128]])
        P.op("pool", lambda: nc.gpsimd.memset(idf, 0.0), writes=[("identf",)])
        P.op("pool", lambda: nc.gpsimd.affine_select(
            out=idf, in_=idf, pattern=[[-1, 128]], compare_op=ALU.not_equal, fill=1.0,
            base=0, channel_multiplier=1), reads=[("identf",)], writes=[("identf",)])
        P.op("dve", lambda: nc.vector.tensor_copy(out=self.identb.ap(0, 0, [[1, 128]]), in_=idf),
             reads=[("identf",)], writes=[("identb",)])
        P.op("dve", lambda: nc.vector.memset(self.onesb.ap(0, 0, [[1, 128]]), 1.0), writes=[("onesb",)])
        P.op("dve", lambda: nc.vector.memset(self.meanb.ap(0, 0, [[1, 128]]), 1.0 / D), writes=[("meanb",)])
        P.op("dve", lambda: nc.vector.memset(self.epsap(), EPS), writes=[("eps",)])
        mk = self.maskT
        P.op("pool", lambda: nc.gpsimd.memset(mk.ap(0, 0, [[1, 256]]), 1.0), writes=[("maskT",)])
        P.op("pool", lambda: nc.gpsimd.affine_select(
            out=mk.ap(0, 0, [[1, 128]]), in_=mk.ap(0, 0, [[1, 128]]), pattern=[[-1, 128]],
            compare_op=ALU.is_ge, fill=0.0, base=0, channel_multiplier=1),
            reads=[("maskT",)], writes=[("maskT",)])
        P.op("pool", lambda: nc.gpsimd.affine_select(
            out=mk.ap(0, 128, [[1, 128]]), in_=mk.ap(0, 128, [[1, 128]]), pattern=[[1, 128]],
            compare_op=ALU.is_ge, fill=0.0, base=0, channel_multiplier=-1),
            reads=[("maskT",)], writes=[("maskT",)])
        P.op("sp", lambda: nc.sync.dma_start(out=self.pv.ap(0, 0, [[1, NPV]]),
                                             in_=bass.AP(self.pv_d, 0, [[NPV, 128], [1, NPV]])),
             writes=[("pv",)], dma=True)

    CK = [("identf",), ("identb",), ("onesb",), ("meanb",), ("maskadd",), ("pv",)]

    def xk(self, c, tile):
        return ("xT", c, tile)

    def xap(self, c, t0, n=512):
        return self.xT.ap(0, c * S + t0, [[1, n]])

    def load_x(self, s_):
        nc, P = self.nc, self.P
        stg = Buf(self, "xstg", D, F32, 2)
        idf = self.identf.ap(0, 0, [[1, 128]])
        for tt in range(S // 128):
            src = bass.AP(self.x_in, (s_ * S + tt * 128) * D, [[D, 128], [1, D]])
            P.op("sp", (lambda tt=tt, src=src: nc.sync.dma_start(out=stg.ap(tt, 0, [[1, D]]), in_=src)),
                 writes=[stg.k(tt)], dma=True)
            for c0 in range(0, NCH, 4):
                b = self.bank()
                pst = self.ps[b]

                def mm(tt=tt, pst=pst, c0=c0):
                    ins = None
                    for cc in range(4):
                        ins = nc.tensor.transpose(AP(pst, cc * 128, [[1, 128]]),
                                                  stg.ap(tt, (c0 + cc) * 128, [[1, 128]]), idf)
                    return ins
                P.op("pe", mm, reads=[stg.k(tt), ("identf",)], writes=[("ps", b)])
                dst = self.xT.ap(0, c0 * S + tt * 128, [[S, 4], [1, 128]])
                srcp = AP(pst, 0, [[128, 4], [1, 128]])
                P.op("act", (lambda dst=dst, srcp=srcp: nc.scalar.copy(out=dst, in_=srcp)),
                     reads=[("ps", b)], writes=[self.xk(c0 + cc, tt // 4) for cc in range(4)])

    def store_x(self, s_):
        nc, P = self.nc, self.P
        stg = Buf(self, "ostg", D, F32, 2)
        idf = self.identf.ap(0, 0, [[1, 128]])
        for tt in range(S // 128):
            for c0 in range(0, NCH, 4):
                b = self.bank()
                pst = self.ps[b]

                def mm(pst=pst, c0=c0, tt=tt):
                    ins = None
                    for cc in range(4):
                        ins = nc.tensor.transpose(AP(pst, cc * 128, [[1, 128]]),
                                                  self.xT.ap(0, (c0 + cc) * S + tt * 128, [[1, 128]]), idf)
                    return ins
                P.op("pe", mm, reads=[self.xk(c0 + cc, tt // 4) for cc in range(4)] + [("identf",)],
                     writes=[("ps", b)])
                dst = stg.ap(tt, c0 * 128, [[1, 512]])
                srcp = AP(pst, 0, [[1, 512]])
                P.op("act", (lambda dst=dst, srcp=srcp: nc.scalar.copy(out=dst, in_=srcp)),
                     reads=[("ps", b)], writes=[stg.k(tt)])
            dstd = bass.AP(self.out, (s_ * S + tt * 128) * D, [[D, 128], [1, D]])
            P.op("pool", (lambda tt=tt, dstd=dstd: nc.gpsimd.dma_start(out=dstd, in_=stg.ap(tt, 0, [[1, D]]))),
                 reads=[stg.k(tt)], writes=[("outd", s_, tt)], dma=True)

    def cast(self, dname, didx, ddims, wname, wbase, sdims):
        nc = self.nc
        dh = self.wb[dname]
        per = 1
        for s_ in dh.shape[1:]:
            per *= s_
        dst = bass.AP(dh, didx * per, [list(x) for x in ddims])
        src = bass.AP(self.w[wname], wbase, [list(x) for x in sdims])
        self.P.op("pool", (lambda: nc.gpsimd.dma_start(out=dst, in_=src)), writes=[("wb", dname, didx)], dma=True)

    def precast_conf(self, li):
        for q4 in range(4):
            pass
        nc = self.nc
        dh = self.wb["cwin"]
        for q4 in range(4):
            dst = bass.AP(dh, li * 128 * NCH * 2 * D + q4 * 512, [[NCH * 2 * D, 128], [2 * D, NCH], [1, 512]])
            src = bass.AP(self.w["cm_w_in"], li * D * 2 * D + q4 * 512, [[2 * D, 128], [128 * 2 * D, NCH], [1, 512]])
            self.P.op("pool", (lambda dst=dst, src=src: nc.gpsimd.dma_start(out=dst, in_=src)),
                      writes=[("wb", "cwin", li, q4)], dma=True)
        dh = self.wb["cwout"]
        for q2 in range(2):
            dst = bass.AP(dh, li * 128 * NCH * D + q2 * 512, [[NCH * D, 128], [D, NCH], [1, 512]])
            src = bass.AP(self.w["cm_w_out"], li * D * D + q2 * 512, [[D, 128], [128 * D, NCH], [1, 512]])
            self.P.op("pool", (lambda dst=dst, src=src: nc.gpsimd.dma_start(out=dst, in_=src)),
                      writes=[("wb", "cwout", li, q2)], dma=True)

    def precast_ffn(self, li):
        nc = self.nc
        dh = self.wb["fwin"]
        for j in range(NFF):
            for half in range(2):
                dst = bass.AP(dh, (li * NFF + j) * 128 * NCH * 256 + half * 128,
                              [[NCH * 256, 128], [256, NCH], [1, 128]])
                src = bass.AP(self.w["ffn_w_in"], li * D * 2 * DFF + half * DFF + j * 128,
                              [[2 * DFF, 128], [128 * 2 * DFF, NCH], [1, 128]])
                self.P.op("pool", (lambda dst=dst, src=src: nc.gpsimd.dma_start(out=dst, in_=src)),
                          writes=[("wb", "fwin", li, j, half)], dma=True)
        dh = self.wb["fwout"]
        for m in range(NCH):
            dst = bass.AP(dh, (li * NCH + m) * 128 * NFF * 128, [[NFF * 128, 128], [128, NFF], [1, 128]])
            src = bass.AP(self.w["ffn_w_out"], li * DFF * D + m * 128, [[D, 128], [128 * D, NFF], [1, 128]])
            self.P.op("pool", (lambda dst=dst, src=src: nc.gpsimd.dma_start(out=dst, in_=src)),
                      writes=[("wb", "fwout", li, m)], dma=True)

    def precast_proj(self, dname, base_idx, n, wname, wbase, rowstride, col0s):
        nc = self.nc
        dh = self.wb[dname]
        for i in range(n):
            dst = bass.AP(dh, (base_idx + i) * 128 * NCH * 128, [[NCH * 128, 128], [128, NCH], [1, 128]])
            src = bass.AP(self.w[wname], wbase + col0s[i], [[rowstride, 128], [128 * rowstride, NCH], [1, 128]])
            self.P.op("pool", (lambda dst=dst, src=src: nc.gpsimd.dma_start(out=dst, in_=src)),
                      writes=[("wb", dname, base_idx + i)], dma=True)

    def precast_wo(self, lj):
        nc = self.nc
        dh = self.wb["wo"]
        for q2 in range(2):
            dst = bass.AP(dh, lj * 128 * NCH * D + q2 * 512, [[NCH * D, 128], [D, NCH], [1, 512]])
            src = bass.AP(self.w["w_o"], lj * D * D + q2 * 512, [[D, 128], [128 * D, NCH], [1, 512]])
            self.P.op("pool", (lambda dst=dst, src=src: nc.gpsimd.dma_start(out=dst, in_=src)),
                      writes=[("wb", "wo", lj, q2)], dma=True)

    def precast_all(self):
        kvcols = [gh * 128 for gh in range(24)] + [3072 + gh * 128 for gh in range(24)]
        self.precast_conf(0)
        self.precast_ffn(0)
        self.precast_conf(1)
        self.precast_ffn(1)
        self.precast_proj("wkv", 0, 48, "w_kv", 0, 6144, kvcols)
        for lj in range(2):
            self.precast_proj("wq", lj * 24, 24, "w_q", lj * D * 3072, 3072, [gh * 128 for gh in range(24)])
            self.precast_wo(lj)
            self.precast_ffn(2 + lj)

    def wload(self, dst_ap, key, dname, didx, off, n, rkeys):
        nc = self.nc
        dh = self.wb[dname]
        per = 1
        for s_ in dh.shape[1:]:
            per *= s_
        rowlen = per // 128
        src = bass.AP(dh, didx * per + off, [[rowlen, 128], [1, n]])
        self.P.op("sp", (lambda: nc.sync.dma_start(out=dst_ap, in_=src)), reads=rkeys, writes=[key], dma=True)

    def rms_rstd(self, src, srck, rstd_ap, rstd_key, sq, n=512):
        nc, P = self.nc, self.P
        b = self.bank()
        pst = AP(self.ps[b], 0, [[1, n]])
        for c in range(NCH):
            i = self.uid
            self.uid += 1
            sa = sq.ap(i, 0, [[1, n]])
            P.op("act", (lambda sa=sa, c=c: nc.scalar.activation(out=sa, in_=src(c), func=AF.Square)),
                 reads=[srck(c)], writes=[sq.k(i)])
            P.op("pe", (lambda sa=sa, c=c: nc.tensor.matmul(pst, lhsT=self.meanb.ap(0, 0, [[1, 128]]), rhs=sa,
                                                            start=(c == 0), stop=(c == NCH - 1))),
                 reads=[sq.k(i), ("meanb",)], writes=[("ps", b)])
        P.op("act", (lambda: nc.scalar.activation(out=rstd_ap, in_=pst, func=AF.Sqrt, bias=self.epsap(), scale=1.0)),
             reads=[("ps", b), ("eps",)], writes=[rstd_key])
        P.op("dve", (lambda: nc.vector.reciprocal(out=rstd_ap, in_=rstd_ap)),
             reads=[rstd_key], writes=[rstd_key])

    def pre_norm(self, gcol, t0, hT, hi, sq, rstd, n=512, hoff=0, hstride=None, slot=0):
        nc, P = self.nc, self.P
        if hstride is None:
            hstride = n
        tile = t0 // 512
        ri = self.uid
        self.uid += 1
        self.rms_rstd(lambda c: self.xap(c, t0, n), lambda c: self.xk(c, tile),
                      rstd.ap(ri, 0, [[1, n]]), rstd.k(ri), sq, n)
        for c in range(NCH):
            P.op("dve", (lambda c=c: nc.vector.scalar_tensor_tensor(
                out=hT.ap(hi, hoff + c * hstride, [[1, n]]), in0=self.xap(c, t0, n), scalar=self.pvap(gcol + c),
                in1=rstd.ap(ri, 0, [[1, n]]), op0=ALU.mult, op1=ALU.mult)),
                reads=[self.xk(c, tile), ("pv",), rstd.k(ri)], writes=[hT.k(hi, c, slot)])

    def post_norm_residual(self, gcol, t0, yt, sq, rstd, tmp, n=512):
        nc, P = self.nc, self.P
        tile = t0 // 512
        ri = self.uid
        self.uid += 1
        self.rms_rstd(lambda c: yt.ap(0, c * n, [[1, n]]), lambda c: yt.k(0, c),
                      rstd.ap(ri, 0, [[1, n]]), rstd.k(ri), sq, n)
        for c in range(NCH):
            ti = self.uid
            self.uid += 1
            P.op("dve", (lambda c=c, ti=ti: nc.vector.scalar_tensor_tensor(
                out=tmp.ap(ti, 0, [[1, n]]), in0=yt.ap(0, c * n, [[1, n]]), scalar=self.pvap(gcol + c),
                in1=rstd.ap(ri, 0, [[1, n]]), op0=ALU.mult, op1=ALU.mult)),
                reads=[yt.k(0, c), ("pv",), rstd.k(ri)], writes=[tmp.k(ti)])
            P.op("dve", (lambda c=c, ti=ti: nc.vector.tensor_tensor(
                out=self.xap(c, t0, n), in0=self.xap(c, t0, n), in1=tmp.ap(ti, 0, [[1, n]]), op=ALU.add)),
                reads=[tmp.k(ti), self.xk(c, tile)], writes=[self.xk(c, tile)])

    def outproj_postnorm(self, mmf, evf, gcol, tcol, yt, sq, rstd, tmp, add_eng="pool"):
        nc, P = self.nc, self.P
        bS = self.bank()
        self.held = {bS}
        pst = AP(self.ps[bS], 0, [[1, 512]])
        ri = self.uid
        self.uid += 1
        sqs = []

        def stat(m):
            sa, sk = sqs[m]
            P.op("pe", (lambda: nc.tensor.matmul(pst, lhsT=self.meanb.ap(0, 0, [[1, 128]]), rhs=sa,
                                                 start=(m == 0), stop=(m == NCH - 1))),
                 reads=[sk, ("meanb",)], writes=[("ps", bS)])
        for m in range(NCH):
            b = self.bank()
            fn, reads = mmf(m, b)
            P.op("pe", fn, reads=reads, writes=[("ps", b)])
            if m > 0:
                stat(m - 1)
            efn, ereads = evf(m, b)
            P.op("act", efn, reads=[("ps", b)] + ereads, writes=[yt.k(0, m)])
            qi = self.uid
            self.uid += 1
            sa = sq.ap(qi, 0, [[1, 512]])
            sqs.append((sa, sq.k(qi)))
            P.op("act", (lambda m=m, sa=sa: nc.scalar.activation(
                out=sa, in_=yt.ap(0, m * 512, [[1, 512]]), func=AF.Square)),
                reads=[yt.k(0, m)], writes=[sq.k(qi)])
        stat(NCH - 1)
        self.held = set()
        ra = rstd.ap(ri, 0, [[1, 512]])
        P.op("act", (lambda: nc.scalar.activation(out=ra, in_=pst, func=AF.Sqrt, bias=self.epsap(), scale=1.0)),
             reads=[("ps", bS), ("eps",)], writes=[rstd.k(ri)])
        P.op("dve", (lambda: nc.vector.reciprocal(out=ra, in_=ra)), reads=[rstd.k(ri)], writes=[rstd.k(ri)])
        tile = tcol // 512
        for c in range(NCH):
            ti = self.uid
            self.uid += 1
            P.op("dve", (lambda c=c, ti=ti: nc.vector.scalar_tensor_tensor(
                out=tmp.ap(ti, 0, [[1, 512]]), in0=yt.ap(0, c * 512, [[1, 512]]), scalar=self.pvap(gcol + c),
                in1=ra, op0=ALU.mult, op1=ALU.mult)),
                reads=[yt.k(0, c), ("pv",), rstd.k(ri)], writes=[tmp.k(ti)])
            eobj = nc.gpsimd if add_eng == "pool" else nc.vector
            P.op(add_eng, (lambda c=c, ti=ti, eobj=eobj: eobj.tensor_tensor(
                out=self.xap(c, tcol), in0=self.xap(c, tcol), in1=tmp.ap(ti, 0, [[1, 512]]), op=ALU.add)),
                reads=[tmp.k(ti), self.xk(c, tile)], writes=[self.xk(c, tile)])

    def conformer(self, li):
        nc, P = self.nc, self.P
        self.phase()
        GL = 30 + S
        glu = Buf(self, "glu", NCH * GL, BF16)
        mark = self.aoff
        win = Buf(self, "cwin", NCH * 2 * D, BF16)
        hT = Buf(self, "chT", NCH * 512, BF16, 2)
        sq = Buf(self, "csq", 512, BF16, 2)
        rstd = Buf(self, "crstd", 512, F32, 2)
        sig = Buf(self, "csig", 512, F32, 2)
        for q4 in range(4):
            self.wload(win.ap(0, q4 * 4096, [[1, 4096]]), win.k(0, q4), "cwin", li, q4 * 4096, 4096,
                       [("wb", "cwin", li, qq) for qq in range(4)])
        for c in range(NCH):
            P.op("dve", (lambda c=c: nc.vector.memset(glu.ap(0, c * GL, [[1, 30]]), 0.0)), writes=[glu.k(0, c, "z")])
        bcol = PVC[("cm_b_in", li)]
        for tile in range(4):
            t0 = tile * 512
            self.pre_norm(PVC[("mix_pre_g", li)], t0, hT, tile, sq, rstd)
            for m in range(NCH):
                ba, bg = self.bank(), self.bank()
                for (b, oc) in ((ba, m), (bg, m + 8)):
                    def mm(b=b, oc=oc, tile=tile):
                        ins = None
                        for kc in range(NCH):
                            ins = nc.tensor.matmul(AP(self.ps[b], 0, [[1, 512]]),
                                                   lhsT=win.ap(0, kc * 2 * D + oc * 128, [[1, 128]]),
                                                   rhs=hT.ap(tile, kc * 512, [[1, 512]]),
                                                   start=(kc == 0), stop=(kc == NCH - 1))
                        return ins
                    P.op("pe", mm, reads=[win.k(0, qq) for qq in range(4)] + [hT.k(tile, kc, 0) for kc in range(NCH)],
                         writes=[("ps", b)])
                si = self.uid
                self.uid += 1
                P.op("act", (lambda bg=bg, si=si, m=m: nc.scalar.activation(
                    out=sig.ap(si, 0, [[1, 512]]), in_=AP(self.ps[bg], 0, [[1, 512]]), func=AF.Sigmoid,
                    bias=self.pvap(bcol + 8 + m), scale=1.0)),
                    reads=[("ps", bg), ("pv",)], writes=[sig.k(si)])
                P.op("dve", (lambda ba=ba, si=si, m=m, t0=t0: nc.vector.scalar_tensor_tensor(
                    out=glu.ap(0, m * GL + 30 + t0, [[1, 512]]), in0=AP(self.ps[ba], 0, [[1, 512]]),
                    scalar=self.pvap(bcol + m), in1=sig.ap(si, 0, [[1, 512]]), op0=ALU.add, op1=ALU.mult)),
                    reads=[("ps", ba), ("pv",), sig.k(si)], writes=[glu.k(0, m, tile)])
        self.aoff = mark
        wout = Buf(self, "cwout", NCH * D, BF16)
        diag = Buf(self, "cdiag", 31 * 128, BF16, 2)
        vt = Buf(self, "cvt", NCH * 512, F32)
        vb = Buf(self, "cvb", 512, BF16, 2)
        sq = Buf(self, "csq2", 512, BF16, 2)
        st4 = Buf(self, "cst4", 512, F32, 4)
        sT = Buf(self, "csT", NCH * 512, BF16)
        yt = Buf(self, "cyt", NCH * 512, F32)
        rstd = Buf(self, "crstd2", 512, F32, 2)
        tmp = Buf(self, "ctmp", 512, F32, 2)
        for q2 in range(2):
            self.wload(wout.ap(0, q2 * 4096, [[1, 4096]]), wout.k(0, q2), "cwout", li, q2 * 4096, 4096,
                       [("wb", "cwout", li, qq) for qq in range(2)])
        dwc = PVC[("cm_dw", li)]
        idb = self.identb
        for tile in range(4):
            t0 = tile * 512
            bM, bQ = self.bank(), self.bank()
            self.held = {bM, bQ}
            for c in range(NCH):
                di = tile * NCH + c
                P.op("dve", (lambda c=c, di=di: nc.vector.tensor_tensor(
                    out=diag.ap(di, 0, [[128, 31], [1, 128]]), in0=idb.ap(0, 0, [[0, 31], [1, 128]]),
                    in1=self.pv.ap(0, dwc + c * 31, [[1, 31], [0, 128]]), op=ALU.mult)),
                    reads=[("identb",), ("pv",)], writes=[diag.k(di)])
                b = self.bank()

                def mm(c=c, di=di, b=b, t0=t0):
                    ins = None
                    for k in range(31):
                        ins = nc.tensor.matmul(AP(self.ps[b], 0, [[1, 512]]),
                                               lhsT=diag.ap(di, k * 128, [[1, 128]]),
                                               rhs=glu.ap(0, c * GL + t0 + k, [[1, 512]]),
                                               start=(k == 0), stop=(k == 30))
                    return ins
                rk = [glu.k(0, c, tile), glu.k(0, c, "z")] + ([glu.k(0, c, tile - 1)] if tile > 0 else [])
                P.op("pe", mm, reads=rk + [diag.k(di)], writes=[("ps", b)])
                pb = AP(self.ps[b], 0, [[1, 512]])
                bias = self.pvap(PVC[("cm_dw_b", li)] + c)
                vi = self.uid
                self.uid += 1
                P.op("act", (lambda c=c, pb=pb, bias=bias: nc.scalar.activation(
                    out=vt.ap(0, c * 512, [[1, 512]]), in_=pb, func=AF.Identity, bias=bias, scale=1.0)),
                    reads=[("ps", b), ("pv",)], writes=[vt.k(0, c)])
                P.op("act", (lambda vi=vi, pb=pb, bias=bias: nc.scalar.activation(
                    out=vb.ap(vi, 0, [[1, 512]]), in_=pb, func=AF.Identity, bias=bias, scale=1.0)),
                    reads=[("ps", b), ("pv",)], writes=[vb.k(vi)])
                P.op("act", (lambda vi=vi, pb=pb, bias=bias: nc.scalar.activation(
                    out=sq.ap(vi, 0, [[1, 512]]), in_=pb, func=AF.Square, bias=bias, scale=1.0)),
                    reads=[("ps", b), ("pv",)], writes=[sq.k(vi)])
                mb = self.meanb.ap(0, 0, [[1, 128]])
                P.op("pe", (lambda vi=vi, c=c, bM=bM: nc.tensor.matmul(
                    AP(self.ps[bM], 0, [[1, 512]]), lhsT=mb, rhs=vb.ap(vi, 0, [[1, 512]]),
                    start=(c == 0), stop=(c == NCH - 1))), reads=[vb.k(vi), ("meanb",)], writes=[("ps", bM)])
                P.op("pe", (lambda vi=vi, c=c, bQ=bQ: nc.tensor.matmul(
                    AP(self.ps[bQ], 0, [[1, 512]]), lhsT=mb, rhs=sq.ap(vi, 0, [[1, 512]]),
                    start=(c == 0), stop=(c == NCH - 1))), reads=[sq.k(vi), ("meanb",)], writes=[("ps", bQ)])
            self.held = set()
            mu, var, rs = (st4.ap(j, 0, [[1, 512]]) for j in range(3))
            pM, pQ = AP(self.ps[bM], 0, [[1, 512]]), AP(self.ps[bQ], 0, [[1, 512]])
            P.op("dve", (lambda mu=mu, pM=pM: nc.vector.tensor_copy(out=mu, in_=pM)), reads=[("ps", bM)], writes=[st4.k(0)])
            P.op("dve", (lambda mu=mu, var=var: nc.vector.tensor_tensor(out=var, in0=mu, in1=mu, op=ALU.mult)),
                 reads=[st4.k(0)], writes=[st4.k(1)])
            P.op("dve", (lambda pQ=pQ, var=var: nc.vector.tensor_tensor(out=var, in0=pQ, in1=var, op=ALU.subtract)),
                 reads=[("ps", bQ), st4.k(1)], writes=[st4.k(1)])
            P.op("act", (lambda rs=rs, var=var: nc.scalar.activation(out=rs, in_=var, func=AF.Sqrt, bias=self.epsap(), scale=1.0)),
                 reads=[st4.k(1), ("eps",)], writes=[st4.k(2)])
            P.op("dve", (lambda rs=rs: nc.vector.reciprocal(out=rs, in_=rs)),
                 reads=[st4.k(2)], writes=[st4.k(2)])
            for c in range(NCH):
                va = vt.ap(0, c * 512, [[1, 512]])
                P.op("dve", (lambda va=va, mu=mu: nc.vector.tensor_tensor(out=va, in0=va, in1=mu, op=ALU.subtract)),
                     reads=[vt.k(0, c), st4.k(0)], writes=[vt.k(0, c)])
                P.op("dve", (lambda va=va, rs=rs: nc.vector.tensor_tensor(out=va, in0=va, in1=rs, op=ALU.mult)),
                     reads=[vt.k(0, c), st4.k(2)], writes=[vt.k(0, c)])
                P.op("act", (lambda va=va, c=c: nc.scalar.activation(
                    out=sT.ap(0, c * 512, [[1, 512]]), in_=va, func=AF.Silu,
                    bias=self.pvap(PVC[("cm_ln_b", li)] + c), scale=self.pvap(PVC[("cm_ln_g", li)] + c))),
                    reads=[vt.k(0, c), ("pv",)], writes=[sT.k(0, c)])
            def mmf(m, b):
                def mm():
                    ins = None
                    for kc in range(NCH):
                        ins = nc.tensor.matmul(AP(self.ps[b], 0, [[1, 512]]),
                                               lhsT=wout.ap(0, kc * D + m * 128, [[1, 128]]),
                                               rhs=sT.ap(0, kc * 512, [[1, 512]]),
                                               start=(kc == 0), stop=(kc == NCH - 1))
                    return ins
                return mm, [wout.k(0, 0), wout.k(0, 1)] + [sT.k(0, kc) for kc in range(NCH)]

            def evf(m, b):
                return (lambda: nc.scalar.activation(
                    out=yt.ap(0, m * 512, [[1, 512]]), in_=AP(self.ps[b], 0, [[1, 512]]), func=AF.Identity,
                    bias=self.pvap(PVC[("cm_b_out", li)] + m), scale=1.0)), [("pv",)]
            self.outproj_postnorm(mmf, evf, PVC[("mix_post_g", li)], t0, yt, sq, rstd, tmp, add_eng="dve")

    def ffn(self, li):
        nc, P = self.nc, self.P
        self.phase()
        TT = 1024
        SG = TT + 4
        hT = Buf(self, "fhT", NCH * TT, BF16)
        win = Buf(self, "fwin", NCH * 256, BF16, 3)
        stg = Buf(self, "fstg", 2 * SG, BF16, 2)
        acc = Buf(self, "facc", 2 * TT, F32, 2)
        gated = Buf(self, "fgated", NFF * TT, BF16)
        wout = Buf(self, "fwout", NFF * 128, BF16, 2)
        yt = Buf(self, "fyt", NCH * 512, F32)
        sq = Buf(self, "fsq", 512, BF16, 2)
        rstd = Buf(self, "frstd", 512, F32, 2)
        tmp = Buf(self, "ftmp", 512, F32, 2)
        P.op("dve", (lambda: nc.vector.memset(self.halo.ap(0, 0, [[1, 88]]), 0.0)),
             writes=[("halo", j) for j in range(44)])
        dwc, dbc = PVC[("ffn_dw", li)], PVC[("ffn_dw_b", li)]

        def load_win(wi):
            j = wi % NFF
            self.wload(win.ap(wi, 0, [[1, NCH * 256]]), win.k(wi), "fwin", li * NFF + j, 0, NCH * 256,
                       [("wb", "fwin", li, j, 0), ("wb", "fwin", li, j, 1)])

        def load_wout(wo):
            m = wo % NCH
            self.wload(wout.ap(wo, 0, [[1, NFF * 128]]), wout.k(wo), "fwout", li * NCH + m, 0, NFF * 128,
                       [("wb", "fwout", li, m)])
        nwin = 2 * NFF
        load_win(0)
        load_win(1)

        def prenorm(hs):
            for sub in range(2):
                self.pre_norm(PVC[("ffn_pre_g", li)], hs * TT + sub * 512, hT, 0, sq, rstd, hoff=sub * 512,
                              hstride=TT, slot=sub)

        def stageA(hs, j):
            wi = hs * NFF + j
            si = wi
            for half in range(2):
                ch = j + half * NFF
                so = half * SG
                P.op("dve", (lambda so=so, ch=ch: nc.vector.tensor_copy(
                    out=stg.ap(si, so, [[1, 2]]), in_=self.halo.ap(0, ch * 2, [[1, 2]]))),
                    reads=[("halo", ch)], writes=[stg.k(si, half, "h")])
                for sub in range(2):
                    b = self.bank()

                    def mm(b=b, half=half, sub=sub):
                        ins = None
                        for kc in range(NCH):
                            ins = nc.tensor.matmul(AP(self.ps[b], 0, [[1, 512]]),
                                                   lhsT=win.ap(wi, kc * 256 + half * 128, [[1, 128]]),
                                                   rhs=hT.ap(0, kc * TT + sub * 512, [[1, 512]]),
                                                   start=(kc == 0), stop=(kc == NCH - 1))
                        return ins
                    P.op("pe", mm, reads=[win.k(wi)] + [hT.k(0, kc, sub) for kc in range(NCH)],
                         writes=[("ps", b)])
                    P.op("act", (lambda b=b, so=so, sub=sub: nc.scalar.copy(
                        out=stg.ap(si, so + 2 + sub * 512, [[1, 512]]), in_=AP(self.ps[b], 0, [[1, 512]]))),
                        reads=[("ps", b)], writes=[stg.k(si, half, sub)])
            for half in range(2):
                ch = j + half * NFF
                so = half * SG
                P.op("dve", (lambda so=so, ch=ch: nc.vector.tensor_copy(
                    out=self.halo.ap(0, ch * 2, [[1, 2]]), in_=stg.ap(si, so + TT, [[1, 2]]))),
                    reads=[stg.k(si, half, 1)], writes=[("halo", ch)])
                aa = acc.ap(si, half * TT, [[1, TT]])
                rk = [stg.k(si, half, 0), stg.k(si, half, 1), stg.k(si, half, "h"), ("pv",)]
                P.op("act", (lambda so=so, ch=ch, aa=aa: nc.scalar.activation(
                    out=aa, in_=stg.ap(si, so, [[1, TT]]), func=AF.Identity,
                    bias=self.pvap(dbc + ch), scale=self.pvap(dwc + ch * 3))),
                    reads=rk, writes=[acc.k(si, half)])
                for k in (1, 2):
                    P.op("dve", (lambda so=so, ch=ch, aa=aa, k=k: nc.vector.scalar_tensor_tensor(
                        out=aa, in0=stg.ap(si, so + k, [[1, TT]]), scalar=self.pvap(dwc + ch * 3 + k),
                        in1=aa, op0=ALU.mult, op1=ALU.add)),
                        reads=rk + [acc.k(si, half)], writes=[acc.k(si, half)])

        def stageB(hs, j):
            si = hs * NFF + j
            ag = acc.ap(si, TT, [[1, TT]])
            P.op("act", (lambda: nc.scalar.activation(out=ag, in_=ag, func=AF.Silu)),
                 reads=[acc.k(si, 1)], writes=[acc.k(si, 1)])
            P.op("dve", (lambda: nc.vector.tensor_tensor(
                out=gated.ap(0, j * TT, [[1, TT]]), in0=acc.ap(si, 0, [[1, TT]]), in1=ag, op=ALU.mult)),
                reads=[acc.k(si, 0), acc.k(si, 1)], writes=[gated.k(0, j)])

        prenorm(0)
        for hs in range(2):
            t0 = hs * TT
            for j in range(NFF):
                wi = hs * NFF + j
                if wi + 2 < nwin:
                    load_win(wi + 2)
                if j == NFF - 1:
                    load_wout(hs * 2 * NCH)
                stageA(hs, j)
                if j > 0:
                    stageB(hs, j - 1)
            stageB(hs, NFF - 1)
            if hs == 0:
                prenorm(1)
            gcol = PVC[("ffn_post_g", li)]
            for sub in range(2):
                tcol = t0 + sub * 512
                bS = self.bank()
                self.held = {bS}
                pst = AP(self.ps[bS], 0, [[1, 512]])
                ri = self.uid
                self.uid += 1
                sqs = []

                def stat(m, bS=bS, pst=pst):
                    sa, sk = sqs[m]
                    P.op("pe", (lambda: nc.tensor.matmul(pst, lhsT=self.meanb.ap(0, 0, [[1, 128]]), rhs=sa,
                                                         start=(m == 0), stop=(m == NCH - 1))),
                         reads=[sk, ("meanb",)], writes=[("ps", bS)])
                for m in range(NCH):
                    wo = (hs * 2 + sub) * NCH + m
                    if not (sub == 1 and m == NCH - 1):
                        load_wout(wo + 1)
                    b = self.bank()

                    def mm(b=b, wo=wo, sub=sub):
                        ins = None
                        for kc in range(NFF):
                            ins = nc.tensor.matmul(AP(self.ps[b], 0, [[1, 512]]),
                                                   lhsT=wout.ap(wo, kc * 128, [[1, 128]]),
                                                   rhs=gated.ap(0, kc * TT + sub * 512, [[1, 512]]),
                                                   start=(kc == 0), stop=(kc == NFF - 1))
                        return ins
                    P.op("pe", mm, reads=[wout.k(wo)] + [gated.k(0, kc) for kc in range(NFF)], writes=[("ps", b)])
                    if m > 0:
                        stat(m - 1)
                    P.op("act", (lambda m=m, b=b: nc.scalar.copy(
                        out=yt.ap(0, m * 512, [[1, 512]]), in_=AP(self.ps[b], 0, [[1, 512]]))),
                        reads=[("ps", b)], writes=[yt.k(0, m)])
                    qi = self.uid
                    self.uid += 1
                    sa = sq.ap(qi, 0, [[1, 512]])
                    sqs.append((sa, sq.k(qi)))
                    P.op("act", (lambda m=m, sa=sa: nc.scalar.activation(
                        out=sa, in_=yt.ap(0, m * 512, [[1, 512]]), func=AF.Square)),
                        reads=[yt.k(0, m)], writes=[sq.k(qi)])
                stat(NCH - 1)
                self.held = set()
                ra = rstd.ap(ri, 0, [[1, 512]])
                P.op("act", (lambda ra=ra, pst=pst: nc.scalar.activation(out=ra, in_=pst, func=AF.Sqrt,
                                                                         bias=self.epsap(), scale=1.0)),
                     reads=[("ps", bS), ("eps",)], writes=[rstd.k(ri)])
                P.op("dve", (lambda ra=ra: nc.vector.reciprocal(out=ra, in_=ra)), reads=[rstd.k(ri)], writes=[rstd.k(ri)])
                tile = tcol // 512
                for c in range(NCH):
                    ti = self.uid
                    self.uid += 1
                    P.op("dve", (lambda c=c, ti=ti, ra=ra: nc.vector.scalar_tensor_tensor(
                        out=tmp.ap(ti, 0, [[1, 512]]), in0=yt.ap(0, c * 512, [[1, 512]]), scalar=self.pvap(gcol + c),
                        in1=ra, op0=ALU.mult, op1=ALU.mult)),
                        reads=[yt.k(0, c), ("pv",), rstd.k(ri)], writes=[tmp.k(ti)])
                    P.op("pool", (lambda c=c, ti=ti, tcol=tcol: nc.gpsimd.tensor_tensor(
                        out=self.xap(c, tcol), in0=self.xap(c, tcol), in1=tmp.ap(ti, 0, [[1, 512]]), op=ALU.add)),
                        reads=[tmp.k(ti), self.xk(c, tile)], writes=[self.xk(c, tile)])

    def project(self, gcol, dname, base_idx, outs):
        nc, P = self.nc, self.P
        self.phase()
        hT = Buf(self, "phT", NCH * S, BF16)
        sq = Buf(self, "psq", 512, BF16, 2)
        rstd = Buf(self, "prstd", 512, F32, 2)
        wp = Buf(self, "pw", NCH * 128, BF16, 3)
        stg = Buf(self, "pstg", S, BF16, 2)

        def load(oi):
            self.wload(wp.ap(oi, 0, [[1, NCH * 128]]), wp.k(oi), dname, base_idx + oi, 0, NCH * 128,
                       [("wb", dname, base_idx + oi)])
        load(0)
        load(1)
        self.pre_norm(gcol, 0, hT, 0, sq, rstd, hoff=0, hstride=S, slot=0)
        self.pre_norm(gcol, 512, hT, 0, sq, rstd, hoff=512, hstride=S, slot=1)
        for oi, (sname, idx) in enumerate(outs):
            if oi + 2 < len(outs):
                load(oi + 2)
            for tile in range(4):
                if oi == 0 and tile < 2:
                    self.pre_norm(gcol, (tile + 2) * 512, hT, 0, sq, rstd, hoff=(tile + 2) * 512, hstride=S,
                                  slot=tile + 2)
                b = self.bank()

                def mm(b=b, oi=oi, tile=tile):
                    ins = None
                    for kc in range(NCH):
                        ins = nc.tensor.matmul(AP(self.ps[b], 0, [[1, 512]]),
                                               lhsT=wp.ap(oi, kc * 128, [[1, 128]]),
                                               rhs=hT.ap(0, kc * S + tile * 512, [[1, 512]]),
                                               start=(kc == 0), stop=(kc == NCH - 1))
                    return ins
                P.op("pe", mm, reads=[wp.k(oi)] + [hT.k(0, kc, tile) for kc in range(NCH)], writes=[("ps", b)])
                P.op("act", (lambda b=b, oi=oi, tile=tile: nc.scalar.copy(
                    out=stg.ap(oi, tile * 512, [[1, 512]]), in_=AP(self.ps[b], 0, [[1, 512]]))),
                    reads=[("ps", b)], writes=[stg.k(oi, tile)])
            dst = bass.AP(self.scr[sname], idx * 128 * S, [[S, 128], [1, S]])
            P.op("pool", (lambda oi=oi, dst=dst: nc.gpsimd.dma_start(out=dst, in_=stg.ap(oi, 0, [[1, S]]))),
                 reads=[stg.k(oi, t) for t in range(4)], writes=[("scr", sname, idx)], dma=True)

    def attention(self, lj, li):
        nc, P = self.nc, self.P
        self.project(PVC[("mix_pre_g", li)], "wq", lj * 24, [("q", gh) for gh in range(24)])
        self.phase()
        mark = self.aoff
        OM01 = Buf(self, "aOM01", 4 * S, BF16)
        OM2 = Buf(self, "aOM2", 2 * S, BF16, 2)
        Dd01 = Buf(self, "aDd01", 2 * S, F32)
        Dd2 = Buf(self, "aDd2", S, F32, 2)
        tA = Buf(self, "atA", S, BF16)
        tE = Buf(self, "atE", S, F32)
        nacc = Buf(self, "anacc", S, F32)
        Dacc = Buf(self, "aDacc", S, F32)
        aS = Buf(self, "aaS", S, BF16, 2)
        qkv = Buf(self, "aqkv", 3 * S, BF16, 2)
        vtok = Buf(self, "avtok", 16 * 128, BF16)
        pbuf = Buf(self, "aP", 256, BF16, 4)
        ptb = Buf(self, "aPT", 256, BF16, 4)
        dg = Buf(self, "adg", 128, BF16, 6)
        st = Buf(self, "ast", 2, F32, 6)
        mbf = Buf(self, "ambf", 2, BF16, 6)
        idb = self.identb.ap(0, 0, [[1, 128]])
        ones = self.onesb.ap(0, 0, [[1, 128]])
        ghs = [(h, g) for h in range(NHEAD) for g in (2, 1, 0)]

        def load_qkv(ci):
            h, g = ghs[ci]
            gh = g * 8 + h
            for qi, nm in enumerate(("q", "k", "v")):
                src = bass.AP(self.scr[nm], gh * 128 * S, [[S, 128], [1, S]])
                P.op("sp", (lambda qi=qi, src=src, ci=ci: nc.sync.dma_start(out=qkv.ap(ci, qi * S, [[1, S]]), in_=src)),
                     reads=[("scr", nm, gh)], writes=[qkv.k(ci, qi)], dma=True)

        def wins(g, n):
            return [0, 1, 2, 3] if g == 2 else ([n] if g == 1 else [n // 4])
        load_qkv(0)
        bi = 0
        pending = []
        sbank = [0]
        xbank = [0]
        mcount = [0]
        for ci, (h, g) in enumerate(ghs):
            if ci + 1 < len(ghs):
                load_qkv(ci + 1)
            r = DIL[g]
            nb = (S // r) // 128
            blocks = [(rho, n) for rho in range(r) for n in range(nb)]
            for hb in range(2):
                pbk = self.psb[hb]

                def mmV(pbk=pbk, hb=hb, r=r, blocks=blocks, ci=ci):
                    ins = None
                    for sl in range(8):
                        rho, n = blocks[hb * 8 + sl]
                        ins = nc.tensor.transpose(AP(pbk, sl * 128, [[1, 128]]),
                                                  qkv.ap(ci, 2 * S + rho + r * 128 * n, [[r, 128]]), idb)
                    return ins
                P.op("pe", mmV, reads=[qkv.k(ci, 2), ("identb",)], writes=[("psb", hb)])
                P.op("act", (lambda pbk=pbk, hb=hb, ci=ci: nc.scalar.copy(
                    out=vtok.ap(ci, hb * 1024, [[1, 1024]]), in_=AP(pbk, 0, [[1, 1024]]))),
                    reads=[("psb", hb)], writes=[vtok.k(ci, hb)])

            def stageA(bidx, bi, ci=ci, r=r, blocks=blocks):
                rho, n = blocks[bidx]
                nk = 256 if n > 0 else 128
                koff = rho + r * 128 * (n - 1 if n > 0 else 0)
                qap = qkv.ap(ci, rho + r * 128 * n, [[r, 128]])
                kap = qkv.ap(ci, S + koff, [[r, nk]])
                b = sbank[0] % 3
                sbank[0] += 1
                pS = AP(self.ps[b], 0, [[1, nk]])
                P.op("pe", (lambda: nc.tensor.matmul(pS, lhsT=qap, rhs=kap, start=True, stop=True)),
                     reads=[qkv.k(ci, 0), qkv.k(ci, 1)], writes=[("ps", b)])
                ng = st.ap(bi, 1, [[1, 1]])
                mb = mbf.ap(bi, 0, [[1, 1]])
                P.op("dve", (lambda: nc.vector.reduce_max(out=mb, in_=pS, axis=AX.X)),
                     reads=[("ps", b)], writes=[mbf.k(bi)])
                P.op("dve", (lambda: nc.vector.tensor_scalar(out=ng, in0=mb, scalar1=-SCALE, scalar2=None, op0=ALU.mult)),
                     reads=[mbf.k(bi)], writes=[st.k(bi, 1)])
                P.op("dve", (lambda: nc.vector.tensor_scalar(
                    out=dg.ap(bi, 0, [[1, 128]]), in0=idb, scalar1=ng, scalar2=-1.0 / SCALE,
                    op0=ALU.mult, op1=ALU.mult)),
                    reads=[st.k(bi, 1), ("identb",)], writes=[dg.k(bi)])
                P.op("act", (lambda: nc.scalar.activation(
                    out=pbuf.ap(bi, 0, [[1, nk]]), in_=pS, func=AF.Exp, bias=ng, scale=SCALE)),
                    reads=[("ps", b), st.k(bi, 1)], writes=[pbuf.k(bi)])

            def stageB(bidx, bi, blocks=blocks):
                rho, n = blocks[bidx]
                nkb = 2 if n > 0 else 1
                nk = nkb * 128
                pbk = self.psb[bi % 2]

                def mmT():
                    ins = None
                    for kb in range(nkb):
                        ins = nc.tensor.transpose(AP(pbk, kb * 128, [[1, 128]]),
                                                  pbuf.ap(bi, kb * 128, [[1, 128]]), idb)
                    return ins
                P.op("pe", mmT, reads=[pbuf.k(bi), ("identb",)], writes=[("psb", bi % 2)])
                P.op("dve", (lambda: nc.vector.tensor_tensor(
                    out=ptb.ap(bi, 0, [[1, nk]]), in0=AP(pbk, 0, [[1, nk]]),
                    in1=self.maskT.ap(0, 256 - nk, [[1, nk]]), op=ALU.mult)),
                    reads=[("psb", bi % 2), ("maskT",)], writes=[ptb.k(bi)])

            def stageC(bidx, bi, g=g, r=r, blocks=blocks, ci=ci, h=h):
                rho, n = blocks[bidx]
                nkb = 2 if n > 0 else 1
                kblocks = ([bidx - 1, bidx] if n > 0 else [bidx])
                bx = 3 + xbank[0] % 3
                xbank[0] += 1

                def mmO():
                    ins = None
                    for kb in range(nkb):
                        ins = nc.tensor.matmul(AP(self.ps[bx], 0, [[1, 128]]),
                                               lhsT=vtok.ap(ci, kblocks[kb] * 128, [[1, 128]]),
                                               rhs=ptb.ap(bi, kb * 128, [[1, 128]]),
                                               start=(kb == 0), stop=(kb == nkb - 1))
                    ins = nc.tensor.matmul(AP(self.ps[bx], 128, [[1, 128]]), lhsT=ones,
                                           rhs=dg.ap(bi, 0, [[1, 128]]), start=True, stop=True)
                    for kb in range(nkb):
                        ins = nc.tensor.matmul(AP(self.ps[bx], 256, [[1, 128]]), lhsT=ones,
                                               rhs=ptb.ap(bi, kb * 128, [[1, 128]]),
                                               start=(kb == 0), stop=(kb == nkb - 1))
                    return ins
                P.op("pe", mmO, reads=[vtok.k(ci, 0), vtok.k(ci, 1), ptb.k(bi), dg.k(bi), ("onesb",)], writes=[("ps", bx)])
                toff = rho + r * 128 * n
                ws = wins(g, n)
                if g == 2:
                    omap = OM2.ap(h, toff, [[S, 2], [r, 128]])
                    omk = [OM2.k(h, w) for w in ws]
                    dap = Dd2.ap(h, toff, [[r, 128]])
                    dk_ = [Dd2.k(h, w) for w in ws]
                else:
                    omap = OM01.ap(0, g * S + toff, [[2 * S, 2], [r, 128]])
                    omk = [OM01.k(0, g, w) for w in ws]
                    dap = Dd01.ap(0, g * S + toff, [[r, 128]])
                    dk_ = [Dd01.k(0, g, w) for w in ws]
                P.op("act", (lambda: nc.scalar.copy(out=omap, in_=AP(self.ps[bx], 0, [[128, 2], [1, 128]]))),
                     reads=[("ps", bx)], writes=omk)
                P.op("act", (lambda: nc.scalar.copy(out=dap, in_=AP(self.ps[bx], 256, [[1, 128]]))),
                     reads=[("ps", bx)], writes=dk_)

            def merge_ops(h=h):
                W4 = range(4)

                def om(kind, g_, w):
                    if g_ == 2:
                        return OM2.ap(h, kind * S + w * 512, [[1, 512]])
                    return OM01.ap(0, (kind * 2 + g_) * S + w * 512, [[1, 512]])

                def dd(g_, w):
                    if g_ == 2:
                        return Dd2.ap(h, w * 512, [[1, 512]])
                    return Dd01.ap(0, g_ * S + w * 512, [[1, 512]])

                def ok(g_, w):
                    return [OM2.k(h, w)] if g_ == 2 else [OM01.k(0, g_, w)]

                def dk(g_, w):
                    return [Dd2.k(h, w)] if g_ == 2 else [Dd01.k(0, g_, w)]

                def tAa(w):
                    return tA.ap(0, w * 512, [[1, 512]])

                def tEa(w):
                    return tE.ap(0, w * 512, [[1, 512]])

                def na(w):
                    return nacc.ap(0, w * 512, [[1, 512]])

                def da(w):
                    return Dacc.ap(0, w * 512, [[1, 512]])
                ops = []

                def add(ph, eng, fn, reads, writes, dma=False):
                    ops.append((ph, lambda: P.op(eng, fn, reads=reads, writes=writes, dma=dma)))
                for w in W4:
                    add(0, "dve", (lambda w=w: nc.vector.tensor_tensor(out=tAa(w), in0=om(1, 0, w), in1=om(1, 1, w), op=ALU.max)),
                        ok(0, w) + ok(1, w), [tA.k(0, w)])
                for w in W4:
                    add(0, "dve", (lambda w=w: nc.vector.tensor_tensor(out=tAa(w), in0=tAa(w), in1=om(1, 2, w), op=ALU.max)),
                        ok(2, w) + [tA.k(0, w)], [tA.k(0, w)])
                for oi_, g_ in enumerate((1, 0, 2)):
                    for w in W4:
                        add(oi_, "dve", (lambda g_=g_, w=w: nc.vector.tensor_tensor(
                            out=tEa(w), in0=om(1, g_, w), in1=tAa(w), op=ALU.subtract)),
                            ok(g_, w) + [tA.k(0, w)], [tE.k(0, w)])
                    for w in W4:
                        add(oi_, "act", (lambda w=w: nc.scalar.activation(out=tEa(w), in_=tEa(w), func=AF.Exp, scale=SCALE)),
                            [tE.k(0, w)], [tE.k(0, w)])
                    if oi_ == 0:
                        for w in W4:
                            add(oi_, "dve", (lambda g_=g_, w=w: nc.vector.tensor_tensor(
                                out=da(w), in0=dd(g_, w), in1=tEa(w), op=ALU.mult)),
                                dk(g_, w) + [tE.k(0, w)], [Dacc.k(0, w)])
                        for w in W4:
                            add(oi_, "dve", (lambda g_=g_, w=w: nc.vector.tensor_tensor(
                                out=na(w), in0=om(0, g_, w), in1=tEa(w), op=ALU.mult)),
                                ok(g_, w) + [tE.k(0, w)], [nacc.k(0, w)])
                    else:
                        for w in W4:
                            add(oi_, "dve", (lambda g_=g_, w=w: nc.vector.tensor_tensor(
                                out=dd(g_, w), in0=dd(g_, w), in1=tEa(w), op=ALU.mult)),
                                dk(g_, w) + [tE.k(0, w)], dk(g_, w))
                        for w in W4:
                            add(oi_, "dve", (lambda g_=g_, w=w: nc.vector.tensor_tensor(
                                out=tEa(w), in0=om(0, g_, w), in1=tEa(w), op=ALU.mult)),
                                ok(g_, w) + [tE.k(0, w)], [tE.k(0, w)])
                        for w in W4:
                            add(oi_, "dve", (lambda g_=g_, w=w: nc.vector.tensor_tensor(
                                out=da(w), in0=da(w), in1=dd(g_, w), op=ALU.add)),
                                dk(g_, w) + [Dacc.k(0, w)], [Dacc.k(0, w)])
                        for w in W4:
                            add(oi_, "dve", (lambda w=w: nc.vector.tensor_tensor(out=na(w), in0=na(w), in1=tEa(w), op=ALU.add)),
                                [nacc.k(0, w), tE.k(0, w)], [nacc.k(0, w)])
                for w in W4:
                    add(2, "act", (lambda w=w: nc.scalar.activation(out=da(w), in_=da(w), func=AF.Ln)),
                        [Dacc.k(0, w)], [Dacc.k(0, w)])
                for w in W4:
                    add(2, "act", (lambda w=w: nc.scalar.activation(out=da(w), in_=da(w), func=AF.Exp, scale=-1.0)),
                        [Dacc.k(0, w)], [Dacc.k(0, w)])
                for w in W4:
                    add(2, "dve", (lambda w=w: nc.vector.tensor_tensor(
                        out=aS.ap(h, w * 512, [[1, 512]]), in0=na(w), in1=da(w), op=ALU.mult)),
                        [Dacc.k(0, w), nacc.k(0, w)], [aS.k(h, w)])
                dst = bass.AP(self.scr_a, h * 128 * S, [[S, 128], [1, S]])
                add(2, "pool", (lambda: nc.gpsimd.dma_start(out=dst, in_=aS.ap(h, 0, [[1, S]]))),
                    [aS.k(h, w) for w in W4], [("scra", h)], dma=True)
                return ops
            nblk = len(blocks)
            phase_i = 2 - g
            mine = [e for (ph, e) in pending if ph == phase_i]
            per_it = -(-len(mine) // nblk)
            mi_ = 0
            LB, LC = 2, 4
            for it in range(nblk + LC):
                if it < nblk:
                    stageA(it, bi + it)
                if 0 <= it - LB < nblk:
                    stageB(it - LB, bi + it - LB)
                if 0 <= it - LC < nblk:
                    stageC(it - LC, bi + it - LC)
                if it >= 1:
                    for _ in range(per_it):
                        if mi_ < len(mine):
                            mine[mi_]()
                            mi_ += 1
            while mi_ < len(mine):
                mine[mi_]()
                mi_ += 1
            bi += nblk
            if g == 0:
                pending = merge_ops()
        for (ph, e) in pending:
            e()
        self.aoff = mark
        wo = Buf(self, "awo", NCH * D, BF16)
        at = Buf(self, "aat", NHEAD * 512, BF16, 2)
        yt = Buf(self, "ayt", NCH * 512, F32)
        sq = Buf(self, "asq", 512, BF16, 2)
        rstd = Buf(self, "arstd", 512, F32, 2)
        tmp = Buf(self, "atmp2", 512, F32, 2)
        for q2 in range(2):
            self.wload(wo.ap(0, q2 * 4096, [[1, 4096]]), wo.k(0, q2), "wo", lj, q2 * 4096, 4096,
                       [("wb", "wo", lj, qq) for qq in range(2)])

        def load_at(tile):
            src = bass.AP(self.scr_a, tile * 512, [[S, 128], [128 * S, NHEAD], [1, 512]])
            P.op("sp", (lambda: nc.sync.dma_start(out=at.ap(tile, 0, [[512, NHEAD], [1, 512]]), in_=src)),
                 reads=[("scra", hh) for hh in range(NHEAD)], writes=[at.k(tile)], dma=True)
        load_at(0)
        for tile in range(4):
            t0 = tile * 512
            if tile + 1 < 4:
                load_at(tile + 1)
            def mmf(m, b, tile=tile):
                def mm():
                    ins = None
                    for kc in range(NCH):
                        ins = nc.tensor.matmul(AP(self.ps[b], 0, [[1, 512]]),
                                               lhsT=wo.ap(0, kc * D + m * 128, [[1, 128]]),
                                               rhs=at.ap(tile, kc * 512, [[1, 512]]),
                                               start=(kc == 0), stop=(kc == NCH - 1))
                    return ins
                return mm, [wo.k(0, 0), wo.k(0, 1), at.k(tile)]

            def evf(m, b):
                return (lambda: nc.scalar.copy(out=yt.ap(0, m * 512, [[1, 512]]),
                                               in_=AP(self.ps[b], 0, [[1, 512]]))), []
            self.outproj_postnorm(mmf, evf, PVC[("mix_post_g", li)], t0, yt, sq, rstd, tmp)

    def kv_project(self):
        outs = [("k", gh) for gh in range(24)] + [("v", gh) for gh in range(24)]
        self.project(PVC[("kv_norm_g", 0)], "wkv", 0, outs)

    def build(self):
        self.consts()
        self.precast_all()
        stop = self.stop_after
        for s_ in range(self.nseq):
            self.phase()
            self.load_x(s_)
            done = False
            for li in range(4):
                if li < 2:
                    self.conformer(li)
                else:
                    self.attention(li - 2, li)
                if stop == ("mix", li):
                    break
                self.ffn(li)
                if stop == ("ffn", li):
                    break
                if li == 1:
                    self.kv_project()
            self.phase()
            self.store_x(s_)
        n = self.P.emit()
        return self.nc, n


WNAMES = ("cm_w_in", "cm_w_out", "w_kv", "w_q", "w_o", "ffn_w_in", "ffn_w_out")


def kernel(**inputs):
    x = np.ascontiguousarray(np.asarray(inputs["x"], dtype=np.float32))
    kb = K()
    nc, _ = kb.build()
    pv = pack_pv(inputs)
    ws = {nm: np.ascontiguousarray(np.asarray(inputs[nm], dtype=np.float32)) for nm in WNAMES}
    in_maps = []
    for c in range(N_CORES):
        m = {"x": x[c * SEQ_PER_CORE:(c + 1) * SEQ_PER_CORE], "pv": pv}
        m.update(ws)
        in_maps.append(m)
    res = run_bass_kernel_spmd(nc, in_maps, core_ids=list(range(N_CORES)))
    return np.concatenate([r["out"] for r in res.results], axis=0)
```
